# Optimizing a Trainium2 kernel written in Bass

```python
import jax, jax.numpy as jnp
from jax import lax
import numpy as np

D_MODEL = 1024
BATCH = 2
SEQ = 16384
DEPTH = 2

PLE_DIM = 256
N_MIXERS = 2
D_FF = 2816
FFN_HALF = 0.5
MLA_HEADS = 16
MLA_Q_RANK = 512
MLA_KV_RANK = 256
MLA_NOPE = 64
MLA_ROPE = 32
MLA_V = 64
ROPE_THETA = 10000.0
FOX_HEADS = 16
FOX_HEAD_DIM = 64
Q_BLOCK = 128
EPS = 1e-6
N_MLA_LAYERS = (DEPTH + N_MIXERS - 1) // N_MIXERS
N_FOX_LAYERS = DEPTH // N_MIXERS

kernel_name = "hybrid_mla_fox_macaron_ple"


def rms_norm(x, g):
    xf = x.astype(jnp.float32)
    y = xf * lax.rsqrt(jnp.mean(xf * xf, axis=-1, keepdims=True) + EPS)
    return (y * g.astype(jnp.float32)).astype(x.dtype)


def swiglu(x, w_in, w_out):
    gate, up = jnp.split(x @ w_in, 2, axis=-1)
    return (jax.nn.silu(gate) * up) @ w_out


def rope(x, pos):
    r = x.shape[-1]
    inv_freq = ROPE_THETA ** (-jnp.arange(0, r, 2, dtype=jnp.float32) / r)
    ang = pos.astype(jnp.float32)[:, :, None, None] * inv_freq
    cos = jnp.cos(ang).astype(x.dtype)
    sin = jnp.sin(ang).astype(x.dtype)
    x1, x2 = jnp.split(x, 2, axis=-1)
    return jnp.concatenate([x1 * cos - x2 * sin, x1 * sin + x2 * cos], axis=-1)


def block_causal_attention(q, k, v, decay=None):
    b, s, h, dk = q.shape
    dv = v.shape[-1]
    nb = s // Q_BLOCK
    scale = dk ** -0.5
    qb = q.reshape(b, nb, Q_BLOCK, h, dk).transpose(1, 0, 2, 3, 4)
    blk = jnp.arange(nb)
    k_pos = jnp.arange(s)
    c_k = None if decay is None else decay.transpose(0, 2, 1)

    def attend(qi, i, c_qi):
        logits = jnp.einsum('bqhd,bkhd->bhqk', qi, k).astype(jnp.float32) * scale
        if c_qi is not None:
            logits = logits + c_qi[:, :, :, None] - c_k[:, :, None, :]
        q_pos = i * Q_BLOCK + jnp.arange(Q_BLOCK)
        mask = k_pos[None, :] <= q_pos[:, None]
        logits = jnp.where(mask[None, None], logits, -jnp.inf)
        probs = jax.nn.softmax(logits, axis=-1)
        return jnp.einsum('bhqk,bkhd->bqhd', probs.astype(v.dtype), v)

    if decay is None:
        out = lax.map(lambda a: attend(a[0], a[1], None), (qb, blk))
    else:
        c_qb = decay.reshape(b, nb, Q_BLOCK, h).transpose(1, 0, 3, 2)
        out = lax.map(lambda a: attend(a[0], a[1], a[2]), (qb, blk, c_qb))
    return out.transpose(1, 0, 2, 3, 4).reshape(b, s, h, dv)


def mla_mixer(hn, pos, w_in, g_q_lat, w_uq, g_kv_lat, w_ukv, g_qn, g_kn, w_o):
    b, s, _ = hn.shape
    z = hn @ w_in
    c_q = z[..., :MLA_Q_RANK]
    c_kv = z[..., MLA_Q_RANK:MLA_Q_RANK + MLA_KV_RANK]
    k_pe = z[..., MLA_Q_RANK + MLA_KV_RANK:]
    q = (rms_norm(c_q, g_q_lat) @ w_uq).reshape(b, s, MLA_HEADS, MLA_NOPE + MLA_ROPE)
    kv = (rms_norm(c_kv, g_kv_lat) @ w_ukv).reshape(b, s, MLA_HEADS, MLA_NOPE + MLA_V)
    k_nope, v = kv[..., :MLA_NOPE], kv[..., MLA_NOPE:]
    k_pe_h = jnp.broadcast_to(k_pe[:, :, None, :], (b, s, MLA_HEADS, MLA_ROPE))
    k = jnp.concatenate([k_nope, k_pe_h], axis=-1)
    q = rms_norm(q, g_qn)
    k = rms_norm(k, g_kn)
    q = jnp.concatenate([q[..., :MLA_NOPE], rope(q[..., MLA_NOPE:], pos)], axis=-1)
    k = jnp.concatenate([k[..., :MLA_NOPE], rope(k[..., MLA_NOPE:], pos)], axis=-1)
    o = block_causal_attention(q, k, v)
    return o.reshape(b, s, MLA_HEADS * MLA_V) @ w_o


def fox_mixer(hn, w_in, b_f, g_qn, g_kn, w_o):
    b, s, _ = hn.shape
    hd = FOX_HEADS * FOX_HEAD_DIM
    z = hn @ w_in
    q = z[..., :hd].reshape(b, s, FOX_HEADS, FOX_HEAD_DIM)
    k = z[..., hd:2 * hd].reshape(b, s, FOX_HEADS, FOX_HEAD_DIM)
    v = z[..., 2 * hd:3 * hd].reshape(b, s, FOX_HEADS, FOX_HEAD_DIM)
    f_logit = z[..., 3 * hd:].astype(jnp.float32) + b_f.astype(jnp.float32)
    log_f = jax.nn.log_sigmoid(f_logit)
    decay = jnp.cumsum(log_f, axis=1)
    q = rms_norm(q, g_qn)
    k = rms_norm(k, g_kn)
    o = block_causal_attention(q, k, v, decay)
    return o.reshape(b, s, hd) @ w_o


def setup_inputs(seed: int = 0) -> dict:
    key = jax.random.key(seed)
    ks = iter(jax.random.split(key, 40))

    def dense(shape, fan_in):
        return jax.random.normal(next(ks), shape, jnp.float32) * (fan_in ** -0.5)

    def gain(shape):
        return 1.0 + 0.1 * jax.random.normal(next(ks), shape, jnp.float32)

    x = jax.random.normal(next(ks), (BATCH, SEQ, D_MODEL), jnp.float32)
    p = jax.random.normal(next(ks), (DEPTH, BATCH, SEQ, PLE_DIM), jnp.float32)
    positions = jnp.broadcast_to(jnp.arange(SEQ, dtype=jnp.int32), (BATCH, SEQ))
    na, nf = N_MLA_LAYERS, N_FOX_LAYERS
    fox_in = 3 * FOX_HEADS * FOX_HEAD_DIM + FOX_HEADS
    return {
        "x": x,
        "p": p,
        "positions": positions,
        "g_ffn1": gain((DEPTH, D_MODEL)),
        "g_mix": gain((DEPTH, D_MODEL)),
        "g_ffn2": gain((DEPTH, D_MODEL)),
        "g_ple": gain((DEPTH, D_MODEL)),
        "ffn1_w_in": dense((DEPTH, D_MODEL, 2 * D_FF), D_MODEL),
        "ffn1_w_out": dense((DEPTH, D_FF, D_MODEL), D_FF),
        "ffn2_w_in": dense((DEPTH, D_MODEL, 2 * D_FF), D_MODEL),
        "ffn2_w_out": dense((DEPTH, D_FF, D_MODEL), D_FF),
        "ple_w_proj": dense((DEPTH, PLE_DIM, D_MODEL), PLE_DIM),
        "ple_w_gate": dense((DEPTH, D_MODEL, D_MODEL), D_MODEL),
        "mla_w_in": dense((na, D_MODEL, MLA_Q_RANK + MLA_KV_RANK + MLA_ROPE), D_MODEL),
        "mla_g_q_lat": gain((na, MLA_Q_RANK)),
        "mla_w_uq": dense((na, MLA_Q_RANK, MLA_HEADS * (MLA_NOPE + MLA_ROPE)), MLA_Q_RANK),
        "mla_g_kv_lat": gain((na, MLA_KV_RANK)),
        "mla_w_ukv": dense((na, MLA_KV_RANK, MLA_HEADS * (MLA_NOPE + MLA_V)), MLA_KV_RANK),
        "mla_g_qn": gain((na, MLA_NOPE + MLA_ROPE)),
        "mla_g_kn": gain((na, MLA_NOPE + MLA_ROPE)),
        "mla_w_o": dense((na, MLA_HEADS * MLA_V, D_MODEL), MLA_HEADS * MLA_V),
        "fox_w_in": dense((nf, D_MODEL, fox_in), D_MODEL),
        "fox_b_f": jax.random.uniform(next(ks), (nf, FOX_HEADS), jnp.float32, 1.0, 4.0),
        "fox_g_qn": gain((nf, FOX_HEAD_DIM)),
        "fox_g_kn": gain((nf, FOX_HEAD_DIM)),
        "fox_w_o": dense((nf, FOX_HEADS * FOX_HEAD_DIM, D_MODEL), FOX_HEADS * FOX_HEAD_DIM),
    }


def reference(x, p, positions, g_ffn1, g_mix, g_ffn2, g_ple,
              ffn1_w_in, ffn1_w_out, ffn2_w_in, ffn2_w_out, ple_w_proj, ple_w_gate,
              mla_w_in, mla_g_q_lat, mla_w_uq, mla_g_kv_lat, mla_w_ukv,
              mla_g_qn, mla_g_kn, mla_w_o,
              fox_w_in, fox_b_f, fox_g_qn, fox_g_kn, fox_w_o):
    h = x
    for i in range(DEPTH):
        h = h + FFN_HALF * swiglu(rms_norm(h, g_ffn1[i]), ffn1_w_in[i], ffn1_w_out[i])
        hn = rms_norm(h, g_mix[i])
        j = i // N_MIXERS
        if i % N_MIXERS == 0:
            h = h + mla_mixer(hn, positions, mla_w_in[j], mla_g_q_lat[j], mla_w_uq[j],
                              mla_g_kv_lat[j], mla_w_ukv[j], mla_g_qn[j], mla_g_kn[j],
                              mla_w_o[j])
        else:
            h = h + fox_mixer(hn, fox_w_in[j], fox_b_f[j], fox_g_qn[j], fox_g_kn[j],
                              fox_w_o[j])
        h = h + FFN_HALF * swiglu(rms_norm(h, g_ffn2[i]), ffn2_w_in[i], ffn2_w_out[i])
        gate = jax.nn.sigmoid(rms_norm(h, g_ple[i]) @ ple_w_gate[i])
        h = h + gate * (p[i] @ ple_w_proj[i])
    return h
```

```python
import numpy as np
from contextlib import ExitStack
import concourse.bass as bass
import concourse.mybir as mybir
from concourse.bass_utils import run_bass_kernel_spmd

F32 = mybir.dt.float32
BF16 = mybir.dt.bfloat16
I32 = mybir.dt.int32
AF = mybir.ActivationFunctionType
ALU = mybir.AluOpType

D = 1024
S = 16384
NT = 4096
TT = 512
DFF = 2816
NC_FF = DFF // 128
EPS = 1e-6
PI = float(np.pi)
C1 = 6.28125
C2 = float(2 * np.pi - 6.28125)


class Buf:
    __slots__ = ("name", "last_w", "rd", "rd_dma", "wr_all")

    def __init__(self, name=""):
        self.name = name
        self.last_w = None
        self.rd = {}
        self.rd_dma = []
        self.wr_all = []

    def reset(self):
        self.last_w = None
        self.rd = {}
        self.rd_dma = []
        self.wr_all = []


class Op:
    __slots__ = ("eng", "fn", "deps", "dma", "sig", "needed", "idx", "cc", "prev")

    def __init__(self, eng, fn, dma, cc):
        self.eng = eng
        self.fn = fn
        self.deps = []
        self.dma = dma
        self.cc = cc
        self.sig = None
        self.needed = False
        self.prev = None


class Prog:
    ENGS = ("pe", "act", "dve", "pool", "sp")
    NDMASEM = 6
    SEM_ROLL = 24000

    def __init__(self, nc, stack):
        self.nc = nc
        self.stack = stack
        self.ops = []
        self.start = 0
        self.bufs = []
        self.sems = {}
        self.cnt = {}
        self.nsem = 0
        self.dma_pool = {}
        self.dma_n = {}
        self.cc_sem = None
        self.cc_n = 0
        self.out_sigs = []

    def buf(self, name=""):
        b = Buf(name)
        self.bufs.append(b)
        return b

    def newsem(self, tag):
        self.nsem += 1
        return self.stack.enter_context(self.nc.semaphore("s_%s_%d" % (tag, self.nsem)))

    def op(self, eng, fn, reads=(), writes=(), dma=False, cc=False, iwrites=()):
        o = Op(eng, fn, dma, cc)
        o.idx = len(self.ops)
        special = dma or cc
        deps = set()
        for b in reads:
            if b.last_w is not None:
                deps.add(b.last_w)
            deps.update(b.wr_all)
        for b in writes:
            if b.last_w is not None:
                deps.add(b.last_w)
            deps.update(b.wr_all)
            for j in b.rd.values():
                deps.add(j)
            for j in b.rd_dma:
                deps.add(j)
        for b in iwrites:
            if b.last_w is not None:
                deps.add(b.last_w)
        best = {}
        for j in deps:
            p = self.ops[j]
            if p.dma or p.cc:
                o.deps.append(j)
                continue
            if p.eng == eng and not special:
                if eng == "pe":
                    continue
                if not any(b.last_w == j for b in reads):
                    continue
            if best.get(p.eng, -1) < j:
                best[p.eng] = j
        for j in best.values():
            o.deps.append(j)
            self.ops[j].needed = True
        for b in reads:
            if special:
                b.rd_dma.append(o.idx)
            else:
                b.rd[eng] = o.idx
        for b in writes:
            b.last_w = o.idx
            b.rd = {}
            b.rd_dma = []
            b.wr_all = []
        for b in iwrites:
            b.wr_all.append(o.idx)
        self.ops.append(o)
        return o

    def flush(self):
        nc = self.nc
        pend = self.ops[self.start:]
        self.start = len(self.ops)
        if not pend:
            return
        streams = {e: [] for e in self.ENGS}
        for e in self.ENGS:
            if e not in self.sems or self.cnt[e] >= self.SEM_ROLL:
                self.sems[e] = self.newsem(e)
                self.cnt[e] = 0
        if self.cc_sem is None:
            self.cc_sem = self.newsem("cc")
        last_dma = {}
        for o in pend:
            e = o.eng
            streams[e].append(o)
            if o.dma:
                if e not in self.dma_pool:
                    self.dma_pool[e] = [self.newsem("dma" + e) for _ in range(self.NDMASEM)]
                    self.dma_n[e] = 0
                i = self.dma_n[e]
                self.dma_n[e] += 1
                si = i % self.NDMASEM
                s = self.dma_pool[e][si]
                prev = 16 * (i // self.NDMASEM)
                o.sig = (s, prev + 16)
                o.prev = (s, prev)
                last_dma[(e, si)] = o.sig
            elif o.cc:
                self.cc_n += 1
                o.sig = (self.cc_sem, self.cc_n)
                last_dma[("cc", 0)] = o.sig
            elif o.needed:
                self.cnt[e] += 1
                o.sig = (self.sems[e], self.cnt[e])
        ops = self.ops
        finals = list(last_dma.values())

        def run_stream(e):
            def body(eng):
                known = {}
                for o in streams[e]:
                    waits = [ops[j].sig for j in o.deps]
                    if o.dma and o.prev[1] > 0:
                        waits.append(o.prev)
                    for (s, v) in waits:
                        k = id(s)
                        if known.get(k, 0) >= v:
                            continue
                        known[k] = v
                        eng.wait_ge(s, v)
                    ins = o.fn(eng)
                    if o.sig is not None:
                        ins.then_inc(o.sig[0], 16 if o.dma else 1)
                if e == "sp":
                    for (s, v) in finals:
                        eng.wait_ge(s, v)
            return body

        with nc.Block() as block:
            for e, deco in (("pe", block.tensor), ("act", block.scalar), ("dve", block.vector),
                            ("pool", block.gpsimd), ("sp", block.sync)):
                if streams[e] or e == "sp":
                    deco(run_stream(e))
        for b in self.bufs:
            b.reset()


class Tile:
    def __init__(self, t, b):
        self.t = t
        self.b = b


class Rot:
    def __init__(self, tiles):
        self.tiles = tiles
        self.i = 0

    def next(self):
        t = self.tiles[self.i % len(self.tiles)]
        self.i += 1
        return t


def build_program(stop_after=None, dbg=()):
    nc = bass.Bass("TRN2", target_bir_lowering=False)
    gstack = ExitStack()
    P = Prog(nc, gstack)

    def din(name, shape, dt=F32):
        return nc.dram_tensor(name, list(shape), dt, kind="ExternalInput")

    def dint(name, shape, dt):
        return nc.dram_tensor(name, list(shape), dt)

    xT = din("xT", [D, NT])
    pT = din("pT", [2, 256, NT])
    pos = din("pos", [32, S], I32)
    gvec = din("gvec", [128, NG])
    w1in = din("w1in", [2, NC_FF, 128, 2048])
    w2in = din("w2in", [2, NC_FF, 128, 2048])
    w1out = din("w1out", [2, 128, NC_FF, D])
    w2out = din("w2out", [2, 128, NC_FF, D])
    wgate = din("wgate", [2, 128, 8, D])
    wproj = din("wproj", [2, 128, 2, D])
    mla_win = din("mla_win", [128, 8, 800])
    wuq = din("wuq", [128, 4, 512])
    wukvk = din("wukvk", [128, 2, 512])
    wukvv = din("wukvv", [128, 2, 256])
    selm = din("selm", [32, 128])
    mla_wo = din("mla_wo", [128, 2, D])
    fwq = din("fwq", [128, 8, 256])
    fwk = din("fwk", [128, 8, 256])
    fwv = din("fwv", [128, 8, 256])
    fwf = din("fwf", [128, 8, 4])
    fox_wo = din("fox_wo", [128, 2, D])
    cmats = din("cmats", [6, 128, 128])
    yT = nc.dram_tensor("yT", [D, NT], F32, kind="ExternalOutput")

    hS = dint("hS", [D, NT], F32)
    agS0 = dint("agS0", [16, 800, 256], BF16)
    agR0 = dint("agR0", [16, 4 * 800, 256], BF16)
    agS1 = dint("agS1", [16, D, 256], BF16)
    agR1 = dint("agR1", [16, 4 * D, 256], BF16)
    Qs = dint("Qs", [4, 96, S], BF16)
    Ks = dint("Ks", [4, 96, S], BF16)
    Vs = dint("Vs", [4, 128, 128 * 64], BF16)
    Lf = dint("Lf", [4, S], F32)
    Cs = dint("Cs", [4, S], F32)
    Os = dint("Os", [128, 1], F32)
    pS = dint("pS", [4, 4 * D, 1024], F32)
    pR = dint("pR", [4, D, 1024], F32)
    B_hS, B_agS0, B_agR0, B_agS1, B_agR1 = (P.buf(n) for n in ("hS", "agS0", "agR0", "agS1", "agR1"))
    B_Qs, B_Ks, B_Vs, B_Lf, B_Cs, B_Os, B_pS, B_pR = (P.buf(n) for n in ("Qs", "Ks", "Vs", "Lf", "Cs", "Os", "pS", "pR"))
    B_y = P.buf("y")
    B_pRq = [P.buf("pR%d" % i) for i in range(4)]
    B_pSq = [P.buf("pS%d" % i) for i in range(4)]
    RG = [[0, 1, 2, 3], [4, 5, 6, 7]]

    def dma(eng, out, in_, reads=(), writes=(), slow=False, iw=()):
        if slow:
            return P.op(eng, lambda e: e.dma_start(out=out, in_=in_, allow_slow_non_contiguous=True), reads, writes, dma=True, iwrites=iw)
        return P.op(eng, lambda e: e.dma_start(out=out, in_=in_), reads, writes, dma=True, iwrites=iw)

    def mm(out, lhsT, rhs, start, stop, reads, writes):
        return P.op("pe", lambda e: e.matmul(out, lhsT=lhsT, rhs=rhs, start=start, stop=stop, skip_group_check=True), reads, writes)

    def act(out, in_, func, reads, writes, bias=None, scale=1.0):
        if bias is None:
            return P.op("act", lambda e: e.activation(out=out, in_=in_, func=func, scale=scale), reads, writes)
        return P.op("act", lambda e: e.activation(out=out, in_=in_, func=func, bias=bias, scale=scale), reads, writes)

    def tt(eng, out, in0, in1, op, reads, writes):
        return P.op(eng, lambda e: e.tensor_tensor(out=out, in0=in0, in1=in1, op=op), reads, writes)

    def ts(eng, out, in0, s1, s2, op0, op1, reads, writes):
        if s2 is None:
            return P.op(eng, lambda e: e.tensor_scalar(out=out, in0=in0, scalar1=s1, scalar2=None, op0=op0), reads, writes)
        return P.op(eng, lambda e: e.tensor_scalar(out=out, in0=in0, scalar1=s1, scalar2=s2, op0=op0, op1=op1), reads, writes)

    def stt(eng, out, in0, scalar, in1, op0, op1, reads, writes):
        return P.op(eng, lambda e: e.scalar_tensor_tensor(out=out, in0=in0, scalar=scalar, in1=in1, op0=op0, op1=op1), reads, writes)

    def cp(eng, out, in_, reads, writes):
        return P.op(eng, lambda e: e.tensor_copy(out=out, in_=in_), reads, writes)

    def recip(out, in_, reads, writes):
        return P.op("dve", lambda e: e.reciprocal(out=out, in_=in_), reads, writes)

    def memset(eng, ap, val, writes):
        return P.op(eng, lambda e: e.memset(ap, val), (), writes)

    def allgather(src, dst, idx, bsrc, bdst):
        if OPT.get('nocc'):
            return
        P.op("pool", lambda e: e.collective_compute("AllGather", ALU.bypass, replica_groups=RG,
                                                    ins=[src[idx, :, :]], outs=[dst[idx, :, :]]),
             [bsrc], (), cc=True, iwrites=[bdst])

    uid = [0]

    def sbuf(stack, name, shape, dt):
        uid[0] += 1
        name = "%s_%d" % (name, uid[0])
        return Tile(stack.enter_context(nc.sbuf_tensor(name, list(shape), dt)), P.buf(name))

    def psum(stack, name):
        return Tile(stack.enter_context(nc.psum_tensor(name, [128, 512], F32)), P.buf(name))

    G = sbuf(gstack, "gv", [128, NG], F32)
    CM = sbuf(gstack, "cm", [128, 5, 128], BF16)
    TRI = sbuf(gstack, "tri", [128, 128], F32)
    ps = [psum(gstack, "ps%d" % i) for i in range(8)]

    def consts_load():
        dma("sp", G.t[:, :], gvec[:, :], (), [G.b])
        for i in range(5):
            dma("pool", CM.t[:, i, :], cmats[i, :, :], (), [CM.b])
        dma("sp", TRI.t[:, :], cmats[5, :, :], (), [TRI.b])
        ts("dve", G.t[:, GC_GQ:GC_GQ + 1], G.t[:, GC_GQ:GC_GQ + 1], float(96 ** -0.5), None, ALU.mult, None, [G.b], [G.b])
        ts("dve", G.t[:, GC_FQ:GC_FQ + 1], G.t[:, GC_FQ:GC_FQ + 1], float(64 ** -0.5), None, ALU.mult, None, [G.b], [G.b])

    ONES_ALL = lambda: CM.t[:, 0, :]
    ONES96 = lambda: CM.t[:, 1, :]
    ONES_BD = lambda: CM.t[:, 2, :]
    IDENT = lambda: CM.t[:, 3, :]
    NEGTRI = lambda: CM.t[:, 4, :]
    EPSC = lambda p0, p1: G.t[p0:p1, GC_EPS:GC_EPS + 1]

    def rmsnorm(src, nch, nfeat, gc0, out, sq, ssps, rs):
        tt("dve", sq.t[:, 0:nch, :], src.t[:, 0:nch, :], src.t[:, 0:nch, :], ALU.mult, [src.b], [sq.b])
        for c in range(nch):
            mm(ssps.t[:, :], ONES_ALL(), sq.t[:, c, :], c == 0, c == nch - 1, [CM.b, sq.b], [ssps.b])
        act(rs.t[:, :], ssps.t[:, :], AF.Sqrt, [ssps.b, G.b], [rs.b], bias=EPSC(0, 128), scale=1.0 / nfeat)
        recip(rs.t[:, :], rs.t[:, :], [rs.b], [rs.b])
        for c in range(nch):
            stt("dve", out.t[:, c, :], src.t[:, c, :], G.t[:, gc0 + c:gc0 + c + 1], rs.t[:, :], ALU.mult, ALU.mult,
                [src.b, G.b, rs.b], [out.b])

    def rstd_act(rs_ap, ss_ap, nfeat, rb, sb_, p0=0, p1=128):
        act(rs_ap, ss_ap, AF.Ln, [sb_, G.b], [rb], bias=EPSC(p0, p1), scale=1.0 / nfeat)
        act(rs_ap, rs_ap, AF.Exp, [rb], [rb], scale=-0.5)

    def stage_tok(stage):
        NH = 2
        ST = NH * TT
        with ExitStack() as st:
            hT = sbuf(st, "hT", [128, 8, ST], F32)
            xn = sbuf(st, "xn", [128, 8, ST], BF16)
            sq = sbuf(st, "sq", [128, 8, TT], BF16)
            aT = sbuf(st, "aT", [128, NC_FF, ST], BF16)
            rs = sbuf(st, "rs", [128, TT], F32)
            sg = Rot([sbuf(st, "sg%d" % i, [128, TT], F32) for i in range(2)])
            win = Rot([sbuf(st, "win%d" % i, [128, 2048], BF16) for i in range(3)])
            wom = Rot([sbuf(st, "wom%d" % i, [128, NC_FF, 128], BF16) for i in range(4)])
            ss_ps = ps[0]
            g_ps = Rot([ps[1], ps[2]])
            u_ps = Rot([ps[3], ps[4]])
            o_ps = Rot([ps[5], ps[6]])
            if stage in (1, 2):
                rT = sbuf(st, "rT", [128, 8, TT], F32)
                wg = sbuf(st, "wg", [128, 8, D], BF16)
                wp = sbuf(st, "wp", [128, 2, D], BF16)
                ptl = sbuf(st, "ptl", [128, 2, ST], BF16)
                li = stage - 1

                def rs_op(q):
                    P.op("pool", lambda e: e.collective_compute("ReduceScatter", ALU.add, replica_groups=RG,
                                                                ins=[pS[q, :, :]], outs=[pR[q, :, :]]),
                         (), [B_pRq[q]], cc=True)
                dma("pool", wg.t[:, :, :], wgate[li, :, :, :], (), [wg.b])
                dma("pool", wp.t[:, :, :], wproj[li, :, :, :], (), [wp.b])
            if stage == 0:
                mw = sbuf(st, "mw", [128, 8, 800], BF16)
                dma("pool", mw.t[:, :, :], mla_win[:, :, :], (), [mw.b])
                zf = sbuf(st, "zf", [128, 6, TT], F32)
                zn = sbuf(st, "zn", [128, 6, TT], BF16)
                kpe = sbuf(st, "kpe", [32, TT], BF16)

            hb = [P.buf("hA"), P.buf("hB")]
            xb = [P.buf("xA"), P.buf("xB")]
            ab = [P.buf("aA"), P.buf("aB")]

            def norm_sq(hf):
                cs = slice(hf * TT, (hf + 1) * TT)
                tt("dve", sq.t[:, :, :], hT.t[:, :, cs], hT.t[:, :, cs], ALU.mult, [hb[hf]], [sq.b])

            def norm_fin(hf, gc0):
                cs = slice(hf * TT, (hf + 1) * TT)
                for c in range(8):
                    mm(ss_ps.t[:, :], ONES_ALL(), sq.t[:, c, :], c == 0, c == 7, [CM.b, sq.b], [ss_ps.b])
                rstd_act(rs.t[:, :], ss_ps.t[:, :], D, rs.b, ss_ps.b)
                for c in range(8):
                    stt("dve", xn.t[:, c, cs], hT.t[:, c, cs], G.t[:, gc0 + c:gc0 + c + 1], rs.t[:, :], ALU.mult, ALU.mult,
                        [hb[hf], G.b, rs.b], [xb[hf]])

            def norm_half(hf, gc0):
                norm_sq(hf)
                norm_fin(hf, gc0)

            def norm_h(gc0):
                for hf in range(NH):
                    norm_half(hf, gc0)

            def ffn(li, win_d, wout_d, next_gc):
                wo_pref = []

                def wo_load(m):
                    wq = wom.next()
                    dma("pool", wq.t[:, :, :], wout_d[li, :, :, m * 128:(m + 1) * 128], (), [wq.b])
                    wo_pref.append(wq)

                for c in range(NC_FF):
                    w = win.next()
                    dma("pool", w.t[:, :], win_d[li, c, :, :], (), [w.b])
                    if c in (8, 12, 16, 20):
                        wo_load(len(wo_pref))
                    for hf in range(NH):
                        cs = slice(hf * TT, (hf + 1) * TT)
                        gp = g_ps.next()
                        up = u_ps.next()
                        for half, pp in ((0, gp), (1, up)):
                            for k in range(8):
                                mm(pp.t[:, :], w.t[:, (half * 8 + k) * 128:(half * 8 + k + 1) * 128], xn.t[:, k, cs],
                                   k == 0, k == 7, [w.b, xb[hf]], [pp.b])
                        s = sg.next()
                        act(s.t[:, :], gp.t[:, :], AF.Silu, [gp.b], [s.b])
                        tt("dve", aT.t[:, c, cs], s.t[:, :], up.t[:, :], ALU.mult, [s.b, up.b], [ab[hf]])
                for g4 in range(2):
                    for hf in range(NH):
                        cs = slice(hf * TT, (hf + 1) * TT)
                        for m in range(4 * g4, 4 * g4 + 4):
                            wq = wo_pref[m]
                            op_ = o_ps.next()
                            for c in range(NC_FF):
                                mm(op_.t[:, :], wq.t[:, c, :], aT.t[:, c, cs], c == 0, c == NC_FF - 1, [wq.b, ab[hf]], [op_.b])
                            stt("dve", hT.t[:, m, cs], op_.t[:, :], 0.5, hT.t[:, m, cs], ALU.mult, ALU.add, [op_.b, hb[hf]], [hb[hf]])
                            if hf == NH - 1 and len(wo_pref) < 8:
                                wo_load(len(wo_pref))
                            if g4 == 1 and next_gc is not None and hf == 1 and m == 5:
                                norm_fin(0, next_gc)
                        if g4 == 1 and next_gc is not None:
                            if hf == 0:
                                norm_sq(0)
                            else:
                                norm_half(1, next_gc)

            def ple(li, t, next_gc):
                dma("pool", ptl.t[:, :, :], pT[li, :, t * ST:(t + 1) * ST].rearrange("(c p) n -> p c n", p=128), (), [ptl.b])
                for hf in range(NH):
                    cs = slice(hf * TT, (hf + 1) * TT)
                    for m in range(8):
                        gp = g_ps.next()
                        up = u_ps.next()
                        for k in range(8):
                            mm(gp.t[:, :], wg.t[:, k, m * 128:(m + 1) * 128], xn.t[:, k, cs], k == 0, k == 7, [wg.b, xb[hf]], [gp.b])
                        for k in range(2):
                            mm(up.t[:, :], wp.t[:, k, m * 128:(m + 1) * 128], ptl.t[:, k, cs], k == 0, k == 1, [wp.b, ptl.b], [up.b])
                        s = sg.next()
                        act(s.t[:, :], gp.t[:, :], AF.Sigmoid, [gp.b], [s.b])
                        tt("dve", s.t[:, :], s.t[:, :], up.t[:, :], ALU.mult, [s.b, up.b], [s.b])
                        tt("dve", hT.t[:, m, cs], hT.t[:, m, cs], s.t[:, :], ALU.add, [hb[hf], s.b], [hb[hf]])
                        if next_gc is not None and hf == 1 and m == 1:
                            norm_fin(0, next_gc)
                    if next_gc is not None:
                        if hf == 0:
                            norm_sq(0)
                        else:
                            norm_half(1, next_gc)

            def mla_pre(t):
                for hf in range(NH):
                    cs = slice(hf * TT, (hf + 1) * TT)
                    for zc in range(7):
                        gp = g_ps.next()
                        mcols = 128 if zc < 6 else 32
                        for k in range(8):
                            mm(gp.t[0:mcols, :], mw.t[:, k, zc * 128:zc * 128 + mcols], xn.t[:, k, cs], k == 0, k == 7,
                               [mw.b, xb[hf]], [gp.b])
                        if zc < 6:
                            act(zf.t[:, zc, :], gp.t[:, :], AF.Copy, [gp.b], [zf.b])
                        else:
                            act(kpe.t[:, :], gp.t[0:32, :], AF.Copy, [gp.b], [kpe.b])
                    for (c0, nch, nf, gc) in ((0, 4, 512, GC_QLAT), (4, 2, 256, GC_KVLAT)):
                        tt("dve", sq.t[:, c0:c0 + nch, :], zf.t[:, c0:c0 + nch, :], zf.t[:, c0:c0 + nch, :], ALU.mult, [zf.b], [sq.b])
                        for c in range(nch):
                            mm(ss_ps.t[:, :], ONES_ALL(), sq.t[:, c0 + c, :], c == 0, c == nch - 1, [CM.b, sq.b], [ss_ps.b])
                        rstd_act(rs.t[:, :], ss_ps.t[:, :], nf, rs.b, ss_ps.b)
                        for c in range(nch):
                            stt("dve", zn.t[:, c0 + c, :], zf.t[:, c0 + c, :], G.t[:, gc + c:gc + c + 1], rs.t[:, :], ALU.mult, ALU.mult,
                                [zf.b, G.b, rs.b], [zn.b])
                    for h2 in range(2):
                        c2 = slice(h2 * 256, (h2 + 1) * 256)
                        idx = 2 * (NH * t + hf) + h2
                        dma("sp", agS0[idx, 0:768, :].rearrange("(c p) n -> p c n", p=128), zn.t[:, :, c2], [zn.b], iw=[B_agS0])
                        dma("sp", agS0[idx, 768:800, :], kpe.t[:, c2], [kpe.b], iw=[B_agS0])
                        allgather(agS0, agR0, idx, B_agS0, B_agR0)

            for t in range(OPT.get('ntiles', NT // ST)):
                tok = slice(t * ST, (t + 1) * ST)
                if stage in (0, 3):
                    dma("sp", hT.t[:, :, :], xT[:, tok].rearrange("(c p) n -> p c n", p=128), (), hb)
                else:
                    dma("sp", hT.t[:, :, :], hS[:, tok].rearrange("(c p) n -> p c n", p=128), [B_hS], hb)
                    for hf in range(NH):
                        cs = slice(hf * TT, (hf + 1) * TT)
                        dma("sp", rT.t[:, :, :], pR[t, :, cs].rearrange("(c p) n -> p c n", p=128), [B_pRq[t]], [rT.b])
                        tt("dve", hT.t[:, :, cs], hT.t[:, :, cs], rT.t[:, :, :], ALU.add, [hb[hf], rT.b], [hb[hf]])
                    li = stage - 1
                    norm_h(GC_FFN2 + 8 * li)
                    ffn(li, w2in, w2out, GC_PLE + 8 * li)
                    ple(li, t, GC_FFN1 + 8 if stage == 1 else None)
                if stage == 0:
                    norm_h(GC_FFN1 + 0)
                    ffn(0, w1in, w1out, GC_MIX + 0)
                    dma("sp", hS[:, tok].rearrange("(c p) n -> p c n", p=128), hT.t[:, :, :], hb, iw=[B_hS])
                    mla_pre(t)
                elif stage in (1, 3):
                    if stage == 3:
                        norm_h(GC_FFN1 + 8)
                    ffn(1, w1in, w1out, GC_MIX + 8)
                    dma("sp", hS[:, tok].rearrange("(c p) n -> p c n", p=128), hT.t[:, :, :], hb, iw=[B_hS])
                    for hf in range(NH):
                        for h2 in range(2):
                            c2 = slice(hf * TT + h2 * 256, hf * TT + (h2 + 1) * 256)
                            idx = 2 * (NH * t + hf) + h2
                            dma("sp", agS1[idx, :, :].rearrange("(c p) n -> p c n", p=128), xn.t[:, :, c2], [xb[hf]], iw=[B_agS1])
                            allgather(agS1, agR1, idx, B_agS1, B_agR1)
                else:
                    dma("sp", yT[:, tok].rearrange("(c p) n -> p c n", p=128), hT.t[:, :, :], hb, iw=[B_y])
            P.flush()

    def stage_proj(li):
        with ExitStack() as st:
            sqh = Rot([sbuf(st, "sqh%d" % i, [128, TT], BF16) for i in range(3)])
            rsh = Rot([sbuf(st, "rsh%d" % i, [128, TT], F32) for i in range(3)])
            vt = Rot([sbuf(st, "vt%d" % i, [128, 4, 256], BF16) for i in range(2)])
            qk_ps = Rot([ps[0], ps[1], ps[2], ps[7]] if li == 0 else [ps[0], ps[1], ps[2]])
            ss_ps = Rot([ps[3], ps[4], ps[5]] if li == 0 else [ps[3], ps[4]])
            v_ps = Rot([ps[6]] if li == 0 else [ps[5], ps[6]])
            f_ps = ps[7]
            Vs_v = Vs[:, :, :].rearrange("h p (b d) -> h p b d", d=64)
            if li == 0:
                w_q = sbuf(st, "w_q", [128, 4, 512], BF16)
                w_k = sbuf(st, "w_k", [128, 2, 512], BF16)
                w_v = sbuf(st, "w_v", [128, 2, 256], BF16)
                sel = sbuf(st, "sel", [32, 128], BF16)
                dma("pool", w_q.t[:, :, :], wuq[:, :, :], (), [w_q.b])
                dma("pool", w_k.t[:, :, :], wukvk[:, :, :], (), [w_k.b])
                dma("pool", w_v.t[:, :, :], wukvv[:, :, :], (), [w_v.b])
                dma("pool", sel.t[:, :], selm[:, :], (), [sel.b])
                zq = Rot([sbuf(st, "zq%d" % i, [128, 4, TT], BF16) for i in range(2)])
                zkv = Rot([sbuf(st, "zkv%d" % i, [128, 2, TT], BF16) for i in range(2)])
                kp = Rot([sbuf(st, "kp%d" % i, [32, TT], BF16) for i in range(2)])
                posi = sbuf(st, "posi", [128, TT], I32)
                ang = sbuf(st, "ang", [128, TT], F32)
                nfl = sbuf(st, "nfl", [128, TT], F32)
                nin = sbuf(st, "nin", [128, TT], I32)
                msk = sbuf(st, "msk", [128, TT], F32)
                cos_tb = [sbuf(st, "cos_t%d" % i, [128, TT], F32) for i in range(2)]
                sin_tb = [sbuf(st, "sin_t%d" % i, [128, TT], F32) for i in range(2)]
                qn = Rot([sbuf(st, "qn%d" % i, [128, TT], F32) for i in range(3)])
                sw = Rot([sbuf(st, "sw%d" % i, [128, TT], F32) for i in range(3)])
                t1 = Rot([sbuf(st, "t1%d" % i, [128, TT], F32) for i in range(3)])
                qo = Rot([sbuf(st, "qo%d" % i, [128, TT], BF16) for i in range(5)])
                R_ = slice(64, 96)

                def wrap(x):
                    for (cmpop, thr, add) in ((ALU.is_gt, PI, -2 * PI), (ALU.is_lt, -PI, 2 * PI)):
                        ts("dve", msk.t[R_, :], x.t[R_, :], thr, add, cmpop, ALU.mult, [x.b], [msk.b])
                        tt("dve", x.t[R_, :], x.t[R_, :], msk.t[R_, :], ALU.add, [x.b, msk.b], [x.b])

                def rope_tables(T):
                    cos_t = cos_tb[T % 2]
                    sin_t = sin_tb[T % 2]
                    dma("sp", posi.t[R_, :], pos[:, T * TT:(T + 1) * TT], (), [posi.b])
                    cp("dve", ang.t[R_, :], posi.t[R_, :], [posi.b], [ang.b])
                    ts("dve", ang.t[R_, :], ang.t[R_, :], G.t[R_, GC_INVF:GC_INVF + 1], None, ALU.mult, None, [ang.b, G.b], [ang.b])
                    ts("dve", nfl.t[R_, :], ang.t[R_, :], 1.0 / (2 * PI), None, ALU.mult, None, [ang.b], [nfl.b])
                    cp("dve", nin.t[R_, :], nfl.t[R_, :], [nfl.b], [nin.b])
                    cp("dve", nfl.t[R_, :], nin.t[R_, :], [nin.b], [nfl.b])
                    stt("dve", ang.t[R_, :], nfl.t[R_, :], -C1, ang.t[R_, :], ALU.mult, ALU.add, [nfl.b, ang.b], [ang.b])
                    stt("dve", ang.t[R_, :], nfl.t[R_, :], -C2, ang.t[R_, :], ALU.mult, ALU.add, [nfl.b, ang.b], [ang.b])
                    wrap(ang)
                    act(sin_t.t[R_, :], ang.t[R_, :], AF.Sin, [ang.b], [sin_t.b])
                    ts("dve", sin_t.t[R_, :], sin_t.t[R_, :], G.t[R_, GC_SGN:GC_SGN + 1], None, ALU.mult, None, [sin_t.b, G.b], [sin_t.b])
                    ts("dve", ang.t[R_, :], ang.t[R_, :], PI / 2, None, ALU.add, None, [ang.b], [ang.b])
                    wrap(ang)
                    act(cos_t.t[R_, :], ang.t[R_, :], AF.Sin, [ang.b], [cos_t.b])

                def hA(pq):
                    s_ = sqh.next()
                    act(s_.t[:, :], pq.t[:, :], AF.Square, [pq.b], [s_.b])
                    sp_ = ss_ps.next()
                    mm(sp_.t[:, :], ONES96(), s_.t[:, :], True, True, [CM.b, s_.b], [sp_.b])
                    return sp_

                def hB(pq, sp_, gc):
                    r_ = rsh.next()
                    rstd_act(r_.t[:, :], sp_.t[:, :], 96, r_.b, sp_.b)
                    o_ = qo.next()
                    stt("dve", o_.t[:, :], pq.t[:, :], G.t[:, gc:gc + 1], r_.t[:, :], ALU.mult, ALU.mult, [pq.b, G.b, r_.b], [o_.b])
                    return o_

                def hC(o_, dst, bdst, T):
                    cos_t = cos_tb[T % 2]
                    sin_t = sin_tb[T % 2]
                    w_ = sw.next()
                    act(w_.t[64:96, :], o_.t[96:128, :], AF.Copy, [o_.b], [w_.b])
                    a_ = t1.next()
                    tt("dve", a_.t[R_, :], o_.t[R_, :], cos_t.t[R_, :], ALU.mult, [o_.b, cos_t.b], [a_.b])
                    tt("dve", w_.t[R_, :], w_.t[R_, :], sin_t.t[R_, :], ALU.mult, [w_.b, sin_t.b], [w_.b])
                    tt("dve", o_.t[R_, :], a_.t[R_, :], w_.t[R_, :], ALU.add, [a_.b, w_.b, o_.b], [o_.b])
                    dma("sp", dst, o_.t[0:96, :], [o_.b], iw=[bdst])

                G0 = agR0[:, :, :].rearrange("i (r f) n -> i r f n", r=4)
                def mla_load(T):
                    r_, lt = T // 8, T % 8
                    a = zq.next()
                    b = zkv.next()
                    c = kp.next()
                    for hf in range(2):
                        cs = slice(hf * 256, (hf + 1) * 256)
                        dma("sp", a.t[:, :, cs], G0[2 * lt + hf, r_, 0:512, :].rearrange("(c p) n -> p c n", p=128), [B_agR0], [a.b])
                        dma("sp", b.t[:, :, cs], G0[2 * lt + hf, r_, 512:768, :].rearrange("(c p) n -> p c n", p=128), [B_agR0], [b.b])
                        dma("sp", c.t[:, cs], G0[2 * lt + hf, r_, 768:800, :], [B_agR0], [c.b])
                    return a, b, c

                nxt = mla_load(0)
                rope_tables(0)
                for T in range(S // TT):
                    gcol = slice(T * TT, (T + 1) * TT)
                    a, b, c = nxt
                    if T + 1 < S // TT:
                        nxt = mla_load(T + 1)
                        rope_tables(T + 1)
                    jobs = []
                    for h in range(4):
                        jobs.append(("q", h))
                        jobs.append(("k", h))
                    stA, stB = [], []
                    for step in range(len(jobs) + 2):
                        if step < len(jobs):
                            kind, h = jobs[step]
                            pq = qk_ps.next()
                            if kind == "q":
                                for k in range(4):
                                    mm(pq.t[:, :], w_q.t[:, k, h * 128:(h + 1) * 128], a.t[:, k, :], k == 0, k == 3, [w_q.b, a.b], [pq.b])
                                info = (GC_GQ, Qs[h, 0:96, gcol], B_Qs)
                            else:
                                for k in range(2):
                                    mm(pq.t[:, :], w_k.t[:, k, h * 128:(h + 1) * 128], b.t[:, k, :], k == 0, False, [w_k.b, b.b], [pq.b])
                                mm(pq.t[:, :], sel.t[:, :], c.t[:, :], False, True, [sel.b, c.b], [pq.b])
                                info = (GC_GK, Ks[h, 0:96, gcol], B_Ks)
                            stA.append((pq, hA(pq), info))
                        if step >= 1 and stA and step - 1 < len(jobs):
                            pq, sp_, info = stA.pop(0)
                            stB.append((hB(pq, sp_, info[0]), info))
                        if step >= 2 and stB:
                            o_, info = stB.pop(0)
                            hC(o_, info[1], info[2], T)
                    v_ = vt.next()
                    for blk in range(4):
                        pv = v_ps.next()
                        for k in range(2):
                            mm(pv.t[:, 0:256], b.t[:, k, blk * 128:(blk + 1) * 128], w_v.t[:, k, :], k == 0, k == 1, [b.b, w_v.b], [pv.b])
                        act(v_.t[:, blk, :], pv.t[:, 0:256], AF.Copy, [pv.b], [v_.b])
                    for h in range(4):
                        dma("sp", Vs_v[h, :, T * 4:(T + 1) * 4, :], v_.t[:, :, h * 64:(h + 1) * 64], [v_.b], iw=[B_Vs])
            else:
                w_q = sbuf(st, "f_q", [128, 8, 256], BF16)
                w_k = sbuf(st, "f_k", [128, 8, 256], BF16)
                w_v = sbuf(st, "f_v", [128, 8, 256], BF16)
                w_f = sbuf(st, "f_f", [128, 8, 4], BF16)
                dma("pool", w_q.t[:, :, :], fwq[:, :, :], (), [w_q.b])
                dma("pool", w_k.t[:, :, :], fwk[:, :, :], (), [w_k.b])
                dma("pool", w_v.t[:, :, :], fwv[:, :, :], (), [w_v.b])
                dma("pool", w_f.t[:, :, :], fwf[:, :, :], (), [w_f.b])
                hn = Rot([sbuf(st, "hn%d" % i, [128, 8, TT], BF16) for i in range(2)])
                qo = Rot([sbuf(st, "fqo%d" % i, [128, TT], BF16) for i in range(4)])
                lf = Rot([sbuf(st, "lf%d" % i, [4, TT], F32) for i in range(2)])
                G1 = agR1[:, :, :].rearrange("i (r f) n -> i r f n", r=4)

                def head_norm(pq, gc, dstT, B_dst, pair, gcol):
                    s_ = sqh.next()
                    act(s_.t[:, :], pq.t[:, :], AF.Square, [pq.b], [s_.b])
                    sp_ = ss_ps.next()
                    mm(sp_.t[:, :], ONES_BD(), s_.t[:, :], True, True, [CM.b, s_.b], [sp_.b])
                    r_ = rsh.next()
                    rstd_act(r_.t[:, :], sp_.t[:, :], 64, r_.b, sp_.b)
                    o_ = qo.next()
                    stt("dve", o_.t[:, :], pq.t[:, :], G.t[:, gc:gc + 1], r_.t[:, :], ALU.mult, ALU.mult, [pq.b, G.b, r_.b], [o_.b])
                    for j in range(2):
                        dma("sp", dstT[2 * pair + j, 0:64, gcol], o_.t[64 * j:64 * j + 64, :], [o_.b], iw=[B_dst])

                def fox_load(T):
                    r_, lt = T // 8, T % 8
                    a = hn.next()
                    for hf in range(2):
                        cs = slice(hf * 256, (hf + 1) * 256)
                        dma("sp", a.t[:, :, cs], G1[2 * lt + hf, r_, :, :].rearrange("(c p) n -> p c n", p=128), [B_agR1], [a.b])
                    return a

                nxt = fox_load(0)
                for T in range(S // TT):
                    gcol = slice(T * TT, (T + 1) * TT)
                    a = nxt
                    if T + 1 < S // TT:
                        nxt = fox_load(T + 1)
                    for pair in range(2):
                        pq = qk_ps.next()
                        for k in range(8):
                            mm(pq.t[:, :], w_q.t[:, k, pair * 128:(pair + 1) * 128], a.t[:, k, :], k == 0, k == 7, [w_q.b, a.b], [pq.b])
                        head_norm(pq, GC_FQ, Qs, B_Qs, pair, gcol)
                        pk = qk_ps.next()
                        for k in range(8):
                            mm(pk.t[:, :], w_k.t[:, k, pair * 128:(pair + 1) * 128], a.t[:, k, :], k == 0, k == 7, [w_k.b, a.b], [pk.b])
                        head_norm(pk, GC_FK, Ks, B_Ks, pair, gcol)
                    v_ = vt.next()
                    for blk in range(4):
                        pv = v_ps.next()
                        for k in range(8):
                            mm(pv.t[:, 0:256], a.t[:, k, blk * 128:(blk + 1) * 128], w_v.t[:, k, :], k == 0, k == 7, [a.b, w_v.b], [pv.b])
                        act(v_.t[:, blk, :], pv.t[:, 0:256], AF.Copy, [pv.b], [v_.b])
                    for h in range(4):
                        dma("sp", Vs_v[h, :, T * 4:(T + 1) * 4, :], v_.t[:, :, h * 64:(h + 1) * 64], [v_.b], iw=[B_Vs])
                    for k in range(8):
                        mm(f_ps.t[0:4, :], w_f.t[:, k, :], a.t[:, k, :], k == 0, k == 7, [w_f.b, a.b], [f_ps.b])
                    l_ = lf.next()
                    act(l_.t[:, :], f_ps.t[0:4, :], AF.Sigmoid, [f_ps.b, G.b], [l_.b], bias=G.t[0:4, GC_BF:GC_BF + 1])
                    act(l_.t[:, :], l_.t[:, :], AF.Ln, [l_.b], [l_.b])
                    dma("sp", Lf[:, gcol], l_.t[:, :], [l_.b], iw=[B_Lf])
                L = sbuf(st, "L", [128, TT], F32)
                one = sbuf(st, "one", [128, TT], F32)
                sc = sbuf(st, "sc", [128, TT], F32)
                hi = sbuf(st, "hi", [128, TT], BF16)
                lo = sbuf(st, "lo", [128, TT], BF16)
                onb = sbuf(st, "onb", [128, TT], BF16)
                off = sbuf(st, "off", [128, 1], F32)
                dma("sp", L.t[:, :], Lf[:, :].rearrange("h (t n) -> (h t) n", n=TT), [B_Lf], [L.b])
                memset("dve", one.t[:, :], 1.0, [one.b])
                memset("pool", onb.t[:, :], 1.0, [onb.b])
                P.op("dve", lambda e: e.tensor_tensor_scan(out=sc.t[:, :], data0=one.t[:, :], data1=L.t[:, :], initial=0.0,
                                                           op0=ALU.mult, op1=ALU.add), [one.b, L.b], [sc.b])
                pq = ps[0]
                mm(pq.t[:, 0:1], TRI.t[:, :], sc.t[:, TT - 1:TT], True, True, [TRI.b, sc.b], [pq.b])
                cp("dve", off.t[:, :], pq.t[:, 0:1], [pq.b], [off.b])
                dma("sp", Os[:, :], off.t[:, :], [off.b], iw=[B_Os])
                cp("dve", hi.t[:, :], sc.t[:, :], [sc.b], [hi.b])
                tt("dve", lo.t[:, :], sc.t[:, :], hi.t[:, :], ALU.subtract, [sc.b, hi.b], [lo.b])
                for h in range(4):
                    dma("sp", Qs[h, 64, :].rearrange("(t n) -> t n", n=TT), hi.t[32 * h:32 * h + 32, :], [hi.b], iw=[B_Qs])
                    dma("sp", Qs[h, 65, :].rearrange("(t n) -> t n", n=TT), lo.t[32 * h:32 * h + 32, :], [lo.b], iw=[B_Qs])
                for h in range(4):
                    dma("sp", Ks[h, 64:66, :].rearrange("a (t n) -> (a t) n", n=TT), onb.t[0:64, :], [onb.b], iw=[B_Ks])
                ts("dve", sc.t[:, :], sc.t[:, :], off.t[:, 0:1], None, ALU.add, None, [sc.b, off.b], [sc.b])
                dma("sp", Cs[:, :].rearrange("h (t n) -> (h t) n", n=TT), sc.t[:, :], [sc.b], iw=[B_Cs])
            P.flush()

    def stage_attn(li):
        dk = 96 if li == 0 else 66
        wo_d = mla_wo if li == 0 else fox_wo
        with ExitStack() as st:
            Kt = sbuf(st, "Kt", [96, S], BF16)
            Vt = sbuf(st, "Vt", [128, 128, 128], BF16)
            Ot = sbuf(st, "Ot", [128, 2, S], BF16)
            Qt = Rot([sbuf(st, "Qt%d" % i, [96, TT], BF16) for i in range(4)])
            LOOKAHEAD = 2
            Pt = Rot([sbuf(st, "Pt%d" % i, [128, TT], BF16) for i in range(4)])
            rsum = Rot([sbuf(st, "rsum%d" % i, [128, TT], F32) for i in range(2)])
            wo = sbuf(st, "wo", [128, 2, D], BF16)
            pt = Rot([sbuf(st, "pt%d" % i, [128, 4, TT], F32) for i in range(7)])
            s_ps = Rot([ps[0], ps[1], ps[2], ps[7]])
            o_ps = Rot([ps[3], ps[4]])
            p_ps = Rot([ps[5], ps[6]])
            dma("pool", wo.t[:, :, :], wo_d[:, :, :], (), [wo.b])
            if li == 1:
                ck = sbuf(st, "ck", [128, 128], F32)
                Rb = sbuf(st, "Rb", [128, 32], F32)
                bT = Rot([sbuf(st, "bT%d" % i, [128, 128], F32) for i in range(2)])
            Vs_v = Vs[:, :, :].rearrange("h p (b d) -> h p b d", d=64)
            pS_v = pS[:, :, :].rearrange("q (r f) n -> q r f n", r=4)
            b3_done = [0, 0, 0, 0]

            def b3(T):
                lt_ = T % 8
                q = lt_ // 2
                for mh in range(2):
                    x_ = pt.next()
                    for m4 in range(4):
                        m = mh * 4 + m4
                        pp = p_ps.next()
                        for pr in range(2):
                            mm(pp.t[:, :], wo.t[:, pr, m * 128:(m + 1) * 128], Ot.t[:, pr, T * TT:(T + 1) * TT], pr == 0, pr == 1,
                               [wo.b, Ot.b], [pp.b])
                        cp("dve", x_.t[:, m4, :], pp.t[:, :], [pp.b], [x_.b])
                    dma("pool", pS_v[q, T // 8, mh * 512:(mh + 1) * 512, (lt_ % 2) * TT:(lt_ % 2 + 1) * TT].rearrange("(c p) n -> p c n", p=128),
                        x_.t[:, :, :], [x_.b], iw=[B_pSq[q]])
                b3_done[q] += 1
                if b3_done[q] == 8 and not OPT.get('nors'):
                    P.op("pool", lambda e: e.collective_compute("ReduceScatter", ALU.add, replica_groups=RG,
                                                                ins=[pS[q, :, :]], outs=[pR[q, :, :]]),
                         [B_pSq[q]], [B_pRq[q]], cc=True)

            for h in range(4):
                odd = h % 2
                pair = h // 2
                vo = 64 if odd else 0
                so = 0 if odd else 64
                dma("sp", Kt.t[0:dk, :], Ks[h, 0:dk, :], [B_Ks], [Kt.b])
                memset("pool", Vt.t[:, :, so:so + 64], 1.0, [Vt.b])
                dma("sp", Vt.t[:, :, vo:vo + 64], Vs_v[h, :, :, :], [B_Vs], [Vt.b])
                if li == 1:
                    dma("sp", ck.t[:, :], Cs[h, :].rearrange("(b p) -> p b", p=128), [B_Cs], [ck.b], slow=True)
                    dma("sp", Rb.t[:, :], Os[h * 32:(h + 1) * 32, :].rearrange("t o -> o t").partition_broadcast(128),
                        [B_Os], [Rb.b], slow=True)
                    ts("dve", ck.t[:, :], ck.t[:, :], -1.0, None, ALU.mult, None, [ck.b], [ck.b])
                nq = OPT.get('nqg', S // TT)
                if h == 3 and nq == S // TT:
                    Torder = [8 * r + 2 * q + l2 for q in range(4) for r in range(4) for l2 in range(2)]
                else:
                    Torder = list(range(nq))
                tiles = [(T, i) for T in Torder for i in range(4 * T + 4)]
                b3_pend = []
                st_T = {}
                pend = []

                def issue_pv(T, i, p_, c0):
                    q_, b_, ob = st_T[T]
                    nblk = 4 * T + 4
                    mm(ob.t[:, c0:TT], Vt.t[:, i, :], p_.t[:, c0:TT], i == 0, i == nblk - 1, [Vt.b, p_.b], [ob.b])
                    if i == nblk - 1:
                        r_ = rsum.next()
                        act(r_.t[vo:vo + 64, :], ob.t[so:so + 64, :], AF.Copy, [ob.b], [r_.b])
                        recip(r_.t[vo:vo + 64, :], r_.t[vo:vo + 64, :], [r_.b], [r_.b])
                        tt("dve", Ot.t[vo:vo + 64, pair, T * TT:(T + 1) * TT], ob.t[vo:vo + 64, :], r_.t[vo:vo + 64, :], ALU.mult,
                           [ob.b, r_.b], [Ot.b])
                        del st_T[T]
                        if h == 3:
                            b3_pend.append(T)
                            if len(b3_pend) > 1:
                                b3(b3_pend.pop(0))

                for (T, i) in tiles:
                    if i == 0:
                        q_ = Qt.next()
                        dma("sp", q_.t[0:dk, :], Qs[h, 0:dk, T * TT:(T + 1) * TT], [B_Qs], [q_.b])
                        b_ = None
                        if li == 1:
                            b_ = bT.next()
                            nblk = 4 * T + 4
                            ts("dve", b_.t[:, 0:nblk], ck.t[:, 0:nblk], Rb.t[:, T:T + 1], None, ALU.add, None, [ck.b, Rb.b], [b_.b])
                        st_T[T] = (q_, b_, o_ps.next())
                    q_, b_, ob = st_T[T]
                    m = i - 4 * T
                    c0 = 128 * m if m > 0 else 0
                    sb_ = s_ps.next()
                    mm(sb_.t[:, c0:TT], Kt.t[0:dk, i * 128:(i + 1) * 128], q_.t[0:dk, c0:TT], True, m < 0, [Kt.b, q_.b], [sb_.b])
                    if m >= 0:
                        mm(sb_.t[:, c0:c0 + 128], IDENT(), NEGTRI(), False, True, [CM.b], [sb_.b])
                    p_ = Pt.next()
                    if li == 1:
                        act(p_.t[:, c0:TT], sb_.t[:, c0:TT], AF.Exp, [sb_.b, b_.b], [p_.b], bias=b_.t[:, i:i + 1])
                    else:
                        act(p_.t[:, c0:TT], sb_.t[:, c0:TT], AF.Exp, [sb_.b], [p_.b])
                    pend.append((T, i, p_, c0))
                    if len(pend) > LOOKAHEAD:
                        issue_pv(*pend.pop(0))
                while pend:
                    issue_pv(*pend.pop(0))
                if h == 3:
                    while b3_pend:
                        b3(b3_pend.pop(0))
            P.flush()

    def dump():
        tens = {"hS": hS, "agR0": agR0, "agR1": agR1, "Qs": Qs, "Ks": Ks, "Vs": Vs, "pR": pR, "Cs": Cs, "Lf": Lf, "pS": pS}
        for nm in dbg:
            t = tens[nm]
            ext = nc.dram_tensor("dbg_" + nm, list(t.shape), t.dtype, kind="ExternalOutput")
            if len(t.shape) == 3:
                dma("sp", ext[:, :, :], t[:, :, :], (), ())
            elif nm == "pS":
                dma("sp", ext[0, 0:D, :], t[0, 0:D, :], (), ())
            elif nm == "hS":
                nn = OPT.get('ntiles', 8) * TT
                dma("sp", ext[:, 0:nn], t[:, 0:nn], (), ())
            else:
                dma("sp", ext[:, :], t[:, :], (), ())
        P.flush()

    consts_load()
    if OPT.get('fox_first'):
        stage_tok(3)
        stage_proj(1)
        if stop_after != "proj1":
            stage_attn(1)
        P.flush()
        dump()
        gstack.close()
        return nc
    stage_tok(0)
    if stop_after != "tok0":
        stage_proj(0)
        if stop_after != "proj0":
            stage_attn(0)
            if stop_after != "attn0":
                stage_tok(1)
                stage_proj(1)
                stage_attn(1)
                stage_tok(2)
    P.flush()
    dump()
    gstack.close()
    return nc


GC_FFN1 = 0
GC_MIX = 16
GC_FFN2 = 32
GC_PLE = 48
GC_QLAT = 64
GC_KVLAT = 68
GC_GQ = 70
GC_GK = 71
GC_FQ = 72
GC_FK = 73
GC_INVF = 74
GC_SGN = 75
GC_BF = 76
GC_EPS = 77
NG = 78
OPT = {}


def _chunk_cols(v):
    return np.ascontiguousarray(v.reshape(-1, 128).T)


def prep_inputs(inp):
    f32 = np.float32
    x = inp["x"]
    p = inp["p"]
    positions = inp["positions"]
    common = {}
    for nm, key in (("w1in", "ffn1_w_in"), ("w2in", "ffn2_w_in")):
        w = inp[key].reshape(2, 8, 128, 2, NC_FF, 128)
        w = w.transpose(0, 4, 2, 3, 1, 5)
        common[nm] = np.ascontiguousarray(w).reshape(2, NC_FF, 128, 2048)
    for nm, key in (("w1out", "ffn1_w_out"), ("w2out", "ffn2_w_out")):
        w = inp[key].reshape(2, NC_FF, 128, D).transpose(0, 2, 1, 3)
        common[nm] = np.ascontiguousarray(w)
    common["wgate"] = np.ascontiguousarray(inp["ple_w_gate"].reshape(2, 8, 128, D).transpose(0, 2, 1, 3))
    common["wproj"] = np.ascontiguousarray(inp["ple_w_proj"].reshape(2, 2, 128, D).transpose(0, 2, 1, 3))
    common["mla_win"] = np.ascontiguousarray(inp["mla_w_in"][0].reshape(8, 128, 800).transpose(1, 0, 2))
    cm = np.zeros((6, 128, 128), f32)
    cm[0] = 1.0
    cm[1, :96, :] = 1.0
    cm[2, :64, :64] = 1.0
    cm[2, 64:, 64:] = 1.0
    cm[3] = np.eye(128, dtype=f32)
    kk, qq = np.meshgrid(np.arange(128), np.arange(128), indexing="ij")
    cm[4] = np.where(kk > qq, -30000.0, 0.0)
    hh, tt_ = np.arange(128) // 32, np.arange(128) % 32
    cm[5] = ((hh[:, None] == hh[None, :]) & (tt_[:, None] < tt_[None, :])).astype(f32)
    common["cmats"] = cm
    sel = np.zeros((32, 128), f32)
    for i in range(32):
        sel[i, 64 + i] = 1.0
    for j in range(32):
        sel[(j + 16) % 32, 96 + j] = 1.0
    common["selm"] = sel
    swap = np.concatenate([np.arange(16, 32), np.arange(0, 16)])
    w_uq = inp["mla_w_uq"][0].reshape(512, 16, 96)
    w_uq_ext = np.concatenate([w_uq, w_uq[:, :, 64 + swap]], axis=2)
    w_ukv = inp["mla_w_ukv"][0].reshape(256, 16, 128)
    w_k_ext = np.concatenate([w_ukv[:, :, :64], np.zeros((256, 16, 64), f32)], axis=2)
    w_v = w_ukv[:, :, 64:]
    gq = inp["mla_g_qn"][0]
    gk = inp["mla_g_kn"][0]
    gq_ext = np.concatenate([gq, gq[64 + swap]])
    gk_ext = np.concatenate([gk, gk[64 + swap]])
    inv_freq = (10000.0 ** (-np.arange(0, 32, 2, dtype=f32) / 32)).astype(f32)
    fox_in = inp["fox_w_in"][0]
    fq = fox_in[:, 0:1024].reshape(D, 16, 64)
    fk = fox_in[:, 1024:2048].reshape(D, 16, 64)
    fv = fox_in[:, 2048:3072].reshape(D, 16, 64)
    ff = fox_in[:, 3072:3088]
    in_maps = []
    for c in range(8):
        b, r = c // 4, c % 4
        hs = slice(4 * r, 4 * r + 4)
        m = dict(common)
        m["xT"] = np.ascontiguousarray(x[b, r * NT:(r + 1) * NT, :].T)
        m["pT"] = np.ascontiguousarray(p[:, b, r * NT:(r + 1) * NT, :].transpose(0, 2, 1))
        m["pos"] = np.ascontiguousarray(np.broadcast_to(positions[b].astype(np.int32)[None, :], (32, S)))
        g = np.zeros((128, NG), f32)
        for l in range(2):
            g[:, GC_FFN1 + 8 * l:GC_FFN1 + 8 * l + 8] = _chunk_cols(inp["g_ffn1"][l])
            g[:, GC_MIX + 8 * l:GC_MIX + 8 * l + 8] = _chunk_cols(inp["g_mix"][l])
            g[:, GC_FFN2 + 8 * l:GC_FFN2 + 8 * l + 8] = _chunk_cols(inp["g_ffn2"][l])
            g[:, GC_PLE + 8 * l:GC_PLE + 8 * l + 8] = _chunk_cols(inp["g_ple"][l])
        g[:, GC_QLAT:GC_QLAT + 4] = _chunk_cols(inp["mla_g_q_lat"][0])
        g[:, GC_KVLAT:GC_KVLAT + 2] = _chunk_cols(inp["mla_g_kv_lat"][0])
        g[:, GC_GQ] = gq_ext
        g[:, GC_GK] = gk_ext
        g[:, GC_FQ] = np.tile(inp["fox_g_qn"][0], 2)
        g[:, GC_FK] = np.tile(inp["fox_g_kn"][0], 2)
        g[64:96, GC_INVF] = np.tile(inv_freq, 2)
        g[64:80, GC_SGN] = -1.0
        g[80:96, GC_SGN] = 1.0
        g[0:4, GC_BF] = inp["fox_b_f"][0][hs]
        g[:, GC_EPS] = EPS
        m["gvec"] = g
        m["wuq"] = np.ascontiguousarray(w_uq_ext[:, hs, :].reshape(4, 128, 512).transpose(1, 0, 2))
        m["wukvk"] = np.ascontiguousarray(w_k_ext[:, hs, :].reshape(2, 128, 512).transpose(1, 0, 2))
        m["wukvv"] = np.ascontiguousarray(w_v[:, hs, :].reshape(2, 128, 256).transpose(1, 0, 2))
        m["mla_wo"] = np.ascontiguousarray(inp["mla_w_o"][0][256 * r:256 * (r + 1), :].reshape(2, 128, D).transpose(1, 0, 2))
        m["fwq"] = np.ascontiguousarray(fq[:, hs, :].reshape(8, 128, 256).transpose(1, 0, 2))
        m["fwk"] = np.ascontiguousarray(fk[:, hs, :].reshape(8, 128, 256).transpose(1, 0, 2))
        m["fwv"] = np.ascontiguousarray(fv[:, hs, :].reshape(8, 128, 256).transpose(1, 0, 2))
        m["fwf"] = np.ascontiguousarray(ff[:, hs].reshape(8, 128, 4).transpose(1, 0, 2))
        m["fox_wo"] = np.ascontiguousarray(inp["fox_w_o"][0][256 * r:256 * (r + 1), :].reshape(2, 128, D).transpose(1, 0, 2))
        in_maps.append(m)
    return in_maps


_NC_CACHE = {}


def kernel(**inputs):
    inp = {k: np.asarray(v) for k, v in inputs.items()}
    in_maps = prep_inputs(inp)
    if "nc" not in _NC_CACHE:
        _NC_CACHE["nc"] = build_program()
    nc = _NC_CACHE["nc"]
    res = run_bass_kernel_spmd(nc, in_maps, core_ids=list(range(8)))
    out = np.empty((2, S, D), np.float32)
    for c in range(8):
        b, r = c // 4, c % 4
        out[b, r * NT:(r + 1) * NT, :] = np.asarray(res.results[c]["yT"]).T
    return out
```

```python
import numpy as np
from contextlib import ExitStack
import concourse.bass as bass
import concourse.mybir as mybir
from concourse.bass_utils import run_bass_kernel_spmd

F32 = mybir.dt.float32
BF16 = mybir.dt.bfloat16
I32 = mybir.dt.int32
AF = mybir.ActivationFunctionType
ALU = mybir.AluOpType

D = 1024
S = 16384
NT = 4096
TT = 512
DFF = 2816
NC_FF = DFF // 128
EPS = 1e-6
PI = float(np.pi)
C1 = 6.28125
C2 = float(2 * np.pi - 6.28125)


class Buf:
    __slots__ = ("name", "last_w", "rd", "rd_dma", "wr_all")

    def __init__(self, name=""):
        self.name = name
        self.last_w = None
        self.rd = {}
        self.rd_dma = []
        self.wr_all = []

    def reset(self):
        self.last_w = None
        self.rd = {}
        self.rd_dma = []
        self.wr_all = []


class Op:
    __slots__ = ("eng", "fn", "deps", "dma", "sig", "needed", "idx", "cc", "prev")

    def __init__(self, eng, fn, dma, cc):
        self.eng = eng
        self.fn = fn
        self.deps = []
        self.dma = dma
        self.cc = cc
        self.sig = None
        self.needed = False
        self.prev = None


class Prog:
    ENGS = ("pe", "act", "dve", "pool", "sp")
    NDMASEM = 6
    SEM_ROLL = 24000

    def __init__(self, nc, stack):
        self.nc = nc
        self.stack = stack
        self.ops = []
        self.start = 0
        self.bufs = []
        self.sems = {}
        self.cnt = {}
        self.nsem = 0
        self.dma_pool = {}
        self.dma_n = {}
        self.cc_sem = None
        self.cc_n = 0
        self.out_sigs = []

    def buf(self, name=""):
        b = Buf(name)
        self.bufs.append(b)
        return b

    def newsem(self, tag):
        self.nsem += 1
        return self.stack.enter_context(self.nc.semaphore("s_%s_%d" % (tag, self.nsem)))

    def op(self, eng, fn, reads=(), writes=(), dma=False, cc=False, iwrites=()):
        o = Op(eng, fn, dma, cc)
        o.idx = len(self.ops)
        special = dma or cc
        deps = set()
        for b in reads:
            if b.last_w is not None:
                deps.add(b.last_w)
            deps.update(b.wr_all)
        for b in writes:
            if b.last_w is not None:
                deps.add(b.last_w)
            deps.update(b.wr_all)
            for j in b.rd.values():
                deps.add(j)
            for j in b.rd_dma:
                deps.add(j)
        for b in iwrites:
            if b.last_w is not None:
                deps.add(b.last_w)
        best = {}
        for j in deps:
            p = self.ops[j]
            if p.dma or p.cc:
                o.deps.append(j)
                continue
            if p.eng == eng and not special:
                if eng == "pe":
                    continue
                if not any(b.last_w == j for b in reads):
                    continue
            if best.get(p.eng, -1) < j:
                best[p.eng] = j
        for j in best.values():
            o.deps.append(j)
            self.ops[j].needed = True
        for b in reads:
            if special:
                b.rd_dma.append(o.idx)
            else:
                b.rd[eng] = o.idx
        for b in writes:
            b.last_w = o.idx
            b.rd = {}
            b.rd_dma = []
            b.wr_all = []
        for b in iwrites:
            b.wr_all.append(o.idx)
        self.ops.append(o)
        return o

    def flush(self):
        nc = self.nc
        pend = self.ops[self.start:]
        self.start = len(self.ops)
        if not pend:
            return
        streams = {e: [] for e in self.ENGS}
        for e in self.ENGS:
            if e not in self.sems or self.cnt[e] >= self.SEM_ROLL:
                self.sems[e] = self.newsem(e)
                self.cnt[e] = 0
        if self.cc_sem is None:
            self.cc_sem = self.newsem("cc")
        last_dma = {}
        for o in pend:
            e = o.eng
            streams[e].append(o)
            if o.dma:
                if e not in self.dma_pool:
                    self.dma_pool[e] = [self.newsem("dma" + e) for _ in range(self.NDMASEM)]
                    self.dma_n[e] = 0
                i = self.dma_n[e]
                self.dma_n[e] += 1
                si = i % self.NDMASEM
                s = self.dma_pool[e][si]
                prev = 16 * (i // self.NDMASEM)
                o.sig = (s, prev + 16)
                o.prev = (s, prev)
                last_dma[(e, si)] = o.sig
            elif o.cc:
                self.cc_n += 1
                o.sig = (self.cc_sem, self.cc_n)
                last_dma[("cc", 0)] = o.sig
            elif o.needed:
                self.cnt[e] += 1
                o.sig = (self.sems[e], self.cnt[e])
        ops = self.ops
        finals = list(last_dma.values())

        def run_stream(e):
            def body(eng):
                known = {}
                for o in streams[e]:
                    waits = [ops[j].sig for j in o.deps]
                    if o.dma and o.prev[1] > 0:
                        waits.append(o.prev)
                    for (s, v) in waits:
                        k = id(s)
                        if known.get(k, 0) >= v:
                            continue
                        known[k] = v
                        eng.wait_ge(s, v)
                    ins = o.fn(eng)
                    if o.sig is not None:
                        ins.then_inc(o.sig[0], 16 if o.dma else 1)
                if e == "sp":
                    for (s, v) in finals:
                        eng.wait_ge(s, v)
            return body

        with nc.Block() as block:
            for e, deco in (("pe", block.tensor), ("act", block.scalar), ("dve", block.vector),
                            ("pool", block.gpsimd), ("sp", block.sync)):
                if streams[e] or e == "sp":
                    deco(run_stream(e))
        for b in self.bufs:
            b.reset()


class Tile:
    def __init__(self, t, b):
        self.t = t
        self.b = b


class Rot:
    def __init__(self, tiles):
        self.tiles = tiles
        self.i = 0

    def next(self):
        t = self.tiles[self.i % len(self.tiles)]
        self.i += 1
        return t


def build_program(stop_after=None, dbg=()):
    nc = bass.Bass("TRN2", target_bir_lowering=False)
    gstack = ExitStack()
    P = Prog(nc, gstack)

    def din(name, shape, dt=F32):
        return nc.dram_tensor(name, list(shape), dt, kind="ExternalInput")

    def dint(name, shape, dt):
        return nc.dram_tensor(name, list(shape), dt)

    xT = din("xT", [D, NT])
    pT = din("pT", [2, 256, NT])
    pos = din("pos", [32, S], I32)
    gvec = din("gvec", [128, NG])
    w1in = din("w1in", [2, NC_FF, 128, 2048])
    w2in = din("w2in", [2, NC_FF, 128, 2048])
    w1out = din("w1out", [2, 128, NC_FF, D])
    w2out = din("w2out", [2, 128, NC_FF, D])
    wgate = din("wgate", [2, 128, 8, D])
    wproj = din("wproj", [2, 128, 2, D])
    mla_win = din("mla_win", [128, 8, 800])
    wuq = din("wuq", [128, 4, 512])
    wukvk = din("wukvk", [128, 2, 512])
    wukvv = din("wukvv", [128, 2, 256])
    selm = din("selm", [32, 128])
    mla_wo = din("mla_wo", [128, 2, D])
    fwq = din("fwq", [128, 8, 256])
    fwk = din("fwk", [128, 8, 256])
    fwv = din("fwv", [128, 8, 256])
    fwf = din("fwf", [128, 8, 4])
    fox_wo = din("fox_wo", [128, 2, D])
    cmats = din("cmats", [6, 128, 128])
    yT = nc.dram_tensor("yT", [D, NT], F32, kind="ExternalOutput")

    hS = dint("hS", [D, NT], F32)
    agS0 = dint("agS0", [16, 800, 256], BF16)
    agR0 = dint("agR0", [16, 4 * 800, 256], BF16)
    agS1 = dint("agS1", [16, D, 256], BF16)
    agR1 = dint("agR1", [16, 4 * D, 256], BF16)
    Qs = dint("Qs", [4, 96, S], BF16)
    Ks = dint("Ks", [4, 96, S], BF16)
    Vs = dint("Vs", [4, 128, 128 * 64], BF16)
    Lf = dint("Lf", [4, S], F32)
    Cs = dint("Cs", [4, S], F32)
    Os = dint("Os", [128, 1], F32)
    pS = dint("pS", [4, 4 * D, 1024], F32)
    pR = dint("pR", [4, D, 1024], F32)
    B_hS, B_agS0, B_agR0, B_agS1, B_agR1 = (P.buf(n) for n in ("hS", "agS0", "agR0", "agS1", "agR1"))
    B_Qs, B_Ks, B_Vs, B_Lf, B_Cs, B_Os, B_pS, B_pR = (P.buf(n) for n in ("Qs", "Ks", "Vs", "Lf", "Cs", "Os", "pS", "pR"))
    B_y = P.buf("y")
    B_pRq = [P.buf("pR%d" % i) for i in range(4)]
    B_pSq = [P.buf("pS%d" % i) for i in range(4)]
    RG = [[0, 1, 2, 3], [4, 5, 6, 7]]

    def dma(eng, out, in_, reads=(), writes=(), slow=False, iw=()):
        if slow:
            return P.op(eng, lambda e: e.dma_start(out=out, in_=in_, allow_slow_non_contiguous=True), reads, writes, dma=True, iwrites=iw)
        return P.op(eng, lambda e: e.dma_start(out=out, in_=in_), reads, writes, dma=True, iwrites=iw)

    def mm(out, lhsT, rhs, start, stop, reads, writes):
        return P.op("pe", lambda e: e.matmul(out, lhsT=lhsT, rhs=rhs, start=start, stop=stop, skip_group_check=True), reads, writes)

    def act(out, in_, func, reads, writes, bias=None, scale=1.0):
        if bias is None:
            return P.op("act", lambda e: e.activation(out=out, in_=in_, func=func, scale=scale), reads, writes)
        return P.op("act", lambda e: e.activation(out=out, in_=in_, func=func, bias=bias, scale=scale), reads, writes)

    def tt(eng, out, in0, in1, op, reads, writes):
        return P.op(eng, lambda e: e.tensor_tensor(out=out, in0=in0, in1=in1, op=op), reads, writes)

    def ts(eng, out, in0, s1, s2, op0, op1, reads, writes):
        if s2 is None:
            return P.op(eng, lambda e: e.tensor_scalar(out=out, in0=in0, scalar1=s1, scalar2=None, op0=op0), reads, writes)
        return P.op(eng, lambda e: e.tensor_scalar(out=out, in0=in0, scalar1=s1, scalar2=s2, op0=op0, op1=op1), reads, writes)

    def stt(eng, out, in0, scalar, in1, op0, op1, reads, writes):
        return P.op(eng, lambda e: e.scalar_tensor_tensor(out=out, in0=in0, scalar=scalar, in1=in1, op0=op0, op1=op1), reads, writes)

    def cp(eng, out, in_, reads, writes):
        return P.op(eng, lambda e: e.tensor_copy(out=out, in_=in_), reads, writes)

    def recip(out, in_, reads, writes):
        return P.op("dve", lambda e: e.reciprocal(out=out, in_=in_), reads, writes)

    def memset(eng, ap, val, writes):
        return P.op(eng, lambda e: e.memset(ap, val), (), writes)

    def allgather(src, dst, idx, bsrc, bdst):
        if OPT.get('nocc'):
            return
        P.op("pool", lambda e: e.collective_compute("AllGather", ALU.bypass, replica_groups=RG,
                                                    ins=[src[idx, :, :]], outs=[dst[idx, :, :]]),
             [bsrc], (), cc=True, iwrites=[bdst])

    uid = [0]

    def sbuf(stack, name, shape, dt):
        uid[0] += 1
        name = "%s_%d" % (name, uid[0])
        return Tile(stack.enter_context(nc.sbuf_tensor(name, list(shape), dt)), P.buf(name))

    def psum(stack, name):
        return Tile(stack.enter_context(nc.psum_tensor(name, [128, 512], F32)), P.buf(name))

    G = sbuf(gstack, "gv", [128, NG], F32)
    CM = sbuf(gstack, "cm", [128, 5, 128], BF16)
    TRI = sbuf(gstack, "tri", [128, 128], F32)
    ps = [psum(gstack, "ps%d" % i) for i in range(8)]

    def consts_load():
        dma("sp", G.t[:, :], gvec[:, :], (), [G.b])
        for i in range(5):
            dma("pool", CM.t[:, i, :], cmats[i, :, :], (), [CM.b])
        dma("sp", TRI.t[:, :], cmats[5, :, :], (), [TRI.b])
        ts("dve", G.t[:, GC_GQ:GC_GQ + 1], G.t[:, GC_GQ:GC_GQ + 1], float(96 ** -0.5), None, ALU.mult, None, [G.b], [G.b])
        ts("dve", G.t[:, GC_FQ:GC_FQ + 1], G.t[:, GC_FQ:GC_FQ + 1], float(64 ** -0.5), None, ALU.mult, None, [G.b], [G.b])

    ONES_ALL = lambda: CM.t[:, 0, :]
    ONES96 = lambda: CM.t[:, 1, :]
    ONES_BD = lambda: CM.t[:, 2, :]
    IDENT = lambda: CM.t[:, 3, :]
    NEGTRI = lambda: CM.t[:, 4, :]
    EPSC = lambda p0, p1: G.t[p0:p1, GC_EPS:GC_EPS + 1]

    def rmsnorm(src, nch, nfeat, gc0, out, sq, ssps, rs):
        tt("dve", sq.t[:, 0:nch, :], src.t[:, 0:nch, :], src.t[:, 0:nch, :], ALU.mult, [src.b], [sq.b])
        for c in range(nch):
            mm(ssps.t[:, :], ONES_ALL(), sq.t[:, c, :], c == 0, c == nch - 1, [CM.b, sq.b], [ssps.b])
        act(rs.t[:, :], ssps.t[:, :], AF.Sqrt, [ssps.b, G.b], [rs.b], bias=EPSC(0, 128), scale=1.0 / nfeat)
        recip(rs.t[:, :], rs.t[:, :], [rs.b], [rs.b])
        for c in range(nch):
            stt("dve", out.t[:, c, :], src.t[:, c, :], G.t[:, gc0 + c:gc0 + c + 1], rs.t[:, :], ALU.mult, ALU.mult,
                [src.b, G.b, rs.b], [out.b])

    def rstd_act(rs_ap, ss_ap, nfeat, rb, sb_, p0=0, p1=128):
        act(rs_ap, ss_ap, AF.Ln, [sb_, G.b], [rb], bias=EPSC(p0, p1), scale=1.0 / nfeat)
        act(rs_ap, rs_ap, AF.Exp, [rb], [rb], scale=-0.5)

    def stage_tok(stage):
        NH = 2
        ST = NH * TT
        with ExitStack() as st:
            hT = sbuf(st, "hT", [128, 8, ST], F32)
            xn = sbuf(st, "xn", [128, 8, ST], BF16)
            sq = sbuf(st, "sq", [128, 8, TT], BF16)
            aT = sbuf(st, "aT", [128, NC_FF, ST], BF16)
            rs = sbuf(st, "rs", [128, TT], F32)
            sg = Rot([sbuf(st, "sg%d" % i, [128, TT], F32) for i in range(2)])
            win = Rot([sbuf(st, "win%d" % i, [128, 2048], BF16) for i in range(3)])
            wom = Rot([sbuf(st, "wom%d" % i, [128, NC_FF, 128], BF16) for i in range(4)])
            ss_ps = ps[0]
            g_ps = Rot([ps[1], ps[2]])
            u_ps = Rot([ps[3], ps[4]])
            o_ps = Rot([ps[5], ps[6]])
            if stage in (1, 2):
                rT = sbuf(st, "rT", [128, 8, TT], F32)
                wg = sbuf(st, "wg", [128, 8, D], BF16)
                wp = sbuf(st, "wp", [128, 2, D], BF16)
                ptl = sbuf(st, "ptl", [128, 2, ST], BF16)
                li = stage - 1

                def rs_op(q):
                    P.op("pool", lambda e: e.collective_compute("ReduceScatter", ALU.add, replica_groups=RG,
                                                                ins=[pS[q, :, :]], outs=[pR[q, :, :]]),
                         (), [B_pRq[q]], cc=True)
                dma("pool", wg.t[:, :, :], wgate[li, :, :, :], (), [wg.b])
                dma("pool", wp.t[:, :, :], wproj[li, :, :, :], (), [wp.b])
            if stage == 0:
                mw = sbuf(st, "mw", [128, 8, 800], BF16)
                dma("pool", mw.t[:, :, :], mla_win[:, :, :], (), [mw.b])
                zf = sbuf(st, "zf", [128, 6, TT], F32)
                zn = sbuf(st, "zn", [128, 6, TT], BF16)
                kpe = sbuf(st, "kpe", [32, TT], BF16)

            hb = [P.buf("hA"), P.buf("hB")]
            xb = [P.buf("xA"), P.buf("xB")]
            ab = [P.buf("aA"), P.buf("aB")]

            def norm_sq(hf):
                cs = slice(hf * TT, (hf + 1) * TT)
                tt("dve", sq.t[:, :, :], hT.t[:, :, cs], hT.t[:, :, cs], ALU.mult, [hb[hf]], [sq.b])

            def norm_fin(hf, gc0):
                cs = slice(hf * TT, (hf + 1) * TT)
                for c in range(8):
                    mm(ss_ps.t[:, :], ONES_ALL(), sq.t[:, c, :], c == 0, c == 7, [CM.b, sq.b], [ss_ps.b])
                rstd_act(rs.t[:, :], ss_ps.t[:, :], D, rs.b, ss_ps.b)
                for c in range(8):
                    stt("dve", xn.t[:, c, cs], hT.t[:, c, cs], G.t[:, gc0 + c:gc0 + c + 1], rs.t[:, :], ALU.mult, ALU.mult,
                        [hb[hf], G.b, rs.b], [xb[hf]])

            def norm_half(hf, gc0):
                norm_sq(hf)
                norm_fin(hf, gc0)

            def norm_h(gc0):
                for hf in range(NH):
                    norm_half(hf, gc0)

            def ffn(li, win_d, wout_d, next_gc):
                wo_pref = []

                def wo_load(m):
                    wq = wom.next()
                    dma("pool", wq.t[:, :, :], wout_d[li, :, :, m * 128:(m + 1) * 128], (), [wq.b])
                    wo_pref.append(wq)

                for c in range(NC_FF):
                    w = win.next()
                    dma("pool", w.t[:, :], win_d[li, c, :, :], (), [w.b])
                    if c in (8, 12, 16, 20):
                        wo_load(len(wo_pref))
                    for hf in range(NH):
                        cs = slice(hf * TT, (hf + 1) * TT)
                        gp = g_ps.next()
                        up = u_ps.next()
                        for half, pp in ((0, gp), (1, up)):
                            for k in range(8):
                                mm(pp.t[:, :], w.t[:, (half * 8 + k) * 128:(half * 8 + k + 1) * 128], xn.t[:, k, cs],
                                   k == 0, k == 7, [w.b, xb[hf]], [pp.b])
                        s = sg.next()
                        act(s.t[:, :], gp.t[:, :], AF.Silu, [gp.b], [s.b])
                        tt("dve", aT.t[:, c, cs], s.t[:, :], up.t[:, :], ALU.mult, [s.b, up.b], [ab[hf]])
                for g4 in range(2):
                    for hf in range(NH):
                        cs = slice(hf * TT, (hf + 1) * TT)
                        for m in range(4 * g4, 4 * g4 + 4):
                            wq = wo_pref[m]
                            op_ = o_ps.next()
                            for c in range(NC_FF):
                                mm(op_.t[:, :], wq.t[:, c, :], aT.t[:, c, cs], c == 0, c == NC_FF - 1, [wq.b, ab[hf]], [op_.b])
                            stt("dve", hT.t[:, m, cs], op_.t[:, :], 0.5, hT.t[:, m, cs], ALU.mult, ALU.add, [op_.b, hb[hf]], [hb[hf]])
                            if hf == NH - 1 and len(wo_pref) < 8:
                                wo_load(len(wo_pref))
                            if g4 == 1 and next_gc is not None and hf == 1 and m == 5:
                                norm_fin(0, next_gc)
                        if g4 == 1 and next_gc is not None:
                            if hf == 0:
                                norm_sq(0)
                            else:
                                norm_half(1, next_gc)

            def ple(li, t, next_gc):
                dma("pool", ptl.t[:, :, :], pT[li, :, t * ST:(t + 1) * ST].rearrange("(c p) n -> p c n", p=128), (), [ptl.b])
                for hf in range(NH):
                    cs = slice(hf * TT, (hf + 1) * TT)
                    for m in range(8):
                        gp = g_ps.next()
                        up = u_ps.next()
                        for k in range(8):
                            mm(gp.t[:, :], wg.t[:, k, m * 128:(m + 1) * 128], xn.t[:, k, cs], k == 0, k == 7, [wg.b, xb[hf]], [gp.b])
                        for k in range(2):
                            mm(up.t[:, :], wp.t[:, k, m * 128:(m + 1) * 128], ptl.t[:, k, cs], k == 0, k == 1, [wp.b, ptl.b], [up.b])
                        s = sg.next()
                        act(s.t[:, :], gp.t[:, :], AF.Sigmoid, [gp.b], [s.b])
                        tt("dve", s.t[:, :], s.t[:, :], up.t[:, :], ALU.mult, [s.b, up.b], [s.b])
                        tt("dve", hT.t[:, m, cs], hT.t[:, m, cs], s.t[:, :], ALU.add, [hb[hf], s.b], [hb[hf]])
                        if next_gc is not None and hf == 1 and m == 1:
                            norm_fin(0, next_gc)
                    if next_gc is not None:
                        if hf == 0:
                            norm_sq(0)
                        else:
                            norm_half(1, next_gc)

            def mla_pre(t):
                for hf in range(NH):
                    cs = slice(hf * TT, (hf + 1) * TT)
                    for zc in range(7):
                        gp = g_ps.next()
                        mcols = 128 if zc < 6 else 32
                        for k in range(8):
                            mm(gp.t[0:mcols, :], mw.t[:, k, zc * 128:zc * 128 + mcols], xn.t[:, k, cs], k == 0, k == 7,
                               [mw.b, xb[hf]], [gp.b])
                        if zc < 6:
                            act(zf.t[:, zc, :], gp.t[:, :], AF.Copy, [gp.b], [zf.b])
                        else:
                            act(kpe.t[:, :], gp.t[0:32, :], AF.Copy, [gp.b], [kpe.b])
                    for (c0, nch, nf, gc) in ((0, 4, 512, GC_QLAT), (4, 2, 256, GC_KVLAT)):
                        tt("dve", sq.t[:, c0:c0 + nch, :], zf.t[:, c0:c0 + nch, :], zf.t[:, c0:c0 + nch, :], ALU.mult, [zf.b], [sq.b])
                        for c in range(nch):
                            mm(ss_ps.t[:, :], ONES_ALL(), sq.t[:, c0 + c, :], c == 0, c == nch - 1, [CM.b, sq.b], [ss_ps.b])
                        rstd_act(rs.t[:, :], ss_ps.t[:, :], nf, rs.b, ss_ps.b)
                        for c in range(nch):
                            stt("dve", zn.t[:, c0 + c, :], zf.t[:, c0 + c, :], G.t[:, gc + c:gc + c + 1], rs.t[:, :], ALU.mult, ALU.mult,
                                [zf.b, G.b, rs.b], [zn.b])
                    for h2 in range(2):
                        c2 = slice(h2 * 256, (h2 + 1) * 256)
                        idx = 2 * (NH * t + hf) + h2
                        dma("sp", agS0[idx, 0:768, :].rearrange("(c p) n -> p c n", p=128), zn.t[:, :, c2], [zn.b], iw=[B_agS0])
                        dma("sp", agS0[idx, 768:800, :], kpe.t[:, c2], [kpe.b], iw=[B_agS0])
                        allgather(agS0, agR0, idx, B_agS0, B_agR0)

            for t in range(OPT.get('ntiles', NT // ST)):
                tok = slice(t * ST, (t + 1) * ST)
                if stage in (0, 3):
                    dma("sp", hT.t[:, :, :], xT[:, tok].rearrange("(c p) n -> p c n", p=128), (), hb)
                else:
                    dma("sp", hT.t[:, :, :], hS[:, tok].rearrange("(c p) n -> p c n", p=128), [B_hS], hb)
                    for hf in range(NH):
                        cs = slice(hf * TT, (hf + 1) * TT)
                        dma("sp", rT.t[:, :, :], pR[t, :, cs].rearrange("(c p) n -> p c n", p=128), [B_pRq[t]], [rT.b])
                        tt("dve", hT.t[:, :, cs], hT.t[:, :, cs], rT.t[:, :, :], ALU.add, [hb[hf], rT.b], [hb[hf]])
                    li = stage - 1
                    norm_h(GC_FFN2 + 8 * li)
                    ffn(li, w2in, w2out, GC_PLE + 8 * li)
                    ple(li, t, GC_FFN1 + 8 if stage == 1 else None)
                if stage == 0:
                    norm_h(GC_FFN1 + 0)
                    ffn(0, w1in, w1out, GC_MIX + 0)
                    dma("sp", hS[:, tok].rearrange("(c p) n -> p c n", p=128), hT.t[:, :, :], hb, iw=[B_hS])
                    mla_pre(t)
                elif stage in (1, 3):
                    if stage == 3:
                        norm_h(GC_FFN1 + 8)
                    ffn(1, w1in, w1out, GC_MIX + 8)
                    dma("sp", hS[:, tok].rearrange("(c p) n -> p c n", p=128), hT.t[:, :, :], hb, iw=[B_hS])
                    for hf in range(NH):
                        for h2 in range(2):
                            c2 = slice(hf * TT + h2 * 256, hf * TT + (h2 + 1) * 256)
                            idx = 2 * (NH * t + hf) + h2
                            dma("sp", agS1[idx, :, :].rearrange("(c p) n -> p c n", p=128), xn.t[:, :, c2], [xb[hf]], iw=[B_agS1])
                            allgather(agS1, agR1, idx, B_agS1, B_agR1)
                else:
                    dma("sp", yT[:, tok].rearrange("(c p) n -> p c n", p=128), hT.t[:, :, :], hb, iw=[B_y])
            P.flush()

    def stage_proj(li):
        with ExitStack() as st:
            sqh = Rot([sbuf(st, "sqh%d" % i, [128, TT], BF16) for i in range(3)])
            rsh = Rot([sbuf(st, "rsh%d" % i, [128, TT], F32) for i in range(3)])
            vt = Rot([sbuf(st, "vt%d" % i, [128, 4, 256], BF16) for i in range(2)])
            qk_ps = Rot([ps[0], ps[1], ps[2], ps[7]] if li == 0 else [ps[0], ps[1], ps[2]])
            ss_ps = Rot([ps[3], ps[4], ps[5]] if li == 0 else [ps[3], ps[4]])
            v_ps = Rot([ps[6]] if li == 0 else [ps[5], ps[6]])
            f_ps = ps[7]
            Vs_v = Vs[:, :, :].rearrange("h p (b d) -> h p b d", d=64)
            if li == 0:
                w_q = sbuf(st, "w_q", [128, 4, 512], BF16)
                w_k = sbuf(st, "w_k", [128, 2, 512], BF16)
                w_v = sbuf(st, "w_v", [128, 2, 256], BF16)
                sel = sbuf(st, "sel", [32, 128], BF16)
                dma("pool", w_q.t[:, :, :], wuq[:, :, :], (), [w_q.b])
                dma("pool", w_k.t[:, :, :], wukvk[:, :, :], (), [w_k.b])
                dma("pool", w_v.t[:, :, :], wukvv[:, :, :], (), [w_v.b])
                dma("pool", sel.t[:, :], selm[:, :], (), [sel.b])
                zq = Rot([sbuf(st, "zq%d" % i, [128, 4, TT], BF16) for i in range(2)])
                zkv = Rot([sbuf(st, "zkv%d" % i, [128, 2, TT], BF16) for i in range(2)])
                kp = Rot([sbuf(st, "kp%d" % i, [32, TT], BF16) for i in range(2)])
                posi = sbuf(st, "posi", [128, TT], I32)
                ang = sbuf(st, "ang", [128, TT], F32)
                nfl = sbuf(st, "nfl", [128, TT], F32)
                nin = sbuf(st, "nin", [128, TT], I32)
                msk = sbuf(st, "msk", [128, TT], F32)
                cos_tb = [sbuf(st, "cos_t%d" % i, [128, TT], F32) for i in range(2)]
                sin_tb = [sbuf(st, "sin_t%d" % i, [128, TT], F32) for i in range(2)]
                qn = Rot([sbuf(st, "qn%d" % i, [128, TT], F32) for i in range(3)])
                sw = Rot([sbuf(st, "sw%d" % i, [128, TT], F32) for i in range(3)])
                t1 = Rot([sbuf(st, "t1%d" % i, [128, TT], F32) for i in range(3)])
                qo = Rot([sbuf(st, "qo%d" % i, [128, TT], BF16) for i in range(5)])
                R_ = slice(64, 96)

                def wrap(x):
                    for (cmpop, thr, add) in ((ALU.is_gt, PI, -2 * PI), (ALU.is_lt, -PI, 2 * PI)):
                        ts("dve", msk.t[R_, :], x.t[R_, :], thr, add, cmpop, ALU.mult, [x.b], [msk.b])
                        tt("dve", x.t[R_, :], x.t[R_, :], msk.t[R_, :], ALU.add, [x.b, msk.b], [x.b])

                def rope_tables(T):
                    cos_t = cos_tb[T % 2]
                    sin_t = sin_tb[T % 2]
                    dma("sp", posi.t[R_, :], pos[:, T * TT:(T + 1) * TT], (), [posi.b])
                    cp("dve", ang.t[R_, :], posi.t[R_, :], [posi.b], [ang.b])
                    ts("dve", ang.t[R_, :], ang.t[R_, :], G.t[R_, GC_INVF:GC_INVF + 1], None, ALU.mult, None, [ang.b, G.b], [ang.b])
                    ts("dve", nfl.t[R_, :], ang.t[R_, :], 1.0 / (2 * PI), None, ALU.mult, None, [ang.b], [nfl.b])
                    cp("dve", nin.t[R_, :], nfl.t[R_, :], [nfl.b], [nin.b])
                    cp("dve", nfl.t[R_, :], nin.t[R_, :], [nin.b], [nfl.b])
                    stt("dve", ang.t[R_, :], nfl.t[R_, :], -C1, ang.t[R_, :], ALU.mult, ALU.add, [nfl.b, ang.b], [ang.b])
                    stt("dve", ang.t[R_, :], nfl.t[R_, :], -C2, ang.t[R_, :], ALU.mult, ALU.add, [nfl.b, ang.b], [ang.b])
                    wrap(ang)
                    act(sin_t.t[R_, :], ang.t[R_, :], AF.Sin, [ang.b], [sin_t.b])
                    ts("dve", sin_t.t[R_, :], sin_t.t[R_, :], G.t[R_, GC_SGN:GC_SGN + 1], None, ALU.mult, None, [sin_t.b, G.b], [sin_t.b])
                    ts("dve", ang.t[R_, :], ang.t[R_, :], PI / 2, None, ALU.add, None, [ang.b], [ang.b])
                    wrap(ang)
                    act(cos_t.t[R_, :], ang.t[R_, :], AF.Sin, [ang.b], [cos_t.b])

                def hA(pq):
                    s_ = sqh.next()
                    act(s_.t[:, :], pq.t[:, :], AF.Square, [pq.b], [s_.b])
                    sp_ = ss_ps.next()
                    mm(sp_.t[:, :], ONES96(), s_.t[:, :], True, True, [CM.b, s_.b], [sp_.b])
                    return sp_

                def hB(pq, sp_, gc):
                    r_ = rsh.next()
                    rstd_act(r_.t[:, :], sp_.t[:, :], 96, r_.b, sp_.b)
                    o_ = qo.next()
                    stt("dve", o_.t[:, :], pq.t[:, :], G.t[:, gc:gc + 1], r_.t[:, :], ALU.mult, ALU.mult, [pq.b, G.b, r_.b], [o_.b])
                    return o_

                def hC(o_, dst, bdst, T):
                    cos_t = cos_tb[T % 2]
                    sin_t = sin_tb[T % 2]
                    w_ = sw.next()
                    act(w_.t[64:96, :], o_.t[96:128, :], AF.Copy, [o_.b], [w_.b])
                    a_ = t1.next()
                    tt("dve", a_.t[R_, :], o_.t[R_, :], cos_t.t[R_, :], ALU.mult, [o_.b, cos_t.b], [a_.b])
                    tt("dve", w_.t[R_, :], w_.t[R_, :], sin_t.t[R_, :], ALU.mult, [w_.b, sin_t.b], [w_.b])
                    tt("dve", o_.t[R_, :], a_.t[R_, :], w_.t[R_, :], ALU.add, [a_.b, w_.b, o_.b], [o_.b])
                    dma("sp", dst, o_.t[0:96, :], [o_.b], iw=[bdst])

                G0 = agR0[:, :, :].rearrange("i (r f) n -> i r f n", r=4)
                def mla_load(T):
                    r_, lt = T // 8, T % 8
                    a = zq.next()
                    b = zkv.next()
                    c = kp.next()
                    for hf in range(2):
                        cs = slice(hf * 256, (hf + 1) * 256)
                        dma("sp", a.t[:, :, cs], G0[2 * lt + hf, r_, 0:512, :].rearrange("(c p) n -> p c n", p=128), [B_agR0], [a.b])
                        dma("sp", b.t[:, :, cs], G0[2 * lt + hf, r_, 512:768, :].rearrange("(c p) n -> p c n", p=128), [B_agR0], [b.b])
                        dma("sp", c.t[:, cs], G0[2 * lt + hf, r_, 768:800, :], [B_agR0], [c.b])
                    return a, b, c

                NTL = S // TT
                tin = {0: mla_load(0)}
                rope_tables(0)

                def v_work(T):
                    a, b, c = tin[T]
                    v_ = vt.next()
                    for blk in range(4):
                        pv = v_ps.next()
                        for k in range(2):
                            mm(pv.t[:, 0:256], b.t[:, k, blk * 128:(blk + 1) * 128], w_v.t[:, k, :], k == 0, k == 1, [b.b, w_v.b], [pv.b])
                        act(v_.t[:, blk, :], pv.t[:, 0:256], AF.Copy, [pv.b], [v_.b])
                    for h in range(4):
                        dma("sp", Vs_v[h, :, T * 4:(T + 1) * 4, :], v_.t[:, :, h * 64:(h + 1) * 64], [v_.b], iw=[B_Vs])

                jobs = [(T, kind, h) for T in range(NTL) for h in range(4) for kind in ("q", "k")]
                stA, stB = [], []
                for step in range(len(jobs) + 2):
                    if step < len(jobs):
                        T, kind, h = jobs[step]
                        jn = step % 8
                        a, b, c = tin[T]
                        gcol = slice(T * TT, (T + 1) * TT)
                        if jn == 0 and T + 1 < NTL:
                            tin[T + 1] = mla_load(T + 1)
                        pq = qk_ps.next()
                        if kind == "q":
                            for k in range(4):
                                mm(pq.t[:, :], w_q.t[:, k, h * 128:(h + 1) * 128], a.t[:, k, :], k == 0, k == 3, [w_q.b, a.b], [pq.b])
                            info = (GC_GQ, Qs[h, 0:96, gcol], B_Qs, T)
                        else:
                            for k in range(2):
                                mm(pq.t[:, :], w_k.t[:, k, h * 128:(h + 1) * 128], b.t[:, k, :], k == 0, False, [w_k.b, b.b], [pq.b])
                            mm(pq.t[:, :], sel.t[:, :], c.t[:, :], False, True, [sel.b, c.b], [pq.b])
                            info = (GC_GK, Ks[h, 0:96, gcol], B_Ks, T)
                        stA.append((pq, hA(pq), info))
                        if jn == 3 and T + 1 < NTL:
                            rope_tables(T + 1)
                        if jn == 5:
                            v_work(T)
                    if step >= 1 and stA and step - 1 < len(jobs):
                        pq, sp_, info = stA.pop(0)
                        stB.append((hB(pq, sp_, info[0]), info))
                    if step >= 2 and stB:
                        o_, info = stB.pop(0)
                        hC(o_, info[1], info[2], info[3])
            else:
                w_q = sbuf(st, "f_q", [128, 8, 256], BF16)
                w_k = sbuf(st, "f_k", [128, 8, 256], BF16)
                w_v = sbuf(st, "f_v", [128, 8, 256], BF16)
                w_f = sbuf(st, "f_f", [128, 8, 4], BF16)
                dma("pool", w_q.t[:, :, :], fwq[:, :, :], (), [w_q.b])
                dma("pool", w_k.t[:, :, :], fwk[:, :, :], (), [w_k.b])
                dma("pool", w_v.t[:, :, :], fwv[:, :, :], (), [w_v.b])
                dma("pool", w_f.t[:, :, :], fwf[:, :, :], (), [w_f.b])
                hn = Rot([sbuf(st, "hn%d" % i, [128, 8, TT], BF16) for i in range(2)])
                qo = Rot([sbuf(st, "fqo%d" % i, [128, TT], BF16) for i in range(4)])
                lf = Rot([sbuf(st, "lf%d" % i, [4, TT], F32) for i in range(2)])
                G1 = agR1[:, :, :].rearrange("i (r f) n -> i r f n", r=4)

                def head_norm(pq, gc, dstT, B_dst, pair, gcol):
                    s_ = sqh.next()
                    act(s_.t[:, :], pq.t[:, :], AF.Square, [pq.b], [s_.b])
                    sp_ = ss_ps.next()
                    mm(sp_.t[:, :], ONES_BD(), s_.t[:, :], True, True, [CM.b, s_.b], [sp_.b])
                    r_ = rsh.next()
                    rstd_act(r_.t[:, :], sp_.t[:, :], 64, r_.b, sp_.b)
                    o_ = qo.next()
                    stt("dve", o_.t[:, :], pq.t[:, :], G.t[:, gc:gc + 1], r_.t[:, :], ALU.mult, ALU.mult, [pq.b, G.b, r_.b], [o_.b])
                    for j in range(2):
                        dma("sp", dstT[2 * pair + j, 0:64, gcol], o_.t[64 * j:64 * j + 64, :], [o_.b], iw=[B_dst])

                def fox_load(T):
                    r_, lt = T // 8, T % 8
                    a = hn.next()
                    for hf in range(2):
                        cs = slice(hf * 256, (hf + 1) * 256)
                        dma("sp", a.t[:, :, cs], G1[2 * lt + hf, r_, :, :].rearrange("(c p) n -> p c n", p=128), [B_agR1], [a.b])
                    return a

                nxt = fox_load(0)
                for T in range(S // TT):
                    gcol = slice(T * TT, (T + 1) * TT)
                    a = nxt
                    if T + 1 < S // TT:
                        nxt = fox_load(T + 1)
                    for pair in range(2):
                        pq = qk_ps.next()
                        for k in range(8):
                            mm(pq.t[:, :], w_q.t[:, k, pair * 128:(pair + 1) * 128], a.t[:, k, :], k == 0, k == 7, [w_q.b, a.b], [pq.b])
                        head_norm(pq, GC_FQ, Qs, B_Qs, pair, gcol)
                        pk = qk_ps.next()
                        for k in range(8):
                            mm(pk.t[:, :], w_k.t[:, k, pair * 128:(pair + 1) * 128], a.t[:, k, :], k == 0, k == 7, [w_k.b, a.b], [pk.b])
                        head_norm(pk, GC_FK, Ks, B_Ks, pair, gcol)
                    v_ = vt.next()
                    for blk in range(4):
                        pv = v_ps.next()
                        for k in range(8):
                            mm(pv.t[:, 0:256], a.t[:, k, blk * 128:(blk + 1) * 128], w_v.t[:, k, :], k == 0, k == 7, [a.b, w_v.b], [pv.b])
                        act(v_.t[:, blk, :], pv.t[:, 0:256], AF.Copy, [pv.b], [v_.b])
                    for h in range(4):
                        dma("sp", Vs_v[h, :, T * 4:(T + 1) * 4, :], v_.t[:, :, h * 64:(h + 1) * 64], [v_.b], iw=[B_Vs])
                    for k in range(8):
                        mm(f_ps.t[0:4, :], w_f.t[:, k, :], a.t[:, k, :], k == 0, k == 7, [w_f.b, a.b], [f_ps.b])
                    l_ = lf.next()
                    act(l_.t[:, :], f_ps.t[0:4, :], AF.Sigmoid, [f_ps.b, G.b], [l_.b], bias=G.t[0:4, GC_BF:GC_BF + 1])
                    act(l_.t[:, :], l_.t[:, :], AF.Ln, [l_.b], [l_.b])
                    dma("sp", Lf[:, gcol], l_.t[:, :], [l_.b], iw=[B_Lf])
                L = sbuf(st, "L", [128, TT], F32)
                one = sbuf(st, "one", [128, TT], F32)
                sc = sbuf(st, "sc", [128, TT], F32)
                hi = sbuf(st, "hi", [128, TT], BF16)
                lo = sbuf(st, "lo", [128, TT], BF16)
                onb = sbuf(st, "onb", [128, TT], BF16)
                off = sbuf(st, "off", [128, 1], F32)
                dma("sp", L.t[:, :], Lf[:, :].rearrange("h (t n) -> (h t) n", n=TT), [B_Lf], [L.b])
                memset("dve", one.t[:, :], 1.0, [one.b])
                memset("pool", onb.t[:, :], 1.0, [onb.b])
                P.op("dve", lambda e: e.tensor_tensor_scan(out=sc.t[:, :], data0=one.t[:, :], data1=L.t[:, :], initial=0.0,
                                                           op0=ALU.mult, op1=ALU.add), [one.b, L.b], [sc.b])
                pq = ps[0]
                mm(pq.t[:, 0:1], TRI.t[:, :], sc.t[:, TT - 1:TT], True, True, [TRI.b, sc.b], [pq.b])
                cp("dve", off.t[:, :], pq.t[:, 0:1], [pq.b], [off.b])
                dma("sp", Os[:, :], off.t[:, :], [off.b], iw=[B_Os])
                cp("dve", hi.t[:, :], sc.t[:, :], [sc.b], [hi.b])
                tt("dve", lo.t[:, :], sc.t[:, :], hi.t[:, :], ALU.subtract, [sc.b, hi.b], [lo.b])
                for h in range(4):
                    dma("sp", Qs[h, 64, :].rearrange("(t n) -> t n", n=TT), hi.t[32 * h:32 * h + 32, :], [hi.b], iw=[B_Qs])
                    dma("sp", Qs[h, 65, :].rearrange("(t n) -> t n", n=TT), lo.t[32 * h:32 * h + 32, :], [lo.b], iw=[B_Qs])
                for h in range(4):
                    dma("sp", Ks[h, 64:66, :].rearrange("a (t n) -> (a t) n", n=TT), onb.t[0:64, :], [onb.b], iw=[B_Ks])
                ts("dve", sc.t[:, :], sc.t[:, :], off.t[:, 0:1], None, ALU.add, None, [sc.b, off.b], [sc.b])
                dma("sp", Cs[:, :].rearrange("h (t n) -> (h t) n", n=TT), sc.t[:, :], [sc.b], iw=[B_Cs])
            P.flush()

    def stage_attn(li):
        dk = 96 if li == 0 else 66
        wo_d = mla_wo if li == 0 else fox_wo
        with ExitStack() as st:
            Kt = sbuf(st, "Kt", [96, S], BF16)
            Vt = sbuf(st, "Vt", [128, 128, 128], BF16)
            Ot = sbuf(st, "Ot", [128, 2, S], BF16)
            Qt = Rot([sbuf(st, "Qt%d" % i, [96, TT], BF16) for i in range(4)])
            LOOKAHEAD = 2
            Pt = Rot([sbuf(st, "Pt%d" % i, [128, TT], BF16) for i in range(4)])
            rsum = Rot([sbuf(st, "rsum%d" % i, [128, TT], F32) for i in range(2)])
            wo = sbuf(st, "wo", [128, 2, D], BF16)
            pt = Rot([sbuf(st, "pt%d" % i, [128, 4, TT], F32) for i in range(7)])
            s_ps = Rot([ps[0], ps[1], ps[2], ps[7]])
            o_ps = Rot([ps[3], ps[4]])
            p_ps = Rot([ps[5], ps[6]])
            dma("pool", wo.t[:, :, :], wo_d[:, :, :], (), [wo.b])
            if li == 1:
                ck = sbuf(st, "ck", [128, 128], F32)
                Rb = sbuf(st, "Rb", [128, 32], F32)
                bT = Rot([sbuf(st, "bT%d" % i, [128, 128], F32) for i in range(2)])
            Vs_v = Vs[:, :, :].rearrange("h p (b d) -> h p b d", d=64)
            pS_v = pS[:, :, :].rearrange("q (r f) n -> q r f n", r=4)
            b3_done = [0, 0, 0, 0]

            def b3(T):
                lt_ = T % 8
                q = lt_ // 2
                for mh in range(2):
                    x_ = pt.next()
                    for m4 in range(4):
                        m = mh * 4 + m4
                        pp = p_ps.next()
                        for pr in range(2):
                            mm(pp.t[:, :], wo.t[:, pr, m * 128:(m + 1) * 128], Ot.t[:, pr, T * TT:(T + 1) * TT], pr == 0, pr == 1,
                               [wo.b, Ot.b], [pp.b])
                        cp("dve", x_.t[:, m4, :], pp.t[:, :], [pp.b], [x_.b])
                    dma("pool", pS_v[q, T // 8, mh * 512:(mh + 1) * 512, (lt_ % 2) * TT:(lt_ % 2 + 1) * TT].rearrange("(c p) n -> p c n", p=128),
                        x_.t[:, :, :], [x_.b], iw=[B_pSq[q]])
                b3_done[q] += 1
                if b3_done[q] == 8 and not OPT.get('nors'):
                    P.op("pool", lambda e: e.collective_compute("ReduceScatter", ALU.add, replica_groups=RG,
                                                                ins=[pS[q, :, :]], outs=[pR[q, :, :]]),
                         [B_pSq[q]], [B_pRq[q]], cc=True)

            for h in range(4):
                odd = h % 2
                pair = h // 2
                vo = 64 if odd else 0
                so = 0 if odd else 64
                dma("sp", Kt.t[0:dk, :], Ks[h, 0:dk, :], [B_Ks], [Kt.b])
                memset("pool", Vt.t[:, :, so:so + 64], 1.0, [Vt.b])
                dma("sp", Vt.t[:, :, vo:vo + 64], Vs_v[h, :, :, :], [B_Vs], [Vt.b])
                if li == 1:
                    dma("sp", ck.t[:, :], Cs[h, :].rearrange("(b p) -> p b", p=128), [B_Cs], [ck.b], slow=True)
                    dma("sp", Rb.t[:, :], Os[h * 32:(h + 1) * 32, :].rearrange("t o -> o t").partition_broadcast(128),
                        [B_Os], [Rb.b], slow=True)
                    ts("dve", ck.t[:, :], ck.t[:, :], -1.0, None, ALU.mult, None, [ck.b], [ck.b])
                nq = OPT.get('nqg', S // TT)
                if h == 3 and nq == S // TT:
                    Torder = [8 * r + 2 * q + l2 for q in range(4) for r in range(4) for l2 in range(2)]
                else:
                    Torder = list(range(nq))
                tiles = [(T, i) for T in Torder for i in range(4 * T + 4)]
                b3_pend = []
                st_T = {}
                pend = []

                def issue_pv(T, i, p_, c0):
                    q_, b_, ob = st_T[T]
                    nblk = 4 * T + 4
                    mm(ob.t[:, c0:TT], Vt.t[:, i, :], p_.t[:, c0:TT], i == 0, i == nblk - 1, [Vt.b, p_.b], [ob.b])
                    if i == nblk - 1:
                        r_ = rsum.next()
                        act(r_.t[vo:vo + 64, :], ob.t[so:so + 64, :], AF.Copy, [ob.b], [r_.b])
                        recip(r_.t[vo:vo + 64, :], r_.t[vo:vo + 64, :], [r_.b], [r_.b])
                        tt("dve", Ot.t[vo:vo + 64, pair, T * TT:(T + 1) * TT], ob.t[vo:vo + 64, :], r_.t[vo:vo + 64, :], ALU.mult,
                           [ob.b, r_.b], [Ot.b])
                        del st_T[T]
                        if h == 3:
                            b3_pend.append(T)
                            if len(b3_pend) > 1:
                                b3(b3_pend.pop(0))

                for (T, i) in tiles:
                    if i == 0:
                        q_ = Qt.next()
                        dma("sp", q_.t[0:dk, :], Qs[h, 0:dk, T * TT:(T + 1) * TT], [B_Qs], [q_.b])
                        b_ = None
                        if li == 1:
                            b_ = bT.next()
                            nblk = 4 * T + 4
                            ts("dve", b_.t[:, 0:nblk], ck.t[:, 0:nblk], Rb.t[:, T:T + 1], None, ALU.add, None, [ck.b, Rb.b], [b_.b])
                        st_T[T] = (q_, b_, o_ps.next())
                    q_, b_, ob = st_T[T]
                    m = i - 4 * T
                    c0 = 128 * m if m > 0 else 0
                    sb_ = s_ps.next()
                    mm(sb_.t[:, c0:TT], Kt.t[0:dk, i * 128:(i + 1) * 128], q_.t[0:dk, c0:TT], True, m < 0, [Kt.b, q_.b], [sb_.b])
                    if m >= 0:
                        mm(sb_.t[:, c0:c0 + 128], IDENT(), NEGTRI(), False, True, [CM.b], [sb_.b])
                    p_ = Pt.next()
                    if li == 1:
                        act(p_.t[:, c0:TT], sb_.t[:, c0:TT], AF.Exp, [sb_.b, b_.b], [p_.b], bias=b_.t[:, i:i + 1])
                    else:
                        act(p_.t[:, c0:TT], sb_.t[:, c0:TT], AF.Exp, [sb_.b], [p_.b])
                    pend.append((T, i, p_, c0))
                    if len(pend) > LOOKAHEAD:
                        issue_pv(*pend.pop(0))
                while pend:
                    issue_pv(*pend.pop(0))
                if h == 3:
                    while b3_pend:
                        b3(b3_pend.pop(0))
            P.flush()

    def dump():
        tens = {"hS": hS, "agR0": agR0, "agR1": agR1, "Qs": Qs, "Ks": Ks, "Vs": Vs, "pR": pR, "Cs": Cs, "Lf": Lf, "pS": pS}
        for nm in dbg:
            t = tens[nm]
            ext = nc.dram_tensor("dbg_" + nm, list(t.shape), t.dtype, kind="ExternalOutput")
            if len(t.shape) == 3:
                dma("sp", ext[:, :, :], t[:, :, :], (), ())
            elif nm == "pS":
                dma("sp", ext[0, 0:D, :], t[0, 0:D, :], (), ())
            elif nm == "hS":
                nn = OPT.get('ntiles', 8) * TT
                dma("sp", ext[:, 0:nn], t[:, 0:nn], (), ())
            else:
                dma("sp", ext[:, :], t[:, :], (), ())
        P.flush()

    consts_load()
    if OPT.get('fox_first'):
        stage_tok(3)
        stage_proj(1)
        if stop_after != "proj1":
            stage_attn(1)
        P.flush()
        dump()
        gstack.close()
        return nc
    stage_tok(0)
    if stop_after != "tok0":
        stage_proj(0)
        if stop_after != "proj0":
            stage_attn(0)
            if stop_after != "attn0":
                stage_tok(1)
                stage_proj(1)
                stage_attn(1)
                stage_tok(2)
    P.flush()
    dump()
    gstack.close()
    return nc


GC_FFN1 = 0
GC_MIX = 16
GC_FFN2 = 32
GC_PLE = 48
GC_QLAT = 64
GC_KVLAT = 68
GC_GQ = 70
GC_GK = 71
GC_FQ = 72
GC_FK = 73
GC_INVF = 74
GC_SGN = 75
GC_BF = 76
GC_EPS = 77
NG = 78
OPT = {}


def _chunk_cols(v):
    return np.ascontiguousarray(v.reshape(-1, 128).T)


def prep_inputs(inp):
    f32 = np.float32
    x = inp["x"]
    p = inp["p"]
    positions = inp["positions"]
    common = {}
    for nm, key in (("w1in", "ffn1_w_in"), ("w2in", "ffn2_w_in")):
        w = inp[key].reshape(2, 8, 128, 2, NC_FF, 128)
        w = w.transpose(0, 4, 2, 3, 1, 5)
        common[nm] = np.ascontiguousarray(w).reshape(2, NC_FF, 128, 2048)
    for nm, key in (("w1out", "ffn1_w_out"), ("w2out", "ffn2_w_out")):
        w = inp[key].reshape(2, NC_FF, 128, D).transpose(0, 2, 1, 3)
        common[nm] = np.ascontiguousarray(w)
    common["wgate"] = np.ascontiguousarray(inp["ple_w_gate"].reshape(2, 8, 128, D).transpose(0, 2, 1, 3))
    common["wproj"] = np.ascontiguousarray(inp["ple_w_proj"].reshape(2, 2, 128, D).transpose(0, 2, 1, 3))
    common["mla_win"] = np.ascontiguousarray(inp["mla_w_in"][0].reshape(8, 128, 800).transpose(1, 0, 2))
    cm = np.zeros((6, 128, 128), f32)
    cm[0] = 1.0
    cm[1, :96, :] = 1.0
    cm[2, :64, :64] = 1.0
    cm[2, 64:, 64:] = 1.0
    cm[3] = np.eye(128, dtype=f32)
    kk, qq = np.meshgrid(np.arange(128), np.arange(128), indexing="ij")
    cm[4] = np.where(kk > qq, -30000.0, 0.0)
    hh, tt_ = np.arange(128) // 32, np.arange(128) % 32
    cm[5] = ((hh[:, None] == hh[None, :]) & (tt_[:, None] < tt_[None, :])).astype(f32)
    common["cmats"] = cm
    sel = np.zeros((32, 128), f32)
    for i in range(32):
        sel[i, 64 + i] = 1.0
    for j in range(32):
        sel[(j + 16) % 32, 96 + j] = 1.0
    common["selm"] = sel
    swap = np.concatenate([np.arange(16, 32), np.arange(0, 16)])
    w_uq = inp["mla_w_uq"][0].reshape(512, 16, 96)
    w_uq_ext = np.concatenate([w_uq, w_uq[:, :, 64 + swap]], axis=2)
    w_ukv = inp["mla_w_ukv"][0].reshape(256, 16, 128)
    w_k_ext = np.concatenate([w_ukv[:, :, :64], np.zeros((256, 16, 64), f32)], axis=2)
    w_v = w_ukv[:, :, 64:]
    gq = inp["mla_g_qn"][0]
    gk = inp["mla_g_kn"][0]
    gq_ext = np.concatenate([gq, gq[64 + swap]])
    gk_ext = np.concatenate([gk, gk[64 + swap]])
    inv_freq = (10000.0 ** (-np.arange(0, 32, 2, dtype=f32) / 32)).astype(f32)
    fox_in = inp["fox_w_in"][0]
    fq = fox_in[:, 0:1024].reshape(D, 16, 64)
    fk = fox_in[:, 1024:2048].reshape(D, 16, 64)
    fv = fox_in[:, 2048:3072].reshape(D, 16, 64)
    ff = fox_in[:, 3072:3088]
    in_maps = []
    for c in range(8):
        b, r = c // 4, c % 4
        hs = slice(4 * r, 4 * r + 4)
        m = dict(common)
        m["xT"] = np.ascontiguousarray(x[b, r * NT:(r + 1) * NT, :].T)
        m["pT"] = np.ascontiguousarray(p[:, b, r * NT:(r + 1) * NT, :].transpose(0, 2, 1))
        m["pos"] = np.ascontiguousarray(np.broadcast_to(positions[b].astype(np.int32)[None, :], (32, S)))
        g = np.zeros((128, NG), f32)
        for l in range(2):
            g[:, GC_FFN1 + 8 * l:GC_FFN1 + 8 * l + 8] = _chunk_cols(inp["g_ffn1"][l])
            g[:, GC_MIX + 8 * l:GC_MIX + 8 * l + 8] = _chunk_cols(inp["g_mix"][l])
            g[:, GC_FFN2 + 8 * l:GC_FFN2 + 8 * l + 8] = _chunk_cols(inp["g_ffn2"][l])
            g[:, GC_PLE + 8 * l:GC_PLE + 8 * l + 8] = _chunk_cols(inp["g_ple"][l])
        g[:, GC_QLAT:GC_QLAT + 4] = _chunk_cols(inp["mla_g_q_lat"][0])
        g[:, GC_KVLAT:GC_KVLAT + 2] = _chunk_cols(inp["mla_g_kv_lat"][0])
        g[:, GC_GQ] = gq_ext
        g[:, GC_GK] = gk_ext
        g[:, GC_FQ] = np.tile(inp["fox_g_qn"][0], 2)
        g[:, GC_FK] = np.tile(inp["fox_g_kn"][0], 2)
        g[64:96, GC_INVF] = np.tile(inv_freq, 2)
        g[64:80, GC_SGN] = -1.0
        g[80:96, GC_SGN] = 1.0
        g[0:4, GC_BF] = inp["fox_b_f"][0][hs]
        g[:, GC_EPS] = EPS
        m["gvec"] = g
        m["wuq"] = np.ascontiguousarray(w_uq_ext[:, hs, :].reshape(4, 128, 512).transpose(1, 0, 2))
        m["wukvk"] = np.ascontiguousarray(w_k_ext[:, hs, :].reshape(2, 128, 512).transpose(1, 0, 2))
        m["wukvv"] = np.ascontiguousarray(w_v[:, hs, :].reshape(2, 128, 256).transpose(1, 0, 2))
        m["mla_wo"] = np.ascontiguousarray(inp["mla_w_o"][0][256 * r:256 * (r + 1), :].reshape(2, 128, D).transpose(1, 0, 2))
        m["fwq"] = np.ascontiguousarray(fq[:, hs, :].reshape(8, 128, 256).transpose(1, 0, 2))
        m["fwk"] = np.ascontiguousarray(fk[:, hs, :].reshape(8, 128, 256).transpose(1, 0, 2))
        m["fwv"] = np.ascontiguousarray(fv[:, hs, :].reshape(8, 128, 256).transpose(1, 0, 2))
        m["fwf"] = np.ascontiguousarray(ff[:, hs].reshape(8, 128, 4).transpose(1, 0, 2))
        m["fox_wo"] = np.ascontiguousarray(inp["fox_w_o"][0][256 * r:256 * (r + 1), :].reshape(2, 128, D).transpose(1, 0, 2))
        in_maps.append(m)
    return in_maps


_NC_CACHE = {}


def kernel(**inputs):
    inp = {k: np.asarray(v) for k, v in inputs.items()}
    in_maps = prep_inputs(inp)
    if "nc" not in _NC_CACHE:
        _NC_CACHE["nc"] = build_program()
    nc = _NC_CACHE["nc"]
    res = run_bass_kernel_spmd(nc, in_maps, core_ids=list(range(8)))
    out = np.empty((2, S, D), np.float32)
    for c in range(8):
        b, r = c // 4, c % 4
        out[b, r * NT:(r + 1) * NT, :] = np.asarray(res.results[c]["yT"]).T
    return out
```

```python
import numpy as np
from contextlib import ExitStack
import concourse.bass as bass
import concourse.mybir as mybir
from concourse.bass_utils import run_bass_kernel_spmd

F32 = mybir.dt.float32
BF16 = mybir.dt.bfloat16
I32 = mybir.dt.int32
AF = mybir.ActivationFunctionType
ALU = mybir.AluOpType

D = 1024
S = 16384
NT = 4096
TT = 512
DFF = 2816
NC_FF = DFF // 128
EPS = 1e-6
PI = float(np.pi)
C1 = 6.28125
C2 = float(2 * np.pi - 6.28125)


class Buf:
    __slots__ = ("name", "last_w", "rd", "rd_dma", "wr_all")

    def __init__(self, name=""):
        self.name = name
        self.last_w = None
        self.rd = {}
        self.rd_dma = []
        self.wr_all = []

    def reset(self):
        self.last_w = None
        self.rd = {}
        self.rd_dma = []
        self.wr_all = []


class Op:
    __slots__ = ("eng", "fn", "deps", "dma", "sig", "needed", "idx", "cc", "prev")

    def __init__(self, eng, fn, dma, cc):
        self.eng = eng
        self.fn = fn
        self.deps = []
        self.dma = dma
        self.cc = cc
        self.sig = None
        self.needed = False
        self.prev = None


class Prog:
    ENGS = ("pe", "act", "dve", "pool", "sp")
    NDMASEM = 6
    SEM_ROLL = 24000

    def __init__(self, nc, stack):
        self.nc = nc
        self.stack = stack
        self.ops = []
        self.start = 0
        self.bufs = []
        self.sems = {}
        self.cnt = {}
        self.nsem = 0
        self.dma_pool = {}
        self.dma_n = {}
        self.cc_sem = None
        self.cc_n = 0
        self.out_sigs = []

    def buf(self, name=""):
        b = Buf(name)
        self.bufs.append(b)
        return b

    def newsem(self, tag):
        self.nsem += 1
        return self.stack.enter_context(self.nc.semaphore("s_%s_%d" % (tag, self.nsem)))

    def op(self, eng, fn, reads=(), writes=(), dma=False, cc=False, iwrites=()):
        o = Op(eng, fn, dma, cc)
        o.idx = len(self.ops)
        special = dma or cc
        deps = set()
        for b in reads:
            if b.last_w is not None:
                deps.add(b.last_w)
            deps.update(b.wr_all)
        for b in writes:
            if b.last_w is not None:
                deps.add(b.last_w)
            deps.update(b.wr_all)
            for j in b.rd.values():
                deps.add(j)
            for j in b.rd_dma:
                deps.add(j)
        for b in iwrites:
            if b.last_w is not None:
                deps.add(b.last_w)
        best = {}
        for j in deps:
            p = self.ops[j]
            if p.dma or p.cc:
                o.deps.append(j)
                continue
            if p.eng == eng and not special:
                if eng == "pe":
                    continue
                if not any(b.last_w == j for b in reads):
                    continue
            if best.get(p.eng, -1) < j:
                best[p.eng] = j
        for j in best.values():
            o.deps.append(j)
            self.ops[j].needed = True
        for b in reads:
            if special:
                b.rd_dma.append(o.idx)
            else:
                b.rd[eng] = o.idx
        for b in writes:
            b.last_w = o.idx
            b.rd = {}
            b.rd_dma = []
            b.wr_all = []
        for b in iwrites:
            b.wr_all.append(o.idx)
        self.ops.append(o)
        return o

    def flush(self):
        nc = self.nc
        pend = self.ops[self.start:]
        self.start = len(self.ops)
        if not pend:
            return
        streams = {e: [] for e in self.ENGS}
        for e in self.ENGS:
            if e not in self.sems or self.cnt[e] >= self.SEM_ROLL:
                self.sems[e] = self.newsem(e)
                self.cnt[e] = 0
        if self.cc_sem is None:
            self.cc_sem = self.newsem("cc")
        last_dma = {}
        for o in pend:
            e = o.eng
            streams[e].append(o)
            if o.dma:
                if e not in self.dma_pool:
                    self.dma_pool[e] = [self.newsem("dma" + e) for _ in range(self.NDMASEM)]
                    self.dma_n[e] = 0
                i = self.dma_n[e]
                self.dma_n[e] += 1
                si = i % self.NDMASEM
                s = self.dma_pool[e][si]
                prev = 16 * (i // self.NDMASEM)
                o.sig = (s, prev + 16)
                o.prev = (s, prev)
                last_dma[(e, si)] = o.sig
            elif o.cc:
                self.cc_n += 1
                o.sig = (self.cc_sem, self.cc_n)
                last_dma[("cc", 0)] = o.sig
            elif o.needed:
                self.cnt[e] += 1
                o.sig = (self.sems[e], self.cnt[e])
        ops = self.ops
        finals = list(last_dma.values())

        def run_stream(e):
            def body(eng):
                known = {}
                for o in streams[e]:
                    waits = [ops[j].sig for j in o.deps]
                    if o.dma and o.prev[1] > 0:
                        waits.append(o.prev)
                    for (s, v) in waits:
                        k = id(s)
                        if known.get(k, 0) >= v:
                            continue
                        known[k] = v
                        eng.wait_ge(s, v)
                    ins = o.fn(eng)
                    if o.sig is not None:
                        ins.then_inc(o.sig[0], 16 if o.dma else 1)
                if e == "sp":
                    for (s, v) in finals:
                        eng.wait_ge(s, v)
            return body

        with nc.Block() as block:
            for e, deco in (("pe", block.tensor), ("act", block.scalar), ("dve", block.vector),
                            ("pool", block.gpsimd), ("sp", block.sync)):
                if streams[e] or e == "sp":
                    deco(run_stream(e))
        for b in self.bufs:
            b.reset()


class Tile:
    def __init__(self, t, b):
        self.t = t
        self.b = b


class Rot:
    def __init__(self, tiles):
        self.tiles = tiles
        self.i = 0

    def next(self):
        t = self.tiles[self.i % len(self.tiles)]
        self.i += 1
        return t


def build_program(stop_after=None, dbg=()):
    nc = bass.Bass("TRN2", target_bir_lowering=False)
    gstack = ExitStack()
    P = Prog(nc, gstack)

    def din(name, shape, dt=F32):
        return nc.dram_tensor(name, list(shape), dt, kind="ExternalInput")

    def dint(name, shape, dt):
        return nc.dram_tensor(name, list(shape), dt)

    xT = din("xT", [D, NT])
    pT = din("pT", [2, 256, NT])
    pos = din("pos", [32, S], I32)
    gvec = din("gvec", [128, NG])
    w1in = din("w1in", [2, NC_FF, 128, 2048])
    w2in = din("w2in", [2, NC_FF, 128, 2048])
    w1out = din("w1out", [2, 128, NC_FF, D])
    w2out = din("w2out", [2, 128, NC_FF, D])
    wgate = din("wgate", [2, 128, 8, D])
    wproj = din("wproj", [2, 128, 2, D])
    mla_win = din("mla_win", [128, 8, 800])
    wuq = din("wuq", [128, 4, 512])
    wukvk = din("wukvk", [128, 2, 512])
    wukvv = din("wukvv", [128, 2, 256])
    selm = din("selm", [32, 128])
    mla_wo = din("mla_wo", [128, 2, D])
    fwq = din("fwq", [128, 8, 256])
    fwk = din("fwk", [128, 8, 256])
    fwv = din("fwv", [128, 8, 256])
    fwf = din("fwf", [128, 8, 4])
    fox_wo = din("fox_wo", [128, 2, D])
    cmats = din("cmats", [6, 128, 128])
    yT = nc.dram_tensor("yT", [D, NT], F32, kind="ExternalOutput")

    hS = dint("hS", [D, NT], F32)
    agS0 = dint("agS0", [16, 800, 256], BF16)
    agR0 = dint("agR0", [16, 4 * 800, 256], BF16)
    agS1 = dint("agS1", [16, D, 256], BF16)
    agR1 = dint("agR1", [16, 4 * D, 256], BF16)
    Qs = dint("Qs", [4, 96, S], BF16)
    Ks = dint("Ks", [4, 96, S], BF16)
    Vs = dint("Vs", [4, 128, 128 * 64], BF16)
    Lf = dint("Lf", [4, S], F32)
    Cs = dint("Cs", [4, S], F32)
    Os = dint("Os", [128, 1], F32)
    pS = dint("pS", [4, 4 * D, 1024], F32)
    pR = dint("pR", [4, D, 1024], F32)
    B_hS, B_agS0, B_agR0, B_agS1, B_agR1 = (P.buf(n) for n in ("hS", "agS0", "agR0", "agS1", "agR1"))
    B_Qs, B_Ks, B_Vs, B_Lf, B_Cs, B_Os, B_pS, B_pR = (P.buf(n) for n in ("Qs", "Ks", "Vs", "Lf", "Cs", "Os", "pS", "pR"))
    B_y = P.buf("y")
    B_pRq = [P.buf("pR%d" % i) for i in range(4)]
    B_pSq = [P.buf("pS%d" % i) for i in range(4)]
    RG = [[0, 1, 2, 3], [4, 5, 6, 7]]

    def dma(eng, out, in_, reads=(), writes=(), slow=False, iw=()):
        if slow:
            return P.op(eng, lambda e: e.dma_start(out=out, in_=in_, allow_slow_non_contiguous=True), reads, writes, dma=True, iwrites=iw)
        return P.op(eng, lambda e: e.dma_start(out=out, in_=in_), reads, writes, dma=True, iwrites=iw)

    def mm(out, lhsT, rhs, start, stop, reads, writes):
        return P.op("pe", lambda e: e.matmul(out, lhsT=lhsT, rhs=rhs, start=start, stop=stop, skip_group_check=True), reads, writes)

    def act(out, in_, func, reads, writes, bias=None, scale=1.0):
        if bias is None:
            return P.op("act", lambda e: e.activation(out=out, in_=in_, func=func, scale=scale), reads, writes)
        return P.op("act", lambda e: e.activation(out=out, in_=in_, func=func, bias=bias, scale=scale), reads, writes)

    def tt(eng, out, in0, in1, op, reads, writes):
        return P.op(eng, lambda e: e.tensor_tensor(out=out, in0=in0, in1=in1, op=op), reads, writes)

    def ts(eng, out, in0, s1, s2, op0, op1, reads, writes):
        if s2 is None:
            return P.op(eng, lambda e: e.tensor_scalar(out=out, in0=in0, scalar1=s1, scalar2=None, op0=op0), reads, writes)
        return P.op(eng, lambda e: e.tensor_scalar(out=out, in0=in0, scalar1=s1, scalar2=s2, op0=op0, op1=op1), reads, writes)

    def stt(eng, out, in0, scalar, in1, op0, op1, reads, writes):
        return P.op(eng, lambda e: e.scalar_tensor_tensor(out=out, in0=in0, scalar=scalar, in1=in1, op0=op0, op1=op1), reads, writes)

    def cp(eng, out, in_, reads, writes):
        return P.op(eng, lambda e: e.tensor_copy(out=out, in_=in_), reads, writes)

    def recip(out, in_, reads, writes):
        return P.op("dve", lambda e: e.reciprocal(out=out, in_=in_), reads, writes)

    def memset(eng, ap, val, writes):
        return P.op(eng, lambda e: e.memset(ap, val), (), writes)

    def allgather(src, dst, idx, bsrc, bdst):
        if OPT.get('nocc'):
            return
        P.op("pool", lambda e: e.collective_compute("AllGather", ALU.bypass, replica_groups=RG,
                                                    ins=[src[idx, :, :]], outs=[dst[idx, :, :]]),
             [bsrc], (), cc=True, iwrites=[bdst])

    uid = [0]

    def sbuf(stack, name, shape, dt):
        uid[0] += 1
        name = "%s_%d" % (name, uid[0])
        return Tile(stack.enter_context(nc.sbuf_tensor(name, list(shape), dt)), P.buf(name))

    def psum(stack, name):
        return Tile(stack.enter_context(nc.psum_tensor(name, [128, 512], F32)), P.buf(name))

    G = sbuf(gstack, "gv", [128, NG], F32)
    CM = sbuf(gstack, "cm", [128, 5, 128], BF16)
    TRI = sbuf(gstack, "tri", [128, 128], F32)
    ps = [psum(gstack, "ps%d" % i) for i in range(8)]

    def consts_load():
        dma("sp", G.t[:, :], gvec[:, :], (), [G.b])
        for i in range(5):
            dma("pool", CM.t[:, i, :], cmats[i, :, :], (), [CM.b])
        dma("sp", TRI.t[:, :], cmats[5, :, :], (), [TRI.b])
        ts("dve", G.t[:, GC_GQ:GC_GQ + 1], G.t[:, GC_GQ:GC_GQ + 1], float(96 ** -0.5), None, ALU.mult, None, [G.b], [G.b])
        ts("dve", G.t[:, GC_FQ:GC_FQ + 1], G.t[:, GC_FQ:GC_FQ + 1], float(64 ** -0.5), None, ALU.mult, None, [G.b], [G.b])

    ONES_ALL = lambda: CM.t[:, 0, :]
    ONES96 = lambda: CM.t[:, 1, :]
    ONES_BD = lambda: CM.t[:, 2, :]
    IDENT = lambda: CM.t[:, 3, :]
    NEGTRI = lambda: CM.t[:, 4, :]
    EPSC = lambda p0, p1: G.t[p0:p1, GC_EPS:GC_EPS + 1]

    def rmsnorm(src, nch, nfeat, gc0, out, sq, ssps, rs):
        tt("dve", sq.t[:, 0:nch, :], src.t[:, 0:nch, :], src.t[:, 0:nch, :], ALU.mult, [src.b], [sq.b])
        for c in range(nch):
            mm(ssps.t[:, :], ONES_ALL(), sq.t[:, c, :], c == 0, c == nch - 1, [CM.b, sq.b], [ssps.b])
        act(rs.t[:, :], ssps.t[:, :], AF.Sqrt, [ssps.b, G.b], [rs.b], bias=EPSC(0, 128), scale=1.0 / nfeat)
        recip(rs.t[:, :], rs.t[:, :], [rs.b], [rs.b])
        for c in range(nch):
            stt("dve", out.t[:, c, :], src.t[:, c, :], G.t[:, gc0 + c:gc0 + c + 1], rs.t[:, :], ALU.mult, ALU.mult,
                [src.b, G.b, rs.b], [out.b])

    def rstd_act(rs_ap, ss_ap, nfeat, rb, sb_, p0=0, p1=128):
        act(rs_ap, ss_ap, AF.Ln, [sb_, G.b], [rb], bias=EPSC(p0, p1), scale=1.0 / nfeat)
        act(rs_ap, rs_ap, AF.Exp, [rb], [rb], scale=-0.5)

    def stage_tok(stage):
        NH = 2
        ST = NH * TT
        with ExitStack() as st:
            hT_slots = [sbuf(st, "hT%d" % i, [128, 8, ST], F32) for i in range(2)]
            hb_slots = [[P.buf("hA%d" % i), P.buf("hB%d" % i)] for i in range(2)]
            hT = hT_slots[0]
            hb = hb_slots[0]
            xn = sbuf(st, "xn", [128, 8, ST], BF16)
            sq = sbuf(st, "sq", [128, 8, TT], BF16)
            aT = sbuf(st, "aT", [128, NC_FF, ST], BF16)
            rs = sbuf(st, "rs", [128, TT], F32)
            sg = Rot([sbuf(st, "sg%d" % i, [128, TT], F32) for i in range(2)])
            win = Rot([sbuf(st, "win%d" % i, [128, 2048], BF16) for i in range(3)])
            wom = Rot([sbuf(st, "wom%d" % i, [128, NC_FF, 128], BF16) for i in range(4)])
            ss_ps = ps[0]
            g_ps = Rot([ps[1], ps[2]])
            u_ps = Rot([ps[3], ps[4]])
            o_ps = Rot([ps[5], ps[6]])
            if stage in (1, 2):
                wg = sbuf(st, "wg", [128, 8, D], BF16)
                wp = sbuf(st, "wp", [128, 2, D], BF16)
                ptl = sbuf(st, "ptl", [128, 2, ST], BF16)
                li = stage - 1

                def rs_op(q):
                    P.op("pool", lambda e: e.collective_compute("ReduceScatter", ALU.add, replica_groups=RG,
                                                                ins=[pS[q, :, :]], outs=[pR[q, :, :]]),
                         (), [B_pRq[q]], cc=True)
                dma("pool", wg.t[:, :, :], wgate[li, :, :, :], (), [wg.b])
                dma("pool", wp.t[:, :, :], wproj[li, :, :, :], (), [wp.b])
            if stage == 0:
                mw = sbuf(st, "mw", [128, 8, 800], BF16)
                dma("pool", mw.t[:, :, :], mla_win[:, :, :], (), [mw.b])
                zf = sbuf(st, "zf", [128, 6, TT], F32)
                zn = sbuf(st, "zn", [128, 6, TT], BF16)
                kpe = sbuf(st, "kpe", [32, TT], BF16)

            xb = [P.buf("xA"), P.buf("xB")]
            ab = [P.buf("aA"), P.buf("aB")]

            def norm_sq(hf):
                cs = slice(hf * TT, (hf + 1) * TT)
                tt("dve", sq.t[:, :, :], hT.t[:, :, cs], hT.t[:, :, cs], ALU.mult, [hb[hf]], [sq.b])

            def norm_fin(hf, gc0):
                cs = slice(hf * TT, (hf + 1) * TT)
                for c in range(8):
                    mm(ss_ps.t[:, :], ONES_ALL(), sq.t[:, c, :], c == 0, c == 7, [CM.b, sq.b], [ss_ps.b])
                rstd_act(rs.t[:, :], ss_ps.t[:, :], D, rs.b, ss_ps.b)
                for c in range(8):
                    stt("dve", xn.t[:, c, cs], hT.t[:, c, cs], G.t[:, gc0 + c:gc0 + c + 1], rs.t[:, :], ALU.mult, ALU.mult,
                        [hb[hf], G.b, rs.b], [xb[hf]])

            def norm_half(hf, gc0):
                norm_sq(hf)
                norm_fin(hf, gc0)

            def norm_h(gc0):
                for hf in range(NH):
                    norm_half(hf, gc0)

            def ffn(li, win_d, wout_d, next_gc):
                wo_pref = []

                def wo_load(m):
                    wq = wom.next()
                    dma("pool", wq.t[:, :, :], wout_d[li, :, :, m * 128:(m + 1) * 128], (), [wq.b])
                    wo_pref.append(wq)

                for c in range(NC_FF):
                    w = win.next()
                    dma("pool", w.t[:, :], win_d[li, c, :, :], (), [w.b])
                    if c in (8, 12, 16, 20):
                        wo_load(len(wo_pref))
                    for hf in range(NH):
                        cs = slice(hf * TT, (hf + 1) * TT)
                        gp = g_ps.next()
                        up = u_ps.next()
                        for half, pp in ((0, gp), (1, up)):
                            for k in range(8):
                                mm(pp.t[:, :], w.t[:, (half * 8 + k) * 128:(half * 8 + k + 1) * 128], xn.t[:, k, cs],
                                   k == 0, k == 7, [w.b, xb[hf]], [pp.b])
                        s = sg.next()
                        act(s.t[:, :], gp.t[:, :], AF.Silu, [gp.b], [s.b])
                        tt("dve", aT.t[:, c, cs], s.t[:, :], up.t[:, :], ALU.mult, [s.b, up.b], [ab[hf]])
                for g4 in range(2):
                    for hf in range(NH):
                        cs = slice(hf * TT, (hf + 1) * TT)
                        for m in range(4 * g4, 4 * g4 + 4):
                            wq = wo_pref[m]
                            op_ = o_ps.next()
                            for c in range(NC_FF):
                                mm(op_.t[:, :], wq.t[:, c, :], aT.t[:, c, cs], c == 0, c == NC_FF - 1, [wq.b, ab[hf]], [op_.b])
                            stt("dve", hT.t[:, m, cs], op_.t[:, :], 0.5, hT.t[:, m, cs], ALU.mult, ALU.add, [op_.b, hb[hf]], [hb[hf]])
                            if hf == NH - 1 and len(wo_pref) < 8:
                                wo_load(len(wo_pref))
                            if g4 == 1 and next_gc is not None and hf == 1 and m == 5:
                                norm_fin(0, next_gc)
                        if g4 == 1 and next_gc is not None:
                            if hf == 0:
                                norm_sq(0)
                            else:
                                norm_half(1, next_gc)

            def ple(li, t, next_gc):
                dma("pool", ptl.t[:, :, :], pT[li, :, t * ST:(t + 1) * ST].rearrange("(c p) n -> p c n", p=128), (), [ptl.b])
                for hf in range(NH):
                    cs = slice(hf * TT, (hf + 1) * TT)
                    for m in range(8):
                        gp = g_ps.next()
                        up = u_ps.next()
                        for k in range(8):
                            mm(gp.t[:, :], wg.t[:, k, m * 128:(m + 1) * 128], xn.t[:, k, cs], k == 0, k == 7, [wg.b, xb[hf]], [gp.b])
                        for k in range(2):
                            mm(up.t[:, :], wp.t[:, k, m * 128:(m + 1) * 128], ptl.t[:, k, cs], k == 0, k == 1, [wp.b, ptl.b], [up.b])
                        s = sg.next()
                        act(s.t[:, :], gp.t[:, :], AF.Sigmoid, [gp.b], [s.b])
                        tt("dve", s.t[:, :], s.t[:, :], up.t[:, :], ALU.mult, [s.b, up.b], [s.b])
                        tt("dve", hT.t[:, m, cs], hT.t[:, m, cs], s.t[:, :], ALU.add, [hb[hf], s.b], [hb[hf]])
                        if next_gc is not None and hf == 1 and m == 1:
                            norm_fin(0, next_gc)
                    if next_gc is not None:
                        if hf == 0:
                            norm_sq(0)
                        else:
                            norm_half(1, next_gc)

            def mla_pre(t):
                for hf in range(NH):
                    cs = slice(hf * TT, (hf + 1) * TT)
                    for zc in range(7):
                        gp = g_ps.next()
                        mcols = 128 if zc < 6 else 32
                        for k in range(8):
                            mm(gp.t[0:mcols, :], mw.t[:, k, zc * 128:zc * 128 + mcols], xn.t[:, k, cs], k == 0, k == 7,
                               [mw.b, xb[hf]], [gp.b])
                        if zc < 6:
                            act(zf.t[:, zc, :], gp.t[:, :], AF.Copy, [gp.b], [zf.b])
                        else:
                            act(kpe.t[:, :], gp.t[0:32, :], AF.Copy, [gp.b], [kpe.b])
                    for (c0, nch, nf, gc) in ((0, 4, 512, GC_QLAT), (4, 2, 256, GC_KVLAT)):
                        tt("dve", sq.t[:, c0:c0 + nch, :], zf.t[:, c0:c0 + nch, :], zf.t[:, c0:c0 + nch, :], ALU.mult, [zf.b], [sq.b])
                        for c in range(nch):
                            mm(ss_ps.t[:, :], ONES_ALL(), sq.t[:, c0 + c, :], c == 0, c == nch - 1, [CM.b, sq.b], [ss_ps.b])
                        rstd_act(rs.t[:, :], ss_ps.t[:, :], nf, rs.b, ss_ps.b)
                        for c in range(nch):
                            stt("dve", zn.t[:, c0 + c, :], zf.t[:, c0 + c, :], G.t[:, gc + c:gc + c + 1], rs.t[:, :], ALU.mult, ALU.mult,
                                [zf.b, G.b, rs.b], [zn.b])
                    for h2 in range(2):
                        c2 = slice(h2 * 256, (h2 + 1) * 256)
                        idx = 2 * (NH * t + hf) + h2
                        dma("sp", agS0[idx, 0:768, :].rearrange("(c p) n -> p c n", p=128), zn.t[:, :, c2], [zn.b], iw=[B_agS0])
                        dma("sp", agS0[idx, 768:800, :], kpe.t[:, c2], [kpe.b], iw=[B_agS0])
                        allgather(agS0, agR0, idx, B_agS0, B_agR0)

            ntl = OPT.get('ntiles', NT // ST)

            def prefetch(t):
                tok = slice(t * ST, (t + 1) * ST)
                hT_ = hT_slots[t % 2]
                hb_ = hb_slots[t % 2]
                if stage in (0, 3):
                    dma("sp", hT_.t[:, :, :], xT[:, tok].rearrange("(c p) n -> p c n", p=128), (), hb_)
                else:
                    dma("sp", hT_.t[:, :, :], hS[:, tok].rearrange("(c p) n -> p c n", p=128), [B_hS], hb_)
                    for hf in range(NH):
                        cs = slice(hf * TT, (hf + 1) * TT)
                        P.op("pool", (lambda o_, i_: (lambda e: e.dma_start(out=o_, in_=i_, accum_op=ALU.add)))(
                            hT_.t[:, :, cs], pR[t, :, cs].rearrange("(c p) n -> p c n", p=128)),
                            [B_pRq[t], hb_[hf]], [hb_[hf]], dma=True)

            prefetch(0)
            for t in range(ntl):
                tok = slice(t * ST, (t + 1) * ST)
                hT = hT_slots[t % 2]
                hb = hb_slots[t % 2]
                if stage in (1, 2):
                    li = stage - 1
                    norm_h(GC_FFN2 + 8 * li)
                    if stage == 2 and t + 1 < ntl:
                        prefetch(t + 1)
                    ffn(li, w2in, w2out, GC_PLE + 8 * li)
                    ple(li, t, GC_FFN1 + 8 if stage == 1 else None)
                if stage == 0:
                    norm_h(GC_FFN1 + 0)
                    if t + 1 < ntl:
                        prefetch(t + 1)
                    ffn(0, w1in, w1out, GC_MIX + 0)
                    dma("sp", hS[:, tok].rearrange("(c p) n -> p c n", p=128), hT.t[:, :, :], hb, iw=[B_hS])
                    mla_pre(t)
                elif stage in (1, 3):
                    if stage == 3:
                        norm_h(GC_FFN1 + 8)
                    if t + 1 < ntl:
                        prefetch(t + 1)
                    ffn(1, w1in, w1out, GC_MIX + 8)
                    dma("sp", hS[:, tok].rearrange("(c p) n -> p c n", p=128), hT.t[:, :, :], hb, iw=[B_hS])
                    for hf in range(NH):
                        for h2 in range(2):
                            c2 = slice(hf * TT + h2 * 256, hf * TT + (h2 + 1) * 256)
                            idx = 2 * (NH * t + hf) + h2
                            dma("sp", agS1[idx, :, :].rearrange("(c p) n -> p c n", p=128), xn.t[:, :, c2], [xb[hf]], iw=[B_agS1])
                            allgather(agS1, agR1, idx, B_agS1, B_agR1)
                else:
                    dma("sp", yT[:, tok].rearrange("(c p) n -> p c n", p=128), hT.t[:, :, :], hb, iw=[B_y])
            P.flush()

    def stage_proj(li):
        with ExitStack() as st:
            sqh = Rot([sbuf(st, "sqh%d" % i, [128, TT], BF16) for i in range(3)])
            rsh = Rot([sbuf(st, "rsh%d" % i, [128, TT], F32) for i in range(3)])
            vt = Rot([sbuf(st, "vt%d" % i, [128, 4, 256], BF16) for i in range(2)])
            qk_ps = Rot([ps[0], ps[1], ps[2], ps[7]] if li == 0 else [ps[0], ps[1], ps[2]])
            ss_ps = Rot([ps[3], ps[4], ps[5]] if li == 0 else [ps[3], ps[4]])
            v_ps = Rot([ps[6]] if li == 0 else [ps[5], ps[6]])
            f_ps = ps[7]
            Vs_v = Vs[:, :, :].rearrange("h p (b d) -> h p b d", d=64)
            if li == 0:
                w_q = sbuf(st, "w_q", [128, 4, 512], BF16)
                w_k = sbuf(st, "w_k", [128, 2, 512], BF16)
                w_v = sbuf(st, "w_v", [128, 2, 256], BF16)
                sel = sbuf(st, "sel", [32, 128], BF16)
                dma("pool", w_q.t[:, :, :], wuq[:, :, :], (), [w_q.b])
                dma("pool", w_k.t[:, :, :], wukvk[:, :, :], (), [w_k.b])
                dma("pool", w_v.t[:, :, :], wukvv[:, :, :], (), [w_v.b])
                dma("pool", sel.t[:, :], selm[:, :], (), [sel.b])
                zq = Rot([sbuf(st, "zq%d" % i, [128, 4, TT], BF16) for i in range(2)])
                zkv = Rot([sbuf(st, "zkv%d" % i, [128, 2, TT], BF16) for i in range(2)])
                kp = Rot([sbuf(st, "kp%d" % i, [32, TT], BF16) for i in range(2)])
                posi = sbuf(st, "posi", [128, TT], I32)
                ang = sbuf(st, "ang", [128, TT], F32)
                nfl = sbuf(st, "nfl", [128, TT], F32)
                nin = sbuf(st, "nin", [128, TT], I32)
                msk = sbuf(st, "msk", [128, TT], F32)
                cos_tb = [sbuf(st, "cos_t%d" % i, [128, TT], F32) for i in range(2)]
                sin_tb = [sbuf(st, "sin_t%d" % i, [128, TT], F32) for i in range(2)]
                qn = Rot([sbuf(st, "qn%d" % i, [128, TT], F32) for i in range(3)])
                sw = Rot([sbuf(st, "sw%d" % i, [128, TT], F32) for i in range(3)])
                t1 = Rot([sbuf(st, "t1%d" % i, [128, TT], F32) for i in range(3)])
                qo = Rot([sbuf(st, "qo%d" % i, [128, TT], BF16) for i in range(5)])
                R_ = slice(64, 96)

                def wrap(x):
                    for (cmpop, thr, add) in ((ALU.is_gt, PI, -2 * PI), (ALU.is_lt, -PI, 2 * PI)):
                        ts("dve", msk.t[R_, :], x.t[R_, :], thr, add, cmpop, ALU.mult, [x.b], [msk.b])
                        tt("dve", x.t[R_, :], x.t[R_, :], msk.t[R_, :], ALU.add, [x.b, msk.b], [x.b])

                def rope_tables(T):
                    cos_t = cos_tb[T % 2]
                    sin_t = sin_tb[T % 2]
                    dma("sp", posi.t[R_, :], pos[:, T * TT:(T + 1) * TT], (), [posi.b])
                    cp("dve", ang.t[R_, :], posi.t[R_, :], [posi.b], [ang.b])
                    ts("dve", ang.t[R_, :], ang.t[R_, :], G.t[R_, GC_INVF:GC_INVF + 1], None, ALU.mult, None, [ang.b, G.b], [ang.b])
                    ts("dve", nfl.t[R_, :], ang.t[R_, :], 1.0 / (2 * PI), None, ALU.mult, None, [ang.b], [nfl.b])
                    cp("dve", nin.t[R_, :], nfl.t[R_, :], [nfl.b], [nin.b])
                    cp("dve", nfl.t[R_, :], nin.t[R_, :], [nin.b], [nfl.b])
                    stt("dve", ang.t[R_, :], nfl.t[R_, :], -C1, ang.t[R_, :], ALU.mult, ALU.add, [nfl.b, ang.b], [ang.b])
                    stt("dve", ang.t[R_, :], nfl.t[R_, :], -C2, ang.t[R_, :], ALU.mult, ALU.add, [nfl.b, ang.b], [ang.b])
                    wrap(ang)
                    act(sin_t.t[R_, :], ang.t[R_, :], AF.Sin, [ang.b], [sin_t.b])
                    ts("dve", sin_t.t[R_, :], sin_t.t[R_, :], G.t[R_, GC_SGN:GC_SGN + 1], None, ALU.mult, None, [sin_t.b, G.b], [sin_t.b])
                    ts("dve", ang.t[R_, :], ang.t[R_, :], PI / 2, None, ALU.add, None, [ang.b], [ang.b])
                    wrap(ang)
                    act(cos_t.t[R_, :], ang.t[R_, :], AF.Sin, [ang.b], [cos_t.b])

                def hA(pq):
                    s_ = sqh.next()
                    act(s_.t[:, :], pq.t[:, :], AF.Square, [pq.b], [s_.b])
                    sp_ = ss_ps.next()
                    mm(sp_.t[:, :], ONES96(), s_.t[:, :], True, True, [CM.b, s_.b], [sp_.b])
                    return sp_

                def hB(pq, sp_, gc):
                    r_ = rsh.next()
                    rstd_act(r_.t[:, :], sp_.t[:, :], 96, r_.b, sp_.b)
                    o_ = qo.next()
                    stt("dve", o_.t[:, :], pq.t[:, :], G.t[:, gc:gc + 1], r_.t[:, :], ALU.mult, ALU.mult, [pq.b, G.b, r_.b], [o_.b])
                    return o_

                def hC(o_, dst, bdst, T):
                    cos_t = cos_tb[T % 2]
                    sin_t = sin_tb[T % 2]
                    w_ = sw.next()
                    act(w_.t[64:96, :], o_.t[96:128, :], AF.Copy, [o_.b], [w_.b])
                    a_ = t1.next()
                    tt("dve", a_.t[R_, :], o_.t[R_, :], cos_t.t[R_, :], ALU.mult, [o_.b, cos_t.b], [a_.b])
                    tt("dve", w_.t[R_, :], w_.t[R_, :], sin_t.t[R_, :], ALU.mult, [w_.b, sin_t.b], [w_.b])
                    tt("dve", o_.t[R_, :], a_.t[R_, :], w_.t[R_, :], ALU.add, [a_.b, w_.b, o_.b], [o_.b])
                    dma("sp", dst, o_.t[0:96, :], [o_.b], iw=[bdst])

                G0 = agR0[:, :, :].rearrange("i (r f) n -> i r f n", r=4)
                def mla_load(T):
                    r_, lt = T // 8, T % 8
                    a = zq.next()
                    b = zkv.next()
                    c = kp.next()
                    for hf in range(2):
                        cs = slice(hf * 256, (hf + 1) * 256)
                        dma("sp", a.t[:, :, cs], G0[2 * lt + hf, r_, 0:512, :].rearrange("(c p) n -> p c n", p=128), [B_agR0], [a.b])
                        dma("sp", b.t[:, :, cs], G0[2 * lt + hf, r_, 512:768, :].rearrange("(c p) n -> p c n", p=128), [B_agR0], [b.b])
                        dma("sp", c.t[:, cs], G0[2 * lt + hf, r_, 768:800, :], [B_agR0], [c.b])
                    return a, b, c

                NTL = S // TT
                tin = {0: mla_load(0)}
                rope_tables(0)

                def v_work(T):
                    a, b, c = tin[T]
                    v_ = vt.next()
                    for blk in range(4):
                        pv = v_ps.next()
                        for k in range(2):
                            mm(pv.t[:, 0:256], b.t[:, k, blk * 128:(blk + 1) * 128], w_v.t[:, k, :], k == 0, k == 1, [b.b, w_v.b], [pv.b])
                        act(v_.t[:, blk, :], pv.t[:, 0:256], AF.Copy, [pv.b], [v_.b])
                    for h in range(4):
                        dma("sp", Vs_v[h, :, T * 4:(T + 1) * 4, :], v_.t[:, :, h * 64:(h + 1) * 64], [v_.b], iw=[B_Vs])

                jobs = [(T, kind, h) for T in range(NTL) for h in range(4) for kind in ("q", "k")]
                stA, stB = [], []
                for step in range(len(jobs) + 2):
                    if step < len(jobs):
                        T, kind, h = jobs[step]
                        jn = step % 8
                        a, b, c = tin[T]
                        gcol = slice(T * TT, (T + 1) * TT)
                        if jn == 0 and T + 1 < NTL:
                            tin[T + 1] = mla_load(T + 1)
                        pq = qk_ps.next()
                        if kind == "q":
                            for k in range(4):
                                mm(pq.t[:, :], w_q.t[:, k, h * 128:(h + 1) * 128], a.t[:, k, :], k == 0, k == 3, [w_q.b, a.b], [pq.b])
                            info = (GC_GQ, Qs[h, 0:96, gcol], B_Qs, T)
                        else:
                            for k in range(2):
                                mm(pq.t[:, :], w_k.t[:, k, h * 128:(h + 1) * 128], b.t[:, k, :], k == 0, False, [w_k.b, b.b], [pq.b])
                            mm(pq.t[:, :], sel.t[:, :], c.t[:, :], False, True, [sel.b, c.b], [pq.b])
                            info = (GC_GK, Ks[h, 0:96, gcol], B_Ks, T)
                        stA.append((pq, hA(pq), info))
                        if jn == 3 and T + 1 < NTL:
                            rope_tables(T + 1)
                        if jn == 5:
                            v_work(T)
                    if step >= 1 and stA and step - 1 < len(jobs):
                        pq, sp_, info = stA.pop(0)
                        stB.append((hB(pq, sp_, info[0]), info))
                    if step >= 2 and stB:
                        o_, info = stB.pop(0)
                        hC(o_, info[1], info[2], info[3])
            else:
                w_q = sbuf(st, "f_q", [128, 8, 256], BF16)
                w_k = sbuf(st, "f_k", [128, 8, 256], BF16)
                w_v = sbuf(st, "f_v", [128, 8, 256], BF16)
                w_f = sbuf(st, "f_f", [128, 8, 4], BF16)
                dma("pool", w_q.t[:, :, :], fwq[:, :, :], (), [w_q.b])
                dma("pool", w_k.t[:, :, :], fwk[:, :, :], (), [w_k.b])
                dma("pool", w_v.t[:, :, :], fwv[:, :, :], (), [w_v.b])
                dma("pool", w_f.t[:, :, :], fwf[:, :, :], (), [w_f.b])
                hn = Rot([sbuf(st, "hn%d" % i, [128, 8, TT], BF16) for i in range(2)])
                qo = Rot([sbuf(st, "fqo%d" % i, [128, TT], BF16) for i in range(4)])
                lf = Rot([sbuf(st, "lf%d" % i, [4, TT], F32) for i in range(2)])
                G1 = agR1[:, :, :].rearrange("i (r f) n -> i r f n", r=4)

                def head_norm(pq, gc, dstT, B_dst, pair, gcol):
                    s_ = sqh.next()
                    act(s_.t[:, :], pq.t[:, :], AF.Square, [pq.b], [s_.b])
                    sp_ = ss_ps.next()
                    mm(sp_.t[:, :], ONES_BD(), s_.t[:, :], True, True, [CM.b, s_.b], [sp_.b])
                    r_ = rsh.next()
                    rstd_act(r_.t[:, :], sp_.t[:, :], 64, r_.b, sp_.b)
                    o_ = qo.next()
                    stt("dve", o_.t[:, :], pq.t[:, :], G.t[:, gc:gc + 1], r_.t[:, :], ALU.mult, ALU.mult, [pq.b, G.b, r_.b], [o_.b])
                    for j in range(2):
                        dma("sp", dstT[2 * pair + j, 0:64, gcol], o_.t[64 * j:64 * j + 64, :], [o_.b], iw=[B_dst])

                def fox_load(T):
                    r_, lt = T // 8, T % 8
                    a = hn.next()
                    for hf in range(2):
                        cs = slice(hf * 256, (hf + 1) * 256)
                        dma("sp", a.t[:, :, cs], G1[2 * lt + hf, r_, :, :].rearrange("(c p) n -> p c n", p=128), [B_agR1], [a.b])
                    return a

                nxt = fox_load(0)
                for T in range(S // TT):
                    gcol = slice(T * TT, (T + 1) * TT)
                    a = nxt
                    if T + 1 < S // TT:
                        nxt = fox_load(T + 1)
                    for pair in range(2):
                        pq = qk_ps.next()
                        for k in range(8):
                            mm(pq.t[:, :], w_q.t[:, k, pair * 128:(pair + 1) * 128], a.t[:, k, :], k == 0, k == 7, [w_q.b, a.b], [pq.b])
                        head_norm(pq, GC_FQ, Qs, B_Qs, pair, gcol)
                        pk = qk_ps.next()
                        for k in range(8):
                            mm(pk.t[:, :], w_k.t[:, k, pair * 128:(pair + 1) * 128], a.t[:, k, :], k == 0, k == 7, [w_k.b, a.b], [pk.b])
                        head_norm(pk, GC_FK, Ks, B_Ks, pair, gcol)
                    v_ = vt.next()
                    for blk in range(4):
                        pv = v_ps.next()
                        for k in range(8):
                            mm(pv.t[:, 0:256], a.t[:, k, blk * 128:(blk + 1) * 128], w_v.t[:, k, :], k == 0, k == 7, [a.b, w_v.b], [pv.b])
                        act(v_.t[:, blk, :], pv.t[:, 0:256], AF.Copy, [pv.b], [v_.b])
                    for h in range(4):
                        dma("sp", Vs_v[h, :, T * 4:(T + 1) * 4, :], v_.t[:, :, h * 64:(h + 1) * 64], [v_.b], iw=[B_Vs])
                    for k in range(8):
                        mm(f_ps.t[0:4, :], w_f.t[:, k, :], a.t[:, k, :], k == 0, k == 7, [w_f.b, a.b], [f_ps.b])
                    l_ = lf.next()
                    act(l_.t[:, :], f_ps.t[0:4, :], AF.Sigmoid, [f_ps.b, G.b], [l_.b], bias=G.t[0:4, GC_BF:GC_BF + 1])
                    act(l_.t[:, :], l_.t[:, :], AF.Ln, [l_.b], [l_.b])
                    dma("sp", Lf[:, gcol], l_.t[:, :], [l_.b], iw=[B_Lf])
                L = sbuf(st, "L", [128, TT], F32)
                one = sbuf(st, "one", [128, TT], F32)
                sc = sbuf(st, "sc", [128, TT], F32)
                hi = sbuf(st, "hi", [128, TT], BF16)
                lo = sbuf(st, "lo", [128, TT], BF16)
                onb = sbuf(st, "onb", [128, TT], BF16)
                off = sbuf(st, "off", [128, 1], F32)
                dma("sp", L.t[:, :], Lf[:, :].rearrange("h (t n) -> (h t) n", n=TT), [B_Lf], [L.b])
                memset("dve", one.t[:, :], 1.0, [one.b])
                memset("pool", onb.t[:, :], 1.0, [onb.b])
                P.op("dve", lambda e: e.tensor_tensor_scan(out=sc.t[:, :], data0=one.t[:, :], data1=L.t[:, :], initial=0.0,
                                                           op0=ALU.mult, op1=ALU.add), [one.b, L.b], [sc.b])
                pq = ps[0]
                mm(pq.t[:, 0:1], TRI.t[:, :], sc.t[:, TT - 1:TT], True, True, [TRI.b, sc.b], [pq.b])
                cp("dve", off.t[:, :], pq.t[:, 0:1], [pq.b], [off.b])
                dma("sp", Os[:, :], off.t[:, :], [off.b], iw=[B_Os])
                cp("dve", hi.t[:, :], sc.t[:, :], [sc.b], [hi.b])
                tt("dve", lo.t[:, :], sc.t[:, :], hi.t[:, :], ALU.subtract, [sc.b, hi.b], [lo.b])
                for h in range(4):
                    dma("sp", Qs[h, 64, :].rearrange("(t n) -> t n", n=TT), hi.t[32 * h:32 * h + 32, :], [hi.b], iw=[B_Qs])
                    dma("sp", Qs[h, 65, :].rearrange("(t n) -> t n", n=TT), lo.t[32 * h:32 * h + 32, :], [lo.b], iw=[B_Qs])
                for h in range(4):
                    dma("sp", Ks[h, 64:66, :].rearrange("a (t n) -> (a t) n", n=TT), onb.t[0:64, :], [onb.b], iw=[B_Ks])
                ts("dve", sc.t[:, :], sc.t[:, :], off.t[:, 0:1], None, ALU.add, None, [sc.b, off.b], [sc.b])
                dma("sp", Cs[:, :].rearrange("h (t n) -> (h t) n", n=TT), sc.t[:, :], [sc.b], iw=[B_Cs])
            P.flush()

    def stage_attn(li):
        dk = 96 if li == 0 else 66
        wo_d = mla_wo if li == 0 else fox_wo
        with ExitStack() as st:
            Kt = sbuf(st, "Kt", [96, S], BF16)
            Vt = sbuf(st, "Vt", [128, 128, 128], BF16)
            Ot = sbuf(st, "Ot", [128, 2, S], BF16)
            Qt = Rot([sbuf(st, "Qt%d" % i, [96, TT], BF16) for i in range(4)])
            LOOKAHEAD = 2
            Pt = Rot([sbuf(st, "Pt%d" % i, [128, TT], BF16) for i in range(4)])
            rsum = Rot([sbuf(st, "rsum%d" % i, [128, TT], F32) for i in range(2)])
            wo = sbuf(st, "wo", [128, 2, D], BF16)
            pt = Rot([sbuf(st, "pt%d" % i, [128, 4, TT], F32) for i in range(7)])
            s_ps = Rot([ps[0], ps[1], ps[2], ps[7]])
            o_ps = Rot([ps[3], ps[4]])
            p_ps = Rot([ps[5], ps[6]])
            dma("pool", wo.t[:, :, :], wo_d[:, :, :], (), [wo.b])
            if li == 1:
                ck = sbuf(st, "ck", [128, 128], F32)
                Rb = sbuf(st, "Rb", [128, 32], F32)
                bT = Rot([sbuf(st, "bT%d" % i, [128, 128], F32) for i in range(2)])
            Vs_v = Vs[:, :, :].rearrange("h p (b d) -> h p b d", d=64)
            pS_v = pS[:, :, :].rearrange("q (r f) n -> q r f n", r=4)
            b3_done = [0, 0, 0, 0]

            def b3(T):
                lt_ = T % 8
                q = lt_ // 2
                for mh in range(2):
                    x_ = pt.next()
                    for m4 in range(4):
                        m = mh * 4 + m4
                        pp = p_ps.next()
                        for pr in range(2):
                            mm(pp.t[:, :], wo.t[:, pr, m * 128:(m + 1) * 128], Ot.t[:, pr, T * TT:(T + 1) * TT], pr == 0, pr == 1,
                               [wo.b, Ot.b], [pp.b])
                        cp("dve", x_.t[:, m4, :], pp.t[:, :], [pp.b], [x_.b])
                    dma("pool", pS_v[q, T // 8, mh * 512:(mh + 1) * 512, (lt_ % 2) * TT:(lt_ % 2 + 1) * TT].rearrange("(c p) n -> p c n", p=128),
                        x_.t[:, :, :], [x_.b], iw=[B_pSq[q]])
                b3_done[q] += 1
                if b3_done[q] == 8 and not OPT.get('nors'):
                    P.op("pool", lambda e: e.collective_compute("ReduceScatter", ALU.add, replica_groups=RG,
                                                                ins=[pS[q, :, :]], outs=[pR[q, :, :]]),
                         [B_pSq[q]], [B_pRq[q]], cc=True)

            for h in range(4):
                odd = h % 2
                pair = h // 2
                vo = 64 if odd else 0
                so = 0 if odd else 64
                dma("sp", Kt.t[0:dk, :], Ks[h, 0:dk, :], [B_Ks], [Kt.b])
                memset("pool", Vt.t[:, :, so:so + 64], 1.0, [Vt.b])
                dma("sp", Vt.t[:, :, vo:vo + 64], Vs_v[h, :, :, :], [B_Vs], [Vt.b])
                if li == 1:
                    dma("sp", ck.t[:, :], Cs[h, :].rearrange("(b p) -> p b", p=128), [B_Cs], [ck.b], slow=True)
                    dma("sp", Rb.t[:, :], Os[h * 32:(h + 1) * 32, :].rearrange("t o -> o t").partition_broadcast(128),
                        [B_Os], [Rb.b], slow=True)
                    ts("dve", ck.t[:, :], ck.t[:, :], -1.0, None, ALU.mult, None, [ck.b], [ck.b])
                nq = OPT.get('nqg', S // TT)
                if h == 3 and nq == S // TT:
                    Torder = [8 * r + 2 * q + l2 for q in range(4) for r in range(4) for l2 in range(2)]
                else:
                    Torder = list(range(nq))
                tiles = [(T, i) for T in Torder for i in range(4 * T + 4)]
                b3_pend = []
                st_T = {}
                pend = []

                def issue_pv(T, i, p_, c0):
                    q_, b_, ob = st_T[T]
                    nblk = 4 * T + 4
                    mm(ob.t[:, c0:TT], Vt.t[:, i, :], p_.t[:, c0:TT], i == 0, i == nblk - 1, [Vt.b, p_.b], [ob.b])
                    if i == nblk - 1:
                        r_ = rsum.next()
                        act(r_.t[vo:vo + 64, :], ob.t[so:so + 64, :], AF.Copy, [ob.b], [r_.b])
                        recip(r_.t[vo:vo + 64, :], r_.t[vo:vo + 64, :], [r_.b], [r_.b])
                        tt("dve", Ot.t[vo:vo + 64, pair, T * TT:(T + 1) * TT], ob.t[vo:vo + 64, :], r_.t[vo:vo + 64, :], ALU.mult,
                           [ob.b, r_.b], [Ot.b])
                        del st_T[T]
                        if h == 3:
                            b3_pend.append(T)
                            if len(b3_pend) > 1:
                                b3(b3_pend.pop(0))

                for (T, i) in tiles:
                    if i == 0:
                        q_ = Qt.next()
                        dma("sp", q_.t[0:dk, :], Qs[h, 0:dk, T * TT:(T + 1) * TT], [B_Qs], [q_.b])
                        b_ = None
                        if li == 1:
                            b_ = bT.next()
                            nblk = 4 * T + 4
                            ts("dve", b_.t[:, 0:nblk], ck.t[:, 0:nblk], Rb.t[:, T:T + 1], None, ALU.add, None, [ck.b, Rb.b], [b_.b])
                        st_T[T] = (q_, b_, o_ps.next())
                    q_, b_, ob = st_T[T]
                    m = i - 4 * T
                    c0 = 128 * m if m > 0 else 0
                    sb_ = s_ps.next()
                    mm(sb_.t[:, c0:TT], Kt.t[0:dk, i * 128:(i + 1) * 128], q_.t[0:dk, c0:TT], True, m < 0, [Kt.b, q_.b], [sb_.b])
                    if m >= 0:
                        mm(sb_.t[:, c0:c0 + 128], IDENT(), NEGTRI(), False, True, [CM.b], [sb_.b])
                    p_ = Pt.next()
                    if li == 1:
                        act(p_.t[:, c0:TT], sb_.t[:, c0:TT], AF.Exp, [sb_.b, b_.b], [p_.b], bias=b_.t[:, i:i + 1])
                    else:
                        act(p_.t[:, c0:TT], sb_.t[:, c0:TT], AF.Exp, [sb_.b], [p_.b])
                    pend.append((T, i, p_, c0))
                    if len(pend) > LOOKAHEAD:
                        issue_pv(*pend.pop(0))
                while pend:
                    issue_pv(*pend.pop(0))
                if h == 3:
                    while b3_pend:
                        b3(b3_pend.pop(0))
            P.flush()

    def dump():
        tens = {"hS": hS, "agR0": agR0, "agR1": agR1, "Qs": Qs, "Ks": Ks, "Vs": Vs, "pR": pR, "Cs": Cs, "Lf": Lf, "pS": pS}
        for nm in dbg:
            t = tens[nm]
            ext = nc.dram_tensor("dbg_" + nm, list(t.shape), t.dtype, kind="ExternalOutput")
            if len(t.shape) == 3:
                dma("sp", ext[:, :, :], t[:, :, :], (), ())
            elif nm == "pS":
                dma("sp", ext[0, 0:D, :], t[0, 0:D, :], (), ())
            elif nm == "hS":
                nn = OPT.get('ntiles', 8) * TT
                dma("sp", ext[:, 0:nn], t[:, 0:nn], (), ())
            else:
                dma("sp", ext[:, :], t[:, :], (), ())
        P.flush()

    consts_load()
    if OPT.get('fox_first'):
        stage_tok(3)
        stage_proj(1)
        if stop_after != "proj1":
            stage_attn(1)
        P.flush()
        dump()
        gstack.close()
        return nc
    stage_tok(0)
    if stop_after != "tok0":
        stage_proj(0)
        if stop_after != "proj0":
            stage_attn(0)
            if stop_after != "attn0":
                stage_tok(1)
                stage_proj(1)
                stage_attn(1)
                stage_tok(2)
    P.flush()
    dump()
    gstack.close()
    return nc


GC_FFN1 = 0
GC_MIX = 16
GC_FFN2 = 32
GC_PLE = 48
GC_QLAT = 64
GC_KVLAT = 68
GC_GQ = 70
GC_GK = 71
GC_FQ = 72
GC_FK = 73
GC_INVF = 74
GC_SGN = 75
GC_BF = 76
GC_EPS = 77
NG = 78
OPT = {}


def _chunk_cols(v):
    return np.ascontiguousarray(v.reshape(-1, 128).T)


def prep_inputs(inp):
    f32 = np.float32
    x = inp["x"]
    p = inp["p"]
    positions = inp["positions"]
    common = {}
    for nm, key in (("w1in", "ffn1_w_in"), ("w2in", "ffn2_w_in")):
        w = inp[key].reshape(2, 8, 128, 2, NC_FF, 128)
        w = w.transpose(0, 4, 2, 3, 1, 5)
        common[nm] = np.ascontiguousarray(w).reshape(2, NC_FF, 128, 2048)
    for nm, key in (("w1out", "ffn1_w_out"), ("w2out", "ffn2_w_out")):
        w = inp[key].reshape(2, NC_FF, 128, D).transpose(0, 2, 1, 3)
        common[nm] = np.ascontiguousarray(w)
    common["wgate"] = np.ascontiguousarray(inp["ple_w_gate"].reshape(2, 8, 128, D).transpose(0, 2, 1, 3))
    common["wproj"] = np.ascontiguousarray(inp["ple_w_proj"].reshape(2, 2, 128, D).transpose(0, 2, 1, 3))
    common["mla_win"] = np.ascontiguousarray(inp["mla_w_in"][0].reshape(8, 128, 800).transpose(1, 0, 2))
    cm = np.zeros((6, 128, 128), f32)
    cm[0] = 1.0
    cm[1, :96, :] = 1.0
    cm[2, :64, :64] = 1.0
    cm[2, 64:, 64:] = 1.0
    cm[3] = np.eye(128, dtype=f32)
    kk, qq = np.meshgrid(np.arange(128), np.arange(128), indexing="ij")
    cm[4] = np.where(kk > qq, -30000.0, 0.0)
    hh, tt_ = np.arange(128) // 32, np.arange(128) % 32
    cm[5] = ((hh[:, None] == hh[None, :]) & (tt_[:, None] < tt_[None, :])).astype(f32)
    common["cmats"] = cm
    sel = np.zeros((32, 128), f32)
    for i in range(32):
        sel[i, 64 + i] = 1.0
    for j in range(32):
        sel[(j + 16) % 32, 96 + j] = 1.0
    common["selm"] = sel
    swap = np.concatenate([np.arange(16, 32), np.arange(0, 16)])
    w_uq = inp["mla_w_uq"][0].reshape(512, 16, 96)
    w_uq_ext = np.concatenate([w_uq, w_uq[:, :, 64 + swap]], axis=2)
    w_ukv = inp["mla_w_ukv"][0].reshape(256, 16, 128)
    w_k_ext = np.concatenate([w_ukv[:, :, :64], np.zeros((256, 16, 64), f32)], axis=2)
    w_v = w_ukv[:, :, 64:]
    gq = inp["mla_g_qn"][0]
    gk = inp["mla_g_kn"][0]
    gq_ext = np.concatenate([gq, gq[64 + swap]])
    gk_ext = np.concatenate([gk, gk[64 + swap]])
    inv_freq = (10000.0 ** (-np.arange(0, 32, 2, dtype=f32) / 32)).astype(f32)
    fox_in = inp["fox_w_in"][0]
    fq = fox_in[:, 0:1024].reshape(D, 16, 64)
    fk = fox_in[:, 1024:2048].reshape(D, 16, 64)
    fv = fox_in[:, 2048:3072].reshape(D, 16, 64)
    ff = fox_in[:, 3072:3088]
    in_maps = []
    for c in range(8):
        b, r = c // 4, c % 4
        hs = slice(4 * r, 4 * r + 4)
        m = dict(common)
        m["xT"] = np.ascontiguousarray(x[b, r * NT:(r + 1) * NT, :].T)
        m["pT"] = np.ascontiguousarray(p[:, b, r * NT:(r + 1) * NT, :].transpose(0, 2, 1))
        m["pos"] = np.ascontiguousarray(np.broadcast_to(positions[b].astype(np.int32)[None, :], (32, S)))
        g = np.zeros((128, NG), f32)
        for l in range(2):
            g[:, GC_FFN1 + 8 * l:GC_FFN1 + 8 * l + 8] = _chunk_cols(inp["g_ffn1"][l])
            g[:, GC_MIX + 8 * l:GC_MIX + 8 * l + 8] = _chunk_cols(inp["g_mix"][l])
            g[:, GC_FFN2 + 8 * l:GC_FFN2 + 8 * l + 8] = _chunk_cols(inp["g_ffn2"][l])
            g[:, GC_PLE + 8 * l:GC_PLE + 8 * l + 8] = _chunk_cols(inp["g_ple"][l])
        g[:, GC_QLAT:GC_QLAT + 4] = _chunk_cols(inp["mla_g_q_lat"][0])
        g[:, GC_KVLAT:GC_KVLAT + 2] = _chunk_cols(inp["mla_g_kv_lat"][0])
        g[:, GC_GQ] = gq_ext
        g[:, GC_GK] = gk_ext
        g[:, GC_FQ] = np.tile(inp["fox_g_qn"][0], 2)
        g[:, GC_FK] = np.tile(inp["fox_g_kn"][0], 2)
        g[64:96, GC_INVF] = np.tile(inv_freq, 2)
        g[64:80, GC_SGN] = -1.0
        g[80:96, GC_SGN] = 1.0
        g[0:4, GC_BF] = inp["fox_b_f"][0][hs]
        g[:, GC_EPS] = EPS
        m["gvec"] = g
        m["wuq"] = np.ascontiguousarray(w_uq_ext[:, hs, :].reshape(4, 128, 512).transpose(1, 0, 2))
        m["wukvk"] = np.ascontiguousarray(w_k_ext[:, hs, :].reshape(2, 128, 512).transpose(1, 0, 2))
        m["wukvv"] = np.ascontiguousarray(w_v[:, hs, :].reshape(2, 128, 256).transpose(1, 0, 2))
        m["mla_wo"] = np.ascontiguousarray(inp["mla_w_o"][0][256 * r:256 * (r + 1), :].reshape(2, 128, D).transpose(1, 0, 2))
        m["fwq"] = np.ascontiguousarray(fq[:, hs, :].reshape(8, 128, 256).transpose(1, 0, 2))
        m["fwk"] = np.ascontiguousarray(fk[:, hs, :].reshape(8, 128, 256).transpose(1, 0, 2))
        m["fwv"] = np.ascontiguousarray(fv[:, hs, :].reshape(8, 128, 256).transpose(1, 0, 2))
        m["fwf"] = np.ascontiguousarray(ff[:, hs].reshape(8, 128, 4).transpose(1, 0, 2))
        m["fox_wo"] = np.ascontiguousarray(inp["fox_w_o"][0][256 * r:256 * (r + 1), :].reshape(2, 128, D).transpose(1, 0, 2))
        in_maps.append(m)
    return in_maps


_NC_CACHE = {}


def kernel(**inputs):
    inp = {k: np.asarray(v) for k, v in inputs.items()}
    in_maps = prep_inputs(inp)
    if "nc" not in _NC_CACHE:
        _NC_CACHE["nc"] = build_program()
    nc = _NC_CACHE["nc"]
    res = run_bass_kernel_spmd(nc, in_maps, core_ids=list(range(8)))
    out = np.empty((2, S, D), np.float32)
    for c in range(8):
        b, r = c // 4, c % 4
        out[b, r * NT:(r + 1) * NT, :] = np.asarray(res.results[c]["yT"]).T
    return out
```

```python
import numpy as np
from contextlib import ExitStack
import concourse.bass as bass
import concourse.mybir as mybir
from concourse.bass_utils import run_bass_kernel_spmd

F32 = mybir.dt.float32
BF16 = mybir.dt.bfloat16
I32 = mybir.dt.int32
AF = mybir.ActivationFunctionType
ALU = mybir.AluOpType

D = 1024
S = 16384
NT = 4096
TT = 512
DFF = 2816
NC_FF = DFF // 128
EPS = 1e-6
PI = float(np.pi)
C1 = 6.28125
C2 = float(2 * np.pi - 6.28125)


class Buf:
    __slots__ = ("name", "last_w", "rd", "rd_dma", "wr_all")

    def __init__(self, name=""):
        self.name = name
        self.last_w = None
        self.rd = {}
        self.rd_dma = []
        self.wr_all = []

    def reset(self):
        self.last_w = None
        self.rd = {}
        self.rd_dma = []
        self.wr_all = []


class Op:
    __slots__ = ("eng", "fn", "deps", "dma", "sig", "needed", "idx", "cc", "prev")

    def __init__(self, eng, fn, dma, cc):
        self.eng = eng
        self.fn = fn
        self.deps = []
        self.dma = dma
        self.cc = cc
        self.sig = None
        self.needed = False
        self.prev = None


class Prog:
    ENGS = ("pe", "act", "dve", "pool", "sp")
    NDMASEM = 6
    SEM_ROLL = 24000

    def __init__(self, nc, stack):
        self.nc = nc
        self.stack = stack
        self.ops = []
        self.start = 0
        self.bufs = []
        self.sems = {}
        self.cnt = {}
        self.nsem = 0
        self.dma_pool = {}
        self.dma_n = {}
        self.cc_sem = None
        self.cc_n = 0
        self.out_sigs = []

    def buf(self, name=""):
        b = Buf(name)
        self.bufs.append(b)
        return b

    def newsem(self, tag):
        self.nsem += 1
        return self.stack.enter_context(self.nc.semaphore("s_%s_%d" % (tag, self.nsem)))

    def op(self, eng, fn, reads=(), writes=(), dma=False, cc=False, iwrites=()):
        o = Op(eng, fn, dma, cc)
        o.idx = len(self.ops)
        special = dma or cc
        deps = set()
        for b in reads:
            if b.last_w is not None:
                deps.add(b.last_w)
            deps.update(b.wr_all)
        for b in writes:
            if b.last_w is not None:
                deps.add(b.last_w)
            deps.update(b.wr_all)
            for j in b.rd.values():
                deps.add(j)
            for j in b.rd_dma:
                deps.add(j)
        for b in iwrites:
            if b.last_w is not None:
                deps.add(b.last_w)
        best = {}
        for j in deps:
            p = self.ops[j]
            if p.dma or p.cc:
                o.deps.append(j)
                continue
            if p.eng == eng and not special:
                if eng == "pe":
                    continue
                if not any(b.last_w == j for b in reads):
                    continue
            if best.get(p.eng, -1) < j:
                best[p.eng] = j
        for j in best.values():
            o.deps.append(j)
            self.ops[j].needed = True
        for b in reads:
            if special:
                b.rd_dma.append(o.idx)
            else:
                b.rd[eng] = o.idx
        for b in writes:
            b.last_w = o.idx
            b.rd = {}
            b.rd_dma = []
            b.wr_all = []
        for b in iwrites:
            b.wr_all.append(o.idx)
        self.ops.append(o)
        return o

    def flush(self):
        nc = self.nc
        pend = self.ops[self.start:]
        self.start = len(self.ops)
        if not pend:
            return
        streams = {e: [] for e in self.ENGS}
        for e in self.ENGS:
            if e not in self.sems or self.cnt[e] >= self.SEM_ROLL:
                self.sems[e] = self.newsem(e)
                self.cnt[e] = 0
        if self.cc_sem is None:
            self.cc_sem = self.newsem("cc")
        last_dma = {}
        for o in pend:
            e = o.eng
            streams[e].append(o)
            if o.dma:
                if e not in self.dma_pool:
                    self.dma_pool[e] = [self.newsem("dma" + e) for _ in range(self.NDMASEM)]
                    self.dma_n[e] = 0
                i = self.dma_n[e]
                self.dma_n[e] += 1
                si = i % self.NDMASEM
                s = self.dma_pool[e][si]
                prev = 16 * (i // self.NDMASEM)
                o.sig = (s, prev + 16)
                o.prev = (s, prev)
                last_dma[(e, si)] = o.sig
            elif o.cc:
                self.cc_n += 1
                o.sig = (self.cc_sem, self.cc_n)
                last_dma[("cc", 0)] = o.sig
            elif o.needed:
                self.cnt[e] += 1
                o.sig = (self.sems[e], self.cnt[e])
        ops = self.ops
        finals = list(last_dma.values())

        def run_stream(e):
            def body(eng):
                known = {}
                for o in streams[e]:
                    waits = [ops[j].sig for j in o.deps]
                    if o.dma and o.prev[1] > 0:
                        waits.append(o.prev)
                    for (s, v) in waits:
                        k = id(s)
                        if known.get(k, 0) >= v:
                            continue
                        known[k] = v
                        eng.wait_ge(s, v)
                    ins = o.fn(eng)
                    if o.sig is not None:
                        ins.then_inc(o.sig[0], 16 if o.dma else 1)
                if e == "sp":
                    for (s, v) in finals:
                        eng.wait_ge(s, v)
            return body

        with nc.Block() as block:
            for e, deco in (("pe", block.tensor), ("act", block.scalar), ("dve", block.vector),
                            ("pool", block.gpsimd), ("sp", block.sync)):
                if streams[e] or e == "sp":
                    deco(run_stream(e))
        for b in self.bufs:
            b.reset()


class Tile:
    def __init__(self, t, b):
        self.t = t
        self.b = b


class Rot:
    def __init__(self, tiles):
        self.tiles = tiles
        self.i = 0

    def next(self):
        t = self.tiles[self.i % len(self.tiles)]
        self.i += 1
        return t


def build_program(stop_after=None, dbg=()):
    nc = bass.Bass("TRN2", target_bir_lowering=False)
    gstack = ExitStack()
    P = Prog(nc, gstack)

    def din(name, shape, dt=F32):
        return nc.dram_tensor(name, list(shape), dt, kind="ExternalInput")

    def dint(name, shape, dt):
        return nc.dram_tensor(name, list(shape), dt)

    xT = din("xT", [D, NT])
    pT = din("pT", [2, 256, NT])
    pos = din("pos", [32, S], I32)
    gvec = din("gvec", [128, NG])
    w1in = din("w1in", [2, NC_FF, 128, 2048])
    w2in = din("w2in", [2, NC_FF, 128, 2048])
    w1out = din("w1out", [2, 128, NC_FF, D])
    w2out = din("w2out", [2, 128, NC_FF, D])
    wgate = din("wgate", [2, 128, 8, D])
    wproj = din("wproj", [2, 128, 2, D])
    mla_win = din("mla_win", [128, 8, 800])
    wuq = din("wuq", [128, 4, 512])
    wukvk = din("wukvk", [128, 2, 512])
    wukvv = din("wukvv", [128, 2, 256])
    selm = din("selm", [32, 128])
    mla_wo = din("mla_wo", [128, 2, D])
    fwq = din("fwq", [128, 8, 256])
    fwk = din("fwk", [128, 8, 256])
    fwv = din("fwv", [128, 8, 256])
    fwf = din("fwf", [128, 8, 4])
    fox_wo = din("fox_wo", [128, 2, D])
    cmats = din("cmats", [6, 128, 128])
    yT = nc.dram_tensor("yT", [D, NT], F32, kind="ExternalOutput")

    hS = dint("hS", [D, NT], F32)
    agS0 = dint("agS0", [16, 800, 256], BF16)
    agR0 = dint("agR0", [16, 4 * 800, 256], BF16)
    agS1 = dint("agS1", [16, D, 256], BF16)
    agR1 = dint("agR1", [16, 4 * D, 256], BF16)
    Qs = dint("Qs", [4, 96, S], BF16)
    Ks = dint("Ks", [4, 96, S], BF16)
    Vs = dint("Vs", [4, 128, 128 * 64], BF16)
    Lf = dint("Lf", [4, S], F32)
    Cs = dint("Cs", [4, S], F32)
    Os = dint("Os", [128, 1], F32)
    pS = dint("pS", [8, 4 * D, 512], F32)
    pR = dint("pR", [8, D, 512], F32)
    B_hS, B_agS0, B_agR0, B_agS1, B_agR1 = (P.buf(n) for n in ("hS", "agS0", "agR0", "agS1", "agR1"))
    B_Qs, B_Ks, B_Vs, B_Lf, B_Cs, B_Os, B_pS, B_pR = (P.buf(n) for n in ("Qs", "Ks", "Vs", "Lf", "Cs", "Os", "pS", "pR"))
    B_y = P.buf("y")
    B_pRq = [P.buf("pR%d" % i) for i in range(8)]
    B_pSq = [P.buf("pS%d" % i) for i in range(8)]
    RG = [[0, 1, 2, 3], [4, 5, 6, 7]]

    def dma(eng, out, in_, reads=(), writes=(), slow=False, iw=()):
        if slow:
            return P.op(eng, lambda e: e.dma_start(out=out, in_=in_, allow_slow_non_contiguous=True), reads, writes, dma=True, iwrites=iw)
        return P.op(eng, lambda e: e.dma_start(out=out, in_=in_), reads, writes, dma=True, iwrites=iw)

    def mm(out, lhsT, rhs, start, stop, reads, writes):
        return P.op("pe", lambda e: e.matmul(out, lhsT=lhsT, rhs=rhs, start=start, stop=stop, skip_group_check=True), reads, writes)

    def act(out, in_, func, reads, writes, bias=None, scale=1.0):
        if bias is None:
            return P.op("act", lambda e: e.activation(out=out, in_=in_, func=func, scale=scale), reads, writes)
        return P.op("act", lambda e: e.activation(out=out, in_=in_, func=func, bias=bias, scale=scale), reads, writes)

    def tt(eng, out, in0, in1, op, reads, writes):
        return P.op(eng, lambda e: e.tensor_tensor(out=out, in0=in0, in1=in1, op=op), reads, writes)

    def ts(eng, out, in0, s1, s2, op0, op1, reads, writes):
        if s2 is None:
            return P.op(eng, lambda e: e.tensor_scalar(out=out, in0=in0, scalar1=s1, scalar2=None, op0=op0), reads, writes)
        return P.op(eng, lambda e: e.tensor_scalar(out=out, in0=in0, scalar1=s1, scalar2=s2, op0=op0, op1=op1), reads, writes)

    def stt(eng, out, in0, scalar, in1, op0, op1, reads, writes):
        return P.op(eng, lambda e: e.scalar_tensor_tensor(out=out, in0=in0, scalar=scalar, in1=in1, op0=op0, op1=op1), reads, writes)

    def cp(eng, out, in_, reads, writes):
        return P.op(eng, lambda e: e.tensor_copy(out=out, in_=in_), reads, writes)

    def recip(out, in_, reads, writes):
        return P.op("dve", lambda e: e.reciprocal(out=out, in_=in_), reads, writes)

    def memset(eng, ap, val, writes):
        return P.op(eng, lambda e: e.memset(ap, val), (), writes)

    def allgather(src, dst, idx, bsrc, bdst):
        if OPT.get('nocc'):
            return
        P.op("pool", lambda e: e.collective_compute("AllGather", ALU.bypass, replica_groups=RG,
                                                    ins=[src[idx, :, :]], outs=[dst[idx, :, :]]),
             [bsrc], (), cc=True, iwrites=[bdst])

    uid = [0]

    def sbuf(stack, name, shape, dt):
        uid[0] += 1
        name = "%s_%d" % (name, uid[0])
        return Tile(stack.enter_context(nc.sbuf_tensor(name, list(shape), dt)), P.buf(name))

    def psum(stack, name):
        return Tile(stack.enter_context(nc.psum_tensor(name, [128, 512], F32)), P.buf(name))

    G = sbuf(gstack, "gv", [128, NG], F32)
    CM = sbuf(gstack, "cm", [128, 5, 128], BF16)
    TRI = sbuf(gstack, "tri", [128, 128], F32)
    ps = [psum(gstack, "ps%d" % i) for i in range(8)]

    def consts_load():
        dma("sp", G.t[:, :], gvec[:, :], (), [G.b])
        for i in range(5):
            dma("pool", CM.t[:, i, :], cmats[i, :, :], (), [CM.b])
        dma("sp", TRI.t[:, :], cmats[5, :, :], (), [TRI.b])
        ts("dve", G.t[:, GC_GQ:GC_GQ + 1], G.t[:, GC_GQ:GC_GQ + 1], float(96 ** -0.5), None, ALU.mult, None, [G.b], [G.b])
        ts("dve", G.t[:, GC_FQ:GC_FQ + 1], G.t[:, GC_FQ:GC_FQ + 1], float(64 ** -0.5), None, ALU.mult, None, [G.b], [G.b])

    ONES_ALL = lambda: CM.t[:, 0, :]
    ONES96 = lambda: CM.t[:, 1, :]
    ONES_BD = lambda: CM.t[:, 2, :]
    IDENT = lambda: CM.t[:, 3, :]
    NEGTRI = lambda: CM.t[:, 4, :]
    EPSC = lambda p0, p1: G.t[p0:p1, GC_EPS:GC_EPS + 1]

    def rmsnorm(src, nch, nfeat, gc0, out, sq, ssps, rs):
        tt("dve", sq.t[:, 0:nch, :], src.t[:, 0:nch, :], src.t[:, 0:nch, :], ALU.mult, [src.b], [sq.b])
        for c in range(nch):
            mm(ssps.t[:, :], ONES_ALL(), sq.t[:, c, :], c == 0, c == nch - 1, [CM.b, sq.b], [ssps.b])
        act(rs.t[:, :], ssps.t[:, :], AF.Sqrt, [ssps.b, G.b], [rs.b], bias=EPSC(0, 128), scale=1.0 / nfeat)
        recip(rs.t[:, :], rs.t[:, :], [rs.b], [rs.b])
        for c in range(nch):
            stt("dve", out.t[:, c, :], src.t[:, c, :], G.t[:, gc0 + c:gc0 + c + 1], rs.t[:, :], ALU.mult, ALU.mult,
                [src.b, G.b, rs.b], [out.b])

    def rstd_act(rs_ap, ss_ap, nfeat, rb, sb_, p0=0, p1=128):
        act(rs_ap, ss_ap, AF.Ln, [sb_, G.b], [rb], bias=EPSC(p0, p1), scale=1.0 / nfeat)
        act(rs_ap, rs_ap, AF.Exp, [rb], [rb], scale=-0.5)

    def stage_tok(stage):
        NH = 2
        ST = NH * TT
        with ExitStack() as st:
            hT_slots = [sbuf(st, "hT%d" % i, [128, 8, ST], F32) for i in range(2)]
            hb_slots = [[P.buf("hA%d" % i), P.buf("hB%d" % i)] for i in range(2)]
            hT = hT_slots[0]
            hb = hb_slots[0]
            xn = sbuf(st, "xn", [128, 8, ST], BF16)
            sq = sbuf(st, "sq", [128, 8, TT], BF16)
            aT = sbuf(st, "aT", [128, NC_FF, ST], BF16)
            rs = sbuf(st, "rs", [128, TT], F32)
            sg = Rot([sbuf(st, "sg%d" % i, [128, TT], F32) for i in range(2)])
            win = Rot([sbuf(st, "win%d" % i, [128, 2048], BF16) for i in range(3)])
            wom = Rot([sbuf(st, "wom%d" % i, [128, NC_FF, 128], BF16) for i in range(4)])
            ss_ps = ps[0]
            g_ps = Rot([ps[1], ps[2]])
            u_ps = Rot([ps[3], ps[4]])
            o_ps = Rot([ps[5], ps[6]])
            if stage in (1, 2):
                wg = sbuf(st, "wg", [128, 8, D], BF16)
                wp = sbuf(st, "wp", [128, 2, D], BF16)
                ptl = sbuf(st, "ptl", [128, 2, ST], BF16)
                li = stage - 1

                def rs_op(q):
                    P.op("pool", lambda e: e.collective_compute("ReduceScatter", ALU.add, replica_groups=RG,
                                                                ins=[pS[q, :, :]], outs=[pR[q, :, :]]),
                         (), [B_pRq[q]], cc=True)
                dma("pool", wg.t[:, :, :], wgate[li, :, :, :], (), [wg.b])
                dma("pool", wp.t[:, :, :], wproj[li, :, :, :], (), [wp.b])
            if stage == 0:
                mw = sbuf(st, "mw", [128, 8, 800], BF16)
                dma("pool", mw.t[:, :, :], mla_win[:, :, :], (), [mw.b])
                zf = sbuf(st, "zf", [128, 6, TT], F32)
                zn = sbuf(st, "zn", [128, 6, TT], BF16)
                kpe = sbuf(st, "kpe", [32, TT], BF16)

            xb = [P.buf("xA"), P.buf("xB")]
            ab = [P.buf("aA"), P.buf("aB")]

            def norm_sq(hf):
                cs = slice(hf * TT, (hf + 1) * TT)
                tt("dve", sq.t[:, :, :], hT.t[:, :, cs], hT.t[:, :, cs], ALU.mult, [hb[hf]], [sq.b])

            def norm_fin(hf, gc0):
                cs = slice(hf * TT, (hf + 1) * TT)
                for c in range(8):
                    mm(ss_ps.t[:, :], ONES_ALL(), sq.t[:, c, :], c == 0, c == 7, [CM.b, sq.b], [ss_ps.b])
                rstd_act(rs.t[:, :], ss_ps.t[:, :], D, rs.b, ss_ps.b)
                for c in range(8):
                    stt("dve", xn.t[:, c, cs], hT.t[:, c, cs], G.t[:, gc0 + c:gc0 + c + 1], rs.t[:, :], ALU.mult, ALU.mult,
                        [hb[hf], G.b, rs.b], [xb[hf]])

            def norm_half(hf, gc0):
                norm_sq(hf)
                norm_fin(hf, gc0)

            def norm_h(gc0):
                for hf in range(NH):
                    norm_half(hf, gc0)

            def ffn(li, win_d, wout_d, next_gc):
                wo_pref = []

                def wo_load(m):
                    wq = wom.next()
                    dma("pool", wq.t[:, :, :], wout_d[li, :, :, m * 128:(m + 1) * 128], (), [wq.b])
                    wo_pref.append(wq)

                for c in range(NC_FF):
                    w = win.next()
                    dma("pool", w.t[:, :], win_d[li, c, :, :], (), [w.b])
                    if c in (8, 12, 16, 20):
                        wo_load(len(wo_pref))
                    for hf in range(NH):
                        cs = slice(hf * TT, (hf + 1) * TT)
                        gp = g_ps.next()
                        up = u_ps.next()
                        for half, pp in ((0, gp), (1, up)):
                            for k in range(8):
                                mm(pp.t[:, :], w.t[:, (half * 8 + k) * 128:(half * 8 + k + 1) * 128], xn.t[:, k, cs],
                                   k == 0, k == 7, [w.b, xb[hf]], [pp.b])
                        s = sg.next()
                        act(s.t[:, :], gp.t[:, :], AF.Silu, [gp.b], [s.b])
                        tt("dve", aT.t[:, c, cs], s.t[:, :], up.t[:, :], ALU.mult, [s.b, up.b], [ab[hf]])
                for g4 in range(2):
                    for hf in range(NH):
                        cs = slice(hf * TT, (hf + 1) * TT)
                        for m in range(4 * g4, 4 * g4 + 4):
                            wq = wo_pref[m]
                            op_ = o_ps.next()
                            for c in range(NC_FF):
                                mm(op_.t[:, :], wq.t[:, c, :], aT.t[:, c, cs], c == 0, c == NC_FF - 1, [wq.b, ab[hf]], [op_.b])
                            stt("dve", hT.t[:, m, cs], op_.t[:, :], 0.5, hT.t[:, m, cs], ALU.mult, ALU.add, [op_.b, hb[hf]], [hb[hf]])
                            if hf == NH - 1 and len(wo_pref) < 8:
                                wo_load(len(wo_pref))
                            if g4 == 1 and next_gc is not None and hf == 1 and m == 5:
                                norm_fin(0, next_gc)
                        if g4 == 1 and next_gc is not None:
                            if hf == 0:
                                norm_sq(0)
                            else:
                                norm_half(1, next_gc)

            def ple(li, t, next_gc):
                dma("pool", ptl.t[:, :, :], pT[li, :, t * ST:(t + 1) * ST].rearrange("(c p) n -> p c n", p=128), (), [ptl.b])
                for hf in range(NH):
                    cs = slice(hf * TT, (hf + 1) * TT)
                    for m in range(8):
                        gp = g_ps.next()
                        up = u_ps.next()
                        for k in range(8):
                            mm(gp.t[:, :], wg.t[:, k, m * 128:(m + 1) * 128], xn.t[:, k, cs], k == 0, k == 7, [wg.b, xb[hf]], [gp.b])
                        for k in range(2):
                            mm(up.t[:, :], wp.t[:, k, m * 128:(m + 1) * 128], ptl.t[:, k, cs], k == 0, k == 1, [wp.b, ptl.b], [up.b])
                        s = sg.next()
                        act(s.t[:, :], gp.t[:, :], AF.Sigmoid, [gp.b], [s.b])
                        tt("dve", s.t[:, :], s.t[:, :], up.t[:, :], ALU.mult, [s.b, up.b], [s.b])
                        tt("dve", hT.t[:, m, cs], hT.t[:, m, cs], s.t[:, :], ALU.add, [hb[hf], s.b], [hb[hf]])
                        if next_gc is not None and hf == 1 and m == 1:
                            norm_fin(0, next_gc)
                    if next_gc is not None:
                        if hf == 0:
                            norm_sq(0)
                        else:
                            norm_half(1, next_gc)

            def mla_pre(t):
                for hf in range(NH):
                    cs = slice(hf * TT, (hf + 1) * TT)
                    for zc in range(7):
                        gp = g_ps.next()
                        mcols = 128 if zc < 6 else 32
                        for k in range(8):
                            mm(gp.t[0:mcols, :], mw.t[:, k, zc * 128:zc * 128 + mcols], xn.t[:, k, cs], k == 0, k == 7,
                               [mw.b, xb[hf]], [gp.b])
                        if zc < 6:
                            act(zf.t[:, zc, :], gp.t[:, :], AF.Copy, [gp.b], [zf.b])
                        else:
                            act(kpe.t[:, :], gp.t[0:32, :], AF.Copy, [gp.b], [kpe.b])
                    for (c0, nch, nf, gc) in ((0, 4, 512, GC_QLAT), (4, 2, 256, GC_KVLAT)):
                        tt("dve", sq.t[:, c0:c0 + nch, :], zf.t[:, c0:c0 + nch, :], zf.t[:, c0:c0 + nch, :], ALU.mult, [zf.b], [sq.b])
                        for c in range(nch):
                            mm(ss_ps.t[:, :], ONES_ALL(), sq.t[:, c0 + c, :], c == 0, c == nch - 1, [CM.b, sq.b], [ss_ps.b])
                        rstd_act(rs.t[:, :], ss_ps.t[:, :], nf, rs.b, ss_ps.b)
                        for c in range(nch):
                            stt("dve", zn.t[:, c0 + c, :], zf.t[:, c0 + c, :], G.t[:, gc + c:gc + c + 1], rs.t[:, :], ALU.mult, ALU.mult,
                                [zf.b, G.b, rs.b], [zn.b])
                    for h2 in range(2):
                        c2 = slice(h2 * 256, (h2 + 1) * 256)
                        idx = 2 * (NH * t + hf) + h2
                        dma("sp", agS0[idx, 0:768, :].rearrange("(c p) n -> p c n", p=128), zn.t[:, :, c2], [zn.b], iw=[B_agS0])
                        dma("sp", agS0[idx, 768:800, :], kpe.t[:, c2], [kpe.b], iw=[B_agS0])
                        allgather(agS0, agR0, idx, B_agS0, B_agR0)

            ntl = OPT.get('ntiles', NT // ST)

            def prefetch(t):
                tok = slice(t * ST, (t + 1) * ST)
                hT_ = hT_slots[t % 2]
                hb_ = hb_slots[t % 2]
                if stage in (0, 3):
                    dma("sp", hT_.t[:, :, :], xT[:, tok].rearrange("(c p) n -> p c n", p=128), (), hb_)
                else:
                    dma("sp", hT_.t[:, :, :], hS[:, tok].rearrange("(c p) n -> p c n", p=128), [B_hS], hb_)
                    for hf in range(NH):
                        cs = slice(hf * TT, (hf + 1) * TT)
                        P.op("pool", (lambda o_, i_: (lambda e: e.dma_start(out=o_, in_=i_, accum_op=ALU.add)))(
                            hT_.t[:, :, cs], pR[NH * t + hf, :, :].rearrange("(c p) n -> p c n", p=128)),
                            [B_pRq[NH * t + hf], hb_[hf]], [hb_[hf]], dma=True)

            prefetch(0)
            for t in range(ntl):
                tok = slice(t * ST, (t + 1) * ST)
                hT = hT_slots[t % 2]
                hb = hb_slots[t % 2]
                if stage in (1, 2):
                    li = stage - 1
                    norm_h(GC_FFN2 + 8 * li)
                    if stage == 2 and t + 1 < ntl:
                        prefetch(t + 1)
                    ffn(li, w2in, w2out, GC_PLE + 8 * li)
                    ple(li, t, GC_FFN1 + 8 if stage == 1 else None)
                if stage == 0:
                    norm_h(GC_FFN1 + 0)
                    if t + 1 < ntl:
                        prefetch(t + 1)
                    ffn(0, w1in, w1out, GC_MIX + 0)
                    dma("sp", hS[:, tok].rearrange("(c p) n -> p c n", p=128), hT.t[:, :, :], hb, iw=[B_hS])
                    mla_pre(t)
                elif stage in (1, 3):
                    if stage == 3:
                        norm_h(GC_FFN1 + 8)
                    if t + 1 < ntl:
                        prefetch(t + 1)
                    ffn(1, w1in, w1out, GC_MIX + 8)
                    dma("sp", hS[:, tok].rearrange("(c p) n -> p c n", p=128), hT.t[:, :, :], hb, iw=[B_hS])
                    for hf in range(NH):
                        for h2 in range(2):
                            c2 = slice(hf * TT + h2 * 256, hf * TT + (h2 + 1) * 256)
                            idx = 2 * (NH * t + hf) + h2
                            dma("sp", agS1[idx, :, :].rearrange("(c p) n -> p c n", p=128), xn.t[:, :, c2], [xb[hf]], iw=[B_agS1])
                            allgather(agS1, agR1, idx, B_agS1, B_agR1)
                else:
                    dma("sp", yT[:, tok].rearrange("(c p) n -> p c n", p=128), hT.t[:, :, :], hb, iw=[B_y])
            P.flush()

    def stage_proj(li):
        with ExitStack() as st:
            sqh = Rot([sbuf(st, "sqh%d" % i, [128, TT], BF16) for i in range(3)])
            rsh = Rot([sbuf(st, "rsh%d" % i, [128, TT], F32) for i in range(3)])
            vt = Rot([sbuf(st, "vt%d" % i, [128, 4, 256], BF16) for i in range(2)])
            qk_ps = Rot([ps[0], ps[1], ps[2], ps[7]] if li == 0 else [ps[0], ps[1], ps[2]])
            ss_ps = Rot([ps[3], ps[4], ps[5]] if li == 0 else [ps[3], ps[4]])
            v_ps = Rot([ps[6]] if li == 0 else [ps[5], ps[6]])
            f_ps = ps[7]
            Vs_v = Vs[:, :, :].rearrange("h p (b d) -> h p b d", d=64)
            if li == 0:
                w_q = sbuf(st, "w_q", [128, 4, 512], BF16)
                w_k = sbuf(st, "w_k", [128, 2, 512], BF16)
                w_v = sbuf(st, "w_v", [128, 2, 256], BF16)
                sel = sbuf(st, "sel", [32, 128], BF16)
                dma("pool", w_q.t[:, :, :], wuq[:, :, :], (), [w_q.b])
                dma("pool", w_k.t[:, :, :], wukvk[:, :, :], (), [w_k.b])
                dma("pool", w_v.t[:, :, :], wukvv[:, :, :], (), [w_v.b])
                dma("pool", sel.t[:, :], selm[:, :], (), [sel.b])
                zq = Rot([sbuf(st, "zq%d" % i, [128, 4, TT], BF16) for i in range(2)])
                zkv = Rot([sbuf(st, "zkv%d" % i, [128, 2, TT], BF16) for i in range(2)])
                kp = Rot([sbuf(st, "kp%d" % i, [32, TT], BF16) for i in range(2)])
                posi = sbuf(st, "posi", [128, TT], I32)
                ang = sbuf(st, "ang", [128, TT], F32)
                nfl = sbuf(st, "nfl", [128, TT], F32)
                nin = sbuf(st, "nin", [128, TT], I32)
                msk = sbuf(st, "msk", [128, TT], F32)
                cos_tb = [sbuf(st, "cos_t%d" % i, [128, TT], F32) for i in range(2)]
                sin_tb = [sbuf(st, "sin_t%d" % i, [128, TT], F32) for i in range(2)]
                qn = Rot([sbuf(st, "qn%d" % i, [128, TT], F32) for i in range(3)])
                sw = Rot([sbuf(st, "sw%d" % i, [128, TT], F32) for i in range(3)])
                t1 = Rot([sbuf(st, "t1%d" % i, [128, TT], F32) for i in range(3)])
                qo = Rot([sbuf(st, "qo%d" % i, [128, TT], BF16) for i in range(5)])
                R_ = slice(64, 96)

                def rope_tables(T):
                    cos_t = cos_tb[T % 2]
                    sin_t = sin_tb[T % 2]
                    dma("sp", posi.t[R_, :], pos[:, T * TT:(T + 1) * TT], (), [posi.b])
                    cp("dve", ang.t[R_, :], posi.t[R_, :], [posi.b], [ang.b])
                    ts("dve", ang.t[R_, :], ang.t[R_, :], G.t[R_, GC_INVF:GC_INVF + 1], None, ALU.mult, None, [ang.b, G.b], [ang.b])
                    ts("dve", nfl.t[R_, :], ang.t[R_, :], 1.0 / (2 * PI), None, ALU.mult, None, [ang.b], [nfl.b])
                    cp("dve", nin.t[R_, :], nfl.t[R_, :], [nfl.b], [nin.b])
                    cp("dve", nfl.t[R_, :], nin.t[R_, :], [nin.b], [nfl.b])
                    stt("dve", ang.t[R_, :], nfl.t[R_, :], -C1, ang.t[R_, :], ALU.mult, ALU.add, [nfl.b, ang.b], [ang.b])
                    stt("dve", ang.t[R_, :], nfl.t[R_, :], -C2, ang.t[R_, :], ALU.mult, ALU.add, [nfl.b, ang.b], [ang.b])
                    ts("dve", ang.t[R_, :], ang.t[R_, :], -PI, PI, ALU.max, ALU.min, [ang.b], [ang.b])
                    act(sin_t.t[R_, :], ang.t[R_, :], AF.Sin, [ang.b], [sin_t.b])
                    ts("dve", sin_t.t[R_, :], sin_t.t[R_, :], G.t[R_, GC_SGN:GC_SGN + 1], None, ALU.mult, None, [sin_t.b, G.b], [sin_t.b])
                    ts("dve", msk.t[R_, :], ang.t[R_, :], -1.0, None, ALU.mult, None, [ang.b], [msk.b])
                    tt("dve", msk.t[R_, :], msk.t[R_, :], ang.t[R_, :], ALU.max, [msk.b, ang.b], [msk.b])
                    ts("dve", msk.t[R_, :], msk.t[R_, :], -1.0, PI / 2, ALU.mult, ALU.add, [msk.b], [msk.b])
                    act(cos_t.t[R_, :], msk.t[R_, :], AF.Sin, [msk.b], [cos_t.b])

                def hA(pq):
                    s_ = sqh.next()
                    act(s_.t[:, :], pq.t[:, :], AF.Square, [pq.b], [s_.b])
                    sp_ = ss_ps.next()
                    mm(sp_.t[:, :], ONES96(), s_.t[:, :], True, True, [CM.b, s_.b], [sp_.b])
                    return sp_

                def hB(pq, sp_, gc):
                    r_ = rsh.next()
                    rstd_act(r_.t[:, :], sp_.t[:, :], 96, r_.b, sp_.b)
                    o_ = qo.next()
                    stt("dve", o_.t[:, :], pq.t[:, :], G.t[:, gc:gc + 1], r_.t[:, :], ALU.mult, ALU.mult, [pq.b, G.b, r_.b], [o_.b])
                    return o_

                def hC(o_, dst, bdst, T):
                    cos_t = cos_tb[T % 2]
                    sin_t = sin_tb[T % 2]
                    w_ = sw.next()
                    act(w_.t[64:96, :], o_.t[96:128, :], AF.Copy, [o_.b], [w_.b])
                    a_ = t1.next()
                    tt("dve", a_.t[R_, :], o_.t[R_, :], cos_t.t[R_, :], ALU.mult, [o_.b, cos_t.b], [a_.b])
                    tt("dve", w_.t[R_, :], w_.t[R_, :], sin_t.t[R_, :], ALU.mult, [w_.b, sin_t.b], [w_.b])
                    tt("dve", o_.t[R_, :], a_.t[R_, :], w_.t[R_, :], ALU.add, [a_.b, w_.b, o_.b], [o_.b])
                    dma("sp", dst, o_.t[0:96, :], [o_.b], iw=[bdst])

                G0 = agR0[:, :, :].rearrange("i (r f) n -> i r f n", r=4)
                def mla_load(T):
                    r_, lt = T // 8, T % 8
                    a = zq.next()
                    b = zkv.next()
                    c = kp.next()
                    for hf in range(2):
                        cs = slice(hf * 256, (hf + 1) * 256)
                        dma("sp", a.t[:, :, cs], G0[2 * lt + hf, r_, 0:512, :].rearrange("(c p) n -> p c n", p=128), [B_agR0], [a.b])
                        dma("sp", b.t[:, :, cs], G0[2 * lt + hf, r_, 512:768, :].rearrange("(c p) n -> p c n", p=128), [B_agR0], [b.b])
                        dma("sp", c.t[:, cs], G0[2 * lt + hf, r_, 768:800, :], [B_agR0], [c.b])
                    return a, b, c

                NTL = S // TT
                tin = {0: mla_load(0)}
                rope_tables(0)

                def v_work(T):
                    a, b, c = tin[T]
                    v_ = vt.next()
                    for blk in range(4):
                        pv = v_ps.next()
                        for k in range(2):
                            mm(pv.t[:, 0:256], b.t[:, k, blk * 128:(blk + 1) * 128], w_v.t[:, k, :], k == 0, k == 1, [b.b, w_v.b], [pv.b])
                        act(v_.t[:, blk, :], pv.t[:, 0:256], AF.Copy, [pv.b], [v_.b])
                    for h in range(4):
                        dma("sp", Vs_v[h, :, T * 4:(T + 1) * 4, :], v_.t[:, :, h * 64:(h + 1) * 64], [v_.b], iw=[B_Vs])

                jobs = [(T, kind, h) for T in range(NTL) for h in range(4) for kind in ("q", "k")]
                stA, stB = [], []
                for step in range(len(jobs) + 2):
                    if step < len(jobs):
                        T, kind, h = jobs[step]
                        jn = step % 8
                        a, b, c = tin[T]
                        gcol = slice(T * TT, (T + 1) * TT)
                        if jn == 0 and T + 1 < NTL:
                            tin[T + 1] = mla_load(T + 1)
                        pq = qk_ps.next()
                        if kind == "q":
                            for k in range(4):
                                mm(pq.t[:, :], w_q.t[:, k, h * 128:(h + 1) * 128], a.t[:, k, :], k == 0, k == 3, [w_q.b, a.b], [pq.b])
                            info = (GC_GQ, Qs[h, 0:96, gcol], B_Qs, T)
                        else:
                            for k in range(2):
                                mm(pq.t[:, :], w_k.t[:, k, h * 128:(h + 1) * 128], b.t[:, k, :], k == 0, False, [w_k.b, b.b], [pq.b])
                            mm(pq.t[:, :], sel.t[:, :], c.t[:, :], False, True, [sel.b, c.b], [pq.b])
                            info = (GC_GK, Ks[h, 0:96, gcol], B_Ks, T)
                        stA.append((pq, hA(pq), info))
                        if jn == 3 and T + 1 < NTL:
                            rope_tables(T + 1)
                        if jn == 5:
                            v_work(T)
                    if step >= 1 and stA and step - 1 < len(jobs):
                        pq, sp_, info = stA.pop(0)
                        stB.append((hB(pq, sp_, info[0]), info))
                    if step >= 2 and stB:
                        o_, info = stB.pop(0)
                        hC(o_, info[1], info[2], info[3])
            else:
                w_q = sbuf(st, "f_q", [128, 8, 256], BF16)
                w_k = sbuf(st, "f_k", [128, 8, 256], BF16)
                w_v = sbuf(st, "f_v", [128, 8, 256], BF16)
                w_f = sbuf(st, "f_f", [128, 8, 4], BF16)
                dma("pool", w_q.t[:, :, :], fwq[:, :, :], (), [w_q.b])
                dma("pool", w_k.t[:, :, :], fwk[:, :, :], (), [w_k.b])
                dma("pool", w_v.t[:, :, :], fwv[:, :, :], (), [w_v.b])
                dma("pool", w_f.t[:, :, :], fwf[:, :, :], (), [w_f.b])
                hn = Rot([sbuf(st, "hn%d" % i, [128, 8, TT], BF16) for i in range(2)])
                qo = Rot([sbuf(st, "fqo%d" % i, [128, TT], BF16) for i in range(4)])
                lf = Rot([sbuf(st, "lf%d" % i, [4, TT], F32) for i in range(2)])
                G1 = agR1[:, :, :].rearrange("i (r f) n -> i r f n", r=4)

                def head_norm(pq, gc, dstT, B_dst, pair, gcol):
                    s_ = sqh.next()
                    act(s_.t[:, :], pq.t[:, :], AF.Square, [pq.b], [s_.b])
                    sp_ = ss_ps.next()
                    mm(sp_.t[:, :], ONES_BD(), s_.t[:, :], True, True, [CM.b, s_.b], [sp_.b])
                    r_ = rsh.next()
                    rstd_act(r_.t[:, :], sp_.t[:, :], 64, r_.b, sp_.b)
                    o_ = qo.next()
                    stt("dve", o_.t[:, :], pq.t[:, :], G.t[:, gc:gc + 1], r_.t[:, :], ALU.mult, ALU.mult, [pq.b, G.b, r_.b], [o_.b])
                    for j in range(2):
                        dma("sp", dstT[2 * pair + j, 0:64, gcol], o_.t[64 * j:64 * j + 64, :], [o_.b], iw=[B_dst])

                def fox_load(T):
                    r_, lt = T // 8, T % 8
                    a = hn.next()
                    for hf in range(2):
                        cs = slice(hf * 256, (hf + 1) * 256)
                        dma("sp", a.t[:, :, cs], G1[2 * lt + hf, r_, :, :].rearrange("(c p) n -> p c n", p=128), [B_agR1], [a.b])
                    return a

                nxt = fox_load(0)
                for T in range(S // TT):
                    gcol = slice(T * TT, (T + 1) * TT)
                    a = nxt
                    if T + 1 < S // TT:
                        nxt = fox_load(T + 1)
                    for pair in range(2):
                        pq = qk_ps.next()
                        for k in range(8):
                            mm(pq.t[:, :], w_q.t[:, k, pair * 128:(pair + 1) * 128], a.t[:, k, :], k == 0, k == 7, [w_q.b, a.b], [pq.b])
                        head_norm(pq, GC_FQ, Qs, B_Qs, pair, gcol)
                        pk = qk_ps.next()
                        for k in range(8):
                            mm(pk.t[:, :], w_k.t[:, k, pair * 128:(pair + 1) * 128], a.t[:, k, :], k == 0, k == 7, [w_k.b, a.b], [pk.b])
                        head_norm(pk, GC_FK, Ks, B_Ks, pair, gcol)
                    v_ = vt.next()
                    for blk in range(4):
                        pv = v_ps.next()
                        for k in range(8):
                            mm(pv.t[:, 0:256], a.t[:, k, blk * 128:(blk + 1) * 128], w_v.t[:, k, :], k == 0, k == 7, [a.b, w_v.b], [pv.b])
                        act(v_.t[:, blk, :], pv.t[:, 0:256], AF.Copy, [pv.b], [v_.b])
                    for h in range(4):
                        dma("sp", Vs_v[h, :, T * 4:(T + 1) * 4, :], v_.t[:, :, h * 64:(h + 1) * 64], [v_.b], iw=[B_Vs])
                    for k in range(8):
                        mm(f_ps.t[0:4, :], w_f.t[:, k, :], a.t[:, k, :], k == 0, k == 7, [w_f.b, a.b], [f_ps.b])
                    l_ = lf.next()
                    act(l_.t[:, :], f_ps.t[0:4, :], AF.Sigmoid, [f_ps.b, G.b], [l_.b], bias=G.t[0:4, GC_BF:GC_BF + 1])
                    act(l_.t[:, :], l_.t[:, :], AF.Ln, [l_.b], [l_.b])
                    dma("sp", Lf[:, gcol], l_.t[:, :], [l_.b], iw=[B_Lf])
                L = sbuf(st, "L", [128, TT], F32)
                one = sbuf(st, "one", [128, TT], F32)
                sc = sbuf(st, "sc", [128, TT], F32)
                hi = sbuf(st, "hi", [128, TT], BF16)
                lo = sbuf(st, "lo", [128, TT], BF16)
                onb = sbuf(st, "onb", [128, TT], BF16)
                off = sbuf(st, "off", [128, 1], F32)
                dma("sp", L.t[:, :], Lf[:, :].rearrange("h (t n) -> (h t) n", n=TT), [B_Lf], [L.b])
                memset("dve", one.t[:, :], 1.0, [one.b])
                memset("pool", onb.t[:, :], 1.0, [onb.b])
                P.op("dve", lambda e: e.tensor_tensor_scan(out=sc.t[:, :], data0=one.t[:, :], data1=L.t[:, :], initial=0.0,
                                                           op0=ALU.mult, op1=ALU.add), [one.b, L.b], [sc.b])
                pq = ps[0]
                mm(pq.t[:, 0:1], TRI.t[:, :], sc.t[:, TT - 1:TT], True, True, [TRI.b, sc.b], [pq.b])
                cp("dve", off.t[:, :], pq.t[:, 0:1], [pq.b], [off.b])
                dma("sp", Os[:, :], off.t[:, :], [off.b], iw=[B_Os])
                cp("dve", hi.t[:, :], sc.t[:, :], [sc.b], [hi.b])
                tt("dve", lo.t[:, :], sc.t[:, :], hi.t[:, :], ALU.subtract, [sc.b, hi.b], [lo.b])
                for h in range(4):
                    dma("sp", Qs[h, 64, :].rearrange("(t n) -> t n", n=TT), hi.t[32 * h:32 * h + 32, :], [hi.b], iw=[B_Qs])
                    dma("sp", Qs[h, 65, :].rearrange("(t n) -> t n", n=TT), lo.t[32 * h:32 * h + 32, :], [lo.b], iw=[B_Qs])
                for h in range(4):
                    dma("sp", Ks[h, 64:66, :].rearrange("a (t n) -> (a t) n", n=TT), onb.t[0:64, :], [onb.b], iw=[B_Ks])
                ts("dve", sc.t[:, :], sc.t[:, :], off.t[:, 0:1], None, ALU.add, None, [sc.b, off.b], [sc.b])
                dma("sp", Cs[:, :].rearrange("h (t n) -> (h t) n", n=TT), sc.t[:, :], [sc.b], iw=[B_Cs])
            P.flush()

    def stage_attn(li):
        dk = 96 if li == 0 else 66
        wo_d = mla_wo if li == 0 else fox_wo
        with ExitStack() as st:
            Kt = sbuf(st, "Kt", [96, S], BF16)
            Vt = sbuf(st, "Vt", [128, 128, 128], BF16)
            Ot = sbuf(st, "Ot", [128, 2, S], BF16)
            Qt = Rot([sbuf(st, "Qt%d" % i, [96, TT], BF16) for i in range(4)])
            LOOKAHEAD = 2
            Pt = Rot([sbuf(st, "Pt%d" % i, [128, TT], BF16) for i in range(4)])
            rsum = Rot([sbuf(st, "rsum%d" % i, [128, TT], F32) for i in range(2)])
            wo = sbuf(st, "wo", [128, 2, D], BF16)
            pt = Rot([sbuf(st, "pt%d" % i, [128, 4, TT], F32) for i in range(7)])
            s_ps = Rot([ps[0], ps[1], ps[2], ps[7]])
            o_ps = Rot([ps[3], ps[4]])
            p_ps = Rot([ps[5], ps[6]])
            dma("pool", wo.t[:, :, :], wo_d[:, :, :], (), [wo.b])
            if li == 1:
                ck = sbuf(st, "ck", [128, 128], F32)
                Rb = sbuf(st, "Rb", [128, 32], F32)
                bT = Rot([sbuf(st, "bT%d" % i, [128, 128], F32) for i in range(2)])
            Vs_v = Vs[:, :, :].rearrange("h p (b d) -> h p b d", d=64)
            pS_v = pS[:, :, :].rearrange("q (r f) n -> q r f n", r=4)
            b3_done = [0] * 8

            def b3(T):
                q = T % 8
                for mh in range(2):
                    x_ = pt.next()
                    for m4 in range(4):
                        m = mh * 4 + m4
                        pp = p_ps.next()
                        for pr in range(2):
                            mm(pp.t[:, :], wo.t[:, pr, m * 128:(m + 1) * 128], Ot.t[:, pr, T * TT:(T + 1) * TT], pr == 0, pr == 1,
                               [wo.b, Ot.b], [pp.b])
                        cp("dve", x_.t[:, m4, :], pp.t[:, :], [pp.b], [x_.b])
                    dma("pool", pS_v[q, T // 8, mh * 512:(mh + 1) * 512, :].rearrange("(c p) n -> p c n", p=128),
                        x_.t[:, :, :], [x_.b], iw=[B_pSq[q]])
                b3_done[q] += 1
                if b3_done[q] == 4 and not OPT.get('nors'):
                    P.op("pool", lambda e: e.collective_compute("ReduceScatter", ALU.add, replica_groups=RG,
                                                                ins=[pS[q, :, :]], outs=[pR[q, :, :]]),
                         [B_pSq[q]], [B_pRq[q]], cc=True)

            for h in range(4):
                odd = h % 2
                pair = h // 2
                vo = 64 if odd else 0
                so = 0 if odd else 64
                dma("sp", Kt.t[0:dk, :], Ks[h, 0:dk, :], [B_Ks], [Kt.b])
                memset("pool", Vt.t[:, :, so:so + 64], 1.0, [Vt.b])
                dma("sp", Vt.t[:, :, vo:vo + 64], Vs_v[h, :, :, :], [B_Vs], [Vt.b])
                if li == 1:
                    dma("sp", ck.t[:, :], Cs[h, :].rearrange("(b p) -> p b", p=128), [B_Cs], [ck.b], slow=True)
                    dma("sp", Rb.t[:, :], Os[h * 32:(h + 1) * 32, :].rearrange("t o -> o t").partition_broadcast(128),
                        [B_Os], [Rb.b], slow=True)
                    ts("dve", ck.t[:, :], ck.t[:, :], -1.0, None, ALU.mult, None, [ck.b], [ck.b])
                nq = OPT.get('nqg', S // TT)
                if h == 3 and nq == S // TT:
                    Torder = [8 * r + j for j in range(8) for r in range(4)]
                else:
                    Torder = list(range(nq))
                tiles = [(T, i) for T in Torder for i in range(4 * T + 4)]
                b3_pend = []
                st_T = {}
                pend = []

                def issue_pv(T, i, p_, c0):
                    q_, b_, ob = st_T[T]
                    nblk = 4 * T + 4
                    mm(ob.t[:, c0:TT], Vt.t[:, i, :], p_.t[:, c0:TT], i == 0, i == nblk - 1, [Vt.b, p_.b], [ob.b])
                    if i == nblk - 1:
                        r_ = rsum.next()
                        act(r_.t[vo:vo + 64, :], ob.t[so:so + 64, :], AF.Copy, [ob.b], [r_.b])
                        recip(r_.t[vo:vo + 64, :], r_.t[vo:vo + 64, :], [r_.b], [r_.b])
                        tt("dve", Ot.t[vo:vo + 64, pair, T * TT:(T + 1) * TT], ob.t[vo:vo + 64, :], r_.t[vo:vo + 64, :], ALU.mult,
                           [ob.b, r_.b], [Ot.b])
                        del st_T[T]
                        if h == 3:
                            b3_pend.append(T)
                            if len(b3_pend) > 1:
                                b3(b3_pend.pop(0))

                for (T, i) in tiles:
                    if i == 0:
                        q_ = Qt.next()
                        dma("sp", q_.t[0:dk, :], Qs[h, 0:dk, T * TT:(T + 1) * TT], [B_Qs], [q_.b])
                        b_ = None
                        if li == 1:
                            b_ = bT.next()
                            nblk = 4 * T + 4
                            ts("dve", b_.t[:, 0:nblk], ck.t[:, 0:nblk], Rb.t[:, T:T + 1], None, ALU.add, None, [ck.b, Rb.b], [b_.b])
                        st_T[T] = (q_, b_, o_ps.next())
                    q_, b_, ob = st_T[T]
                    m = i - 4 * T
                    c0 = 128 * m if m > 0 else 0
                    sb_ = s_ps.next()
                    mm(sb_.t[:, c0:TT], Kt.t[0:dk, i * 128:(i + 1) * 128], q_.t[0:dk, c0:TT], True, m < 0, [Kt.b, q_.b], [sb_.b])
                    if m >= 0:
                        mm(sb_.t[:, c0:c0 + 128], IDENT(), NEGTRI(), False, True, [CM.b], [sb_.b])
                    p_ = Pt.next()
                    if li == 1:
                        act(p_.t[:, c0:TT], sb_.t[:, c0:TT], AF.Exp, [sb_.b, b_.b], [p_.b], bias=b_.t[:, i:i + 1])
                    else:
                        act(p_.t[:, c0:TT], sb_.t[:, c0:TT], AF.Exp, [sb_.b], [p_.b])
                    pend.append((T, i, p_, c0))
                    if len(pend) > LOOKAHEAD:
                        issue_pv(*pend.pop(0))
                while pend:
                    issue_pv(*pend.pop(0))
                if h == 3:
                    while b3_pend:
                        b3(b3_pend.pop(0))
            P.flush()

    def dump():
        tens = {"hS": hS, "agR0": agR0, "agR1": agR1, "Qs": Qs, "Ks": Ks, "Vs": Vs, "pR": pR, "Cs": Cs, "Lf": Lf, "pS": pS}
        for nm in dbg:
            t = tens[nm]
            ext = nc.dram_tensor("dbg_" + nm, list(t.shape), t.dtype, kind="ExternalOutput")
            if len(t.shape) == 3:
                dma("sp", ext[:, :, :], t[:, :, :], (), ())
            elif nm == "pS":
                dma("sp", ext[0, 0:D, :], t[0, 0:D, :], (), ())
            elif nm == "hS":
                nn = OPT.get('ntiles', 8) * TT
                dma("sp", ext[:, 0:nn], t[:, 0:nn], (), ())
            else:
                dma("sp", ext[:, :], t[:, :], (), ())
        P.flush()

    consts_load()
    if OPT.get('fox_first'):
        stage_tok(3)
        stage_proj(1)
        if stop_after != "proj1":
            stage_attn(1)
        P.flush()
        dump()
        gstack.close()
        return nc
    stage_tok(0)
    if stop_after != "tok0":
        stage_proj(0)
        if stop_after != "proj0":
            stage_attn(0)
            if stop_after != "attn0":
                stage_tok(1)
                stage_proj(1)
                stage_attn(1)
                stage_tok(2)
    P.flush()
    dump()
    gstack.close()
    return nc


GC_FFN1 = 0
GC_MIX = 16
GC_FFN2 = 32
GC_PLE = 48
GC_QLAT = 64
GC_KVLAT = 68
GC_GQ = 70
GC_GK = 71
GC_FQ = 72
GC_FK = 73
GC_INVF = 74
GC_SGN = 75
GC_BF = 76
GC_EPS = 77
NG = 78
OPT = {}


def _chunk_cols(v):
    return np.ascontiguousarray(v.reshape(-1, 128).T)


def prep_inputs(inp):
    f32 = np.float32
    x = inp["x"]
    p = inp["p"]
    positions = inp["positions"]
    common = {}
    for nm, key in (("w1in", "ffn1_w_in"), ("w2in", "ffn2_w_in")):
        w = inp[key].reshape(2, 8, 128, 2, NC_FF, 128)
        w = w.transpose(0, 4, 2, 3, 1, 5)
        common[nm] = np.ascontiguousarray(w).reshape(2, NC_FF, 128, 2048)
    for nm, key in (("w1out", "ffn1_w_out"), ("w2out", "ffn2_w_out")):
        w = inp[key].reshape(2, NC_FF, 128, D).transpose(0, 2, 1, 3)
        common[nm] = np.ascontiguousarray(w)
    common["wgate"] = np.ascontiguousarray(inp["ple_w_gate"].reshape(2, 8, 128, D).transpose(0, 2, 1, 3))
    common["wproj"] = np.ascontiguousarray(inp["ple_w_proj"].reshape(2, 2, 128, D).transpose(0, 2, 1, 3))
    common["mla_win"] = np.ascontiguousarray(inp["mla_w_in"][0].reshape(8, 128, 800).transpose(1, 0, 2))
    cm = np.zeros((6, 128, 128), f32)
    cm[0] = 1.0
    cm[1, :96, :] = 1.0
    cm[2, :64, :64] = 1.0
    cm[2, 64:, 64:] = 1.0
    cm[3] = np.eye(128, dtype=f32)
    kk, qq = np.meshgrid(np.arange(128), np.arange(128), indexing="ij")
    cm[4] = np.where(kk > qq, -30000.0, 0.0)
    hh, tt_ = np.arange(128) // 32, np.arange(128) % 32
    cm[5] = ((hh[:, None] == hh[None, :]) & (tt_[:, None] < tt_[None, :])).astype(f32)
    common["cmats"] = cm
    sel = np.zeros((32, 128), f32)
    for i in range(32):
        sel[i, 64 + i] = 1.0
    for j in range(32):
        sel[(j + 16) % 32, 96 + j] = 1.0
    common["selm"] = sel
    swap = np.concatenate([np.arange(16, 32), np.arange(0, 16)])
    w_uq = inp["mla_w_uq"][0].reshape(512, 16, 96)
    w_uq_ext = np.concatenate([w_uq, w_uq[:, :, 64 + swap]], axis=2)
    w_ukv = inp["mla_w_ukv"][0].reshape(256, 16, 128)
    w_k_ext = np.concatenate([w_ukv[:, :, :64], np.zeros((256, 16, 64), f32)], axis=2)
    w_v = w_ukv[:, :, 64:]
    gq = inp["mla_g_qn"][0]
    gk = inp["mla_g_kn"][0]
    gq_ext = np.concatenate([gq, gq[64 + swap]])
    gk_ext = np.concatenate([gk, gk[64 + swap]])
    inv_freq = (10000.0 ** (-np.arange(0, 32, 2, dtype=f32) / 32)).astype(f32)
    fox_in = inp["fox_w_in"][0]
    fq = fox_in[:, 0:1024].reshape(D, 16, 64)
    fk = fox_in[:, 1024:2048].reshape(D, 16, 64)
    fv = fox_in[:, 2048:3072].reshape(D, 16, 64)
    ff = fox_in[:, 3072:3088]
    in_maps = []
    for c in range(8):
        b, r = c // 4, c % 4
        hs = slice(4 * r, 4 * r + 4)
        m = dict(common)
        m["xT"] = np.ascontiguousarray(x[b, r * NT:(r + 1) * NT, :].T)
        m["pT"] = np.ascontiguousarray(p[:, b, r * NT:(r + 1) * NT, :].transpose(0, 2, 1))
        m["pos"] = np.ascontiguousarray(np.broadcast_to(positions[b].astype(np.int32)[None, :], (32, S)))
        g = np.zeros((128, NG), f32)
        for l in range(2):
            g[:, GC_FFN1 + 8 * l:GC_FFN1 + 8 * l + 8] = _chunk_cols(inp["g_ffn1"][l])
            g[:, GC_MIX + 8 * l:GC_MIX + 8 * l + 8] = _chunk_cols(inp["g_mix"][l])
            g[:, GC_FFN2 + 8 * l:GC_FFN2 + 8 * l + 8] = _chunk_cols(inp["g_ffn2"][l])
            g[:, GC_PLE + 8 * l:GC_PLE + 8 * l + 8] = _chunk_cols(inp["g_ple"][l])
        g[:, GC_QLAT:GC_QLAT + 4] = _chunk_cols(inp["mla_g_q_lat"][0])
        g[:, GC_KVLAT:GC_KVLAT + 2] = _chunk_cols(inp["mla_g_kv_lat"][0])
        g[:, GC_GQ] = gq_ext
        g[:, GC_GK] = gk_ext
        g[:, GC_FQ] = np.tile(inp["fox_g_qn"][0], 2)
        g[:, GC_FK] = np.tile(inp["fox_g_kn"][0], 2)
        g[64:96, GC_INVF] = np.tile(inv_freq, 2)
        g[64:80, GC_SGN] = -1.0
        g[80:96, GC_SGN] = 1.0
        g[0:4, GC_BF] = inp["fox_b_f"][0][hs]
        g[:, GC_EPS] = EPS
        m["gvec"] = g
        m["wuq"] = np.ascontiguousarray(w_uq_ext[:, hs, :].reshape(4, 128, 512).transpose(1, 0, 2))
        m["wukvk"] = np.ascontiguousarray(w_k_ext[:, hs, :].reshape(2, 128, 512).transpose(1, 0, 2))
        m["wukvv"] = np.ascontiguousarray(w_v[:, hs, :].reshape(2, 128, 256).transpose(1, 0, 2))
        m["mla_wo"] = np.ascontiguousarray(inp["mla_w_o"][0][256 * r:256 * (r + 1), :].reshape(2, 128, D).transpose(1, 0, 2))
        m["fwq"] = np.ascontiguousarray(fq[:, hs, :].reshape(8, 128, 256).transpose(1, 0, 2))
        m["fwk"] = np.ascontiguousarray(fk[:, hs, :].reshape(8, 128, 256).transpose(1, 0, 2))
        m["fwv"] = np.ascontiguousarray(fv[:, hs, :].reshape(8, 128, 256).transpose(1, 0, 2))
        m["fwf"] = np.ascontiguousarray(ff[:, hs].reshape(8, 128, 4).transpose(1, 0, 2))
        m["fox_wo"] = np.ascontiguousarray(inp["fox_w_o"][0][256 * r:256 * (r + 1), :].reshape(2, 128, D).transpose(1, 0, 2))
        in_maps.append(m)
    return in_maps


_NC_CACHE = {}


def kernel(**inputs):
    inp = {k: np.asarray(v) for k, v in inputs.items()}
    in_maps = prep_inputs(inp)
    if "nc" not in _NC_CACHE:
        _NC_CACHE["nc"] = build_program()
    nc = _NC_CACHE["nc"]
    res = run_bass_kernel_spmd(nc, in_maps, core_ids=list(range(8)))
    out = np.empty((2, S, D), np.float32)
    for c in range(8):
        b, r = c // 4, c % 4
        out[b, r * NT:(r + 1) * NT, :] = np.asarray(res.results[c]["yT"]).T
    return out
```

```python
import numpy as np
from contextlib import ExitStack
import concourse.bass as bass
import concourse.mybir as mybir
from concourse.bass_utils import run_bass_kernel_spmd

F32 = mybir.dt.float32
BF16 = mybir.dt.bfloat16
I32 = mybir.dt.int32
AF = mybir.ActivationFunctionType
ALU = mybir.AluOpType

D = 1024
S = 16384
NT = 4096
TT = 512
DFF = 2816
NC_FF = DFF // 128
EPS = 1e-6
PI = float(np.pi)
C1 = 6.28125
C2 = float(2 * np.pi - 6.28125)


class Buf:
    __slots__ = ("name", "last_w", "rd", "rd_dma", "wr_all")

    def __init__(self, name=""):
        self.name = name
        self.last_w = None
        self.rd = {}
        self.rd_dma = []
        self.wr_all = []

    def reset(self):
        self.last_w = None
        self.rd = {}
        self.rd_dma = []
        self.wr_all = []


class Op:
    __slots__ = ("eng", "fn", "deps", "dma", "sig", "needed", "idx", "cc", "prev")

    def __init__(self, eng, fn, dma, cc):
        self.eng = eng
        self.fn = fn
        self.deps = []
        self.dma = dma
        self.cc = cc
        self.sig = None
        self.needed = False
        self.prev = None


class Prog:
    ENGS = ("pe", "act", "dve", "pool", "sp")
    NDMASEM = 6
    SEM_ROLL = 24000

    def __init__(self, nc, stack):
        self.nc = nc
        self.stack = stack
        self.ops = []
        self.start = 0
        self.bufs = []
        self.sems = {}
        self.cnt = {}
        self.nsem = 0
        self.dma_pool = {}
        self.dma_n = {}
        self.cc_sem = None
        self.cc_n = 0
        self.out_sigs = []

    def buf(self, name=""):
        b = Buf(name)
        self.bufs.append(b)
        return b

    def newsem(self, tag):
        self.nsem += 1
        return self.stack.enter_context(self.nc.semaphore("s_%s_%d" % (tag, self.nsem)))

    def op(self, eng, fn, reads=(), writes=(), dma=False, cc=False, iwrites=()):
        o = Op(eng, fn, dma, cc)
        o.idx = len(self.ops)
        special = dma or cc
        deps = set()
        for b in reads:
            if b.last_w is not None:
                deps.add(b.last_w)
            deps.update(b.wr_all)
        for b in writes:
            if b.last_w is not None:
                deps.add(b.last_w)
            deps.update(b.wr_all)
            for j in b.rd.values():
                deps.add(j)
            for j in b.rd_dma:
                deps.add(j)
        for b in iwrites:
            if b.last_w is not None:
                deps.add(b.last_w)
        best = {}
        for j in deps:
            p = self.ops[j]
            if p.dma or p.cc:
                o.deps.append(j)
                continue
            if p.eng == eng and not special:
                if eng == "pe":
                    continue
                if not any(b.last_w == j for b in reads):
                    continue
            if best.get(p.eng, -1) < j:
                best[p.eng] = j
        for j in best.values():
            o.deps.append(j)
            self.ops[j].needed = True
        for b in reads:
            if special:
                b.rd_dma.append(o.idx)
            else:
                b.rd[eng] = o.idx
        for b in writes:
            b.last_w = o.idx
            b.rd = {}
            b.rd_dma = []
            b.wr_all = []
        for b in iwrites:
            b.wr_all.append(o.idx)
        self.ops.append(o)
        return o

    def flush(self):
        nc = self.nc
        pend = self.ops[self.start:]
        self.start = len(self.ops)
        if not pend:
            return
        streams = {e: [] for e in self.ENGS}
        for e in self.ENGS:
            if e not in self.sems or self.cnt[e] >= self.SEM_ROLL:
                self.sems[e] = self.newsem(e)
                self.cnt[e] = 0
        if self.cc_sem is None:
            self.cc_sem = self.newsem("cc")
        last_dma = {}
        for o in pend:
            e = o.eng
            streams[e].append(o)
            if o.dma:
                if e not in self.dma_pool:
                    self.dma_pool[e] = [self.newsem("dma" + e) for _ in range(self.NDMASEM)]
                    self.dma_n[e] = 0
                i = self.dma_n[e]
                self.dma_n[e] += 1
                si = i % self.NDMASEM
                s = self.dma_pool[e][si]
                prev = 16 * (i // self.NDMASEM)
                o.sig = (s, prev + 16)
                o.prev = (s, prev)
                last_dma[(e, si)] = o.sig
            elif o.cc:
                self.cc_n += 1
                o.sig = (self.cc_sem, self.cc_n)
                last_dma[("cc", 0)] = o.sig
            elif o.needed:
                self.cnt[e] += 1
                o.sig = (self.sems[e], self.cnt[e])
        ops = self.ops
        finals = list(last_dma.values())

        def run_stream(e):
            def body(eng):
                known = {}
                for o in streams[e]:
                    waits = [ops[j].sig for j in o.deps]
                    if o.dma and o.prev[1] > 0:
                        waits.append(o.prev)
                    for (s, v) in waits:
                        k = id(s)
                        if known.get(k, 0) >= v:
                            continue
                        known[k] = v
                        eng.wait_ge(s, v)
                    ins = o.fn(eng)
                    if o.sig is not None:
                        ins.then_inc(o.sig[0], 16 if o.dma else 1)
                if e == "sp":
                    for (s, v) in finals:
                        eng.wait_ge(s, v)
            return body

        with nc.Block() as block:
            for e, deco in (("pe", block.tensor), ("act", block.scalar), ("dve", block.vector),
                            ("pool", block.gpsimd), ("sp", block.sync)):
                if streams[e] or e == "sp":
                    deco(run_stream(e))
        for b in self.bufs:
            b.reset()


class Tile:
    def __init__(self, t, b):
        self.t = t
        self.b = b


class Rot:
    def __init__(self, tiles):
        self.tiles = tiles
        self.i = 0

    def next(self):
        t = self.tiles[self.i % len(self.tiles)]
        self.i += 1
        return t


def build_program(stop_after=None, dbg=()):
    nc = bass.Bass("TRN2", target_bir_lowering=False)
    gstack = ExitStack()
    P = Prog(nc, gstack)

    def din(name, shape, dt=F32):
        return nc.dram_tensor(name, list(shape), dt, kind="ExternalInput")

    def dint(name, shape, dt):
        return nc.dram_tensor(name, list(shape), dt)

    xT = din("xT", [D, NT])
    pT = din("pT", [2, 256, NT])
    pos = din("pos", [32, S], I32)
    gvec = din("gvec", [128, NG])
    w1in = din("w1in", [2, NC_FF, 128, 2048])
    w2in = din("w2in", [2, NC_FF, 128, 2048])
    w1out = din("w1out", [2, 128, NC_FF, D])
    w2out = din("w2out", [2, 128, NC_FF, D])
    wgate = din("wgate", [2, 128, 8, D])
    wproj = din("wproj", [2, 128, 2, D])
    mla_win = din("mla_win", [128, 8, 800])
    wuq = din("wuq", [128, 4, 512])
    wukvk = din("wukvk", [128, 2, 512])
    wukvv = din("wukvv", [128, 2, 256])
    selm = din("selm", [32, 128])
    mla_wo = din("mla_wo", [128, 2, D])
    fwq = din("fwq", [128, 8, 256])
    fwk = din("fwk", [128, 8, 256])
    fwv = din("fwv", [128, 8, 256])
    fwf = din("fwf", [128, 8, 4])
    fox_wo = din("fox_wo", [128, 2, D])
    cmats = din("cmats", [6, 128, 128])
    yT = nc.dram_tensor("yT", [D, NT], F32, kind="ExternalOutput")

    hS = dint("hS", [D, NT], F32)
    agS0 = dint("agS0", [16, 800, 256], BF16)
    agR0 = dint("agR0", [16, 4 * 800, 256], BF16)
    agS1 = dint("agS1", [16, D, 256], BF16)
    agR1 = dint("agR1", [16, 4 * D, 256], BF16)
    Qs = dint("Qs", [4, 96, S], BF16)
    Ks = dint("Ks", [4, 96, S], BF16)
    Vs = dint("Vs", [4, 128, 128 * 64], BF16)
    Lf = dint("Lf", [4, S], F32)
    Cs = dint("Cs", [4, S], F32)
    Os = dint("Os", [128, 1], F32)
    pS = dint("pS", [8, 4 * D, 512], F32)
    pR = dint("pR", [8, D, 512], F32)
    B_hS, B_agS0, B_agR0, B_agS1, B_agR1 = (P.buf(n) for n in ("hS", "agS0", "agR0", "agS1", "agR1"))
    B_Qs, B_Ks, B_Vs, B_Lf, B_Cs, B_Os, B_pS, B_pR = (P.buf(n) for n in ("Qs", "Ks", "Vs", "Lf", "Cs", "Os", "pS", "pR"))
    B_y = P.buf("y")
    B_pRq = [P.buf("pR%d" % i) for i in range(8)]
    B_pSq = [P.buf("pS%d" % i) for i in range(8)]
    RG = [[0, 1, 2, 3], [4, 5, 6, 7]]

    def dma(eng, out, in_, reads=(), writes=(), slow=False, iw=()):
        if slow:
            return P.op(eng, lambda e: e.dma_start(out=out, in_=in_, allow_slow_non_contiguous=True), reads, writes, dma=True, iwrites=iw)
        return P.op(eng, lambda e: e.dma_start(out=out, in_=in_), reads, writes, dma=True, iwrites=iw)

    def mm(out, lhsT, rhs, start, stop, reads, writes):
        return P.op("pe", lambda e: e.matmul(out, lhsT=lhsT, rhs=rhs, start=start, stop=stop, skip_group_check=True), reads, writes)

    def act(out, in_, func, reads, writes, bias=None, scale=1.0):
        if bias is None:
            return P.op("act", lambda e: e.activation(out=out, in_=in_, func=func, scale=scale), reads, writes)
        return P.op("act", lambda e: e.activation(out=out, in_=in_, func=func, bias=bias, scale=scale), reads, writes)

    def tt(eng, out, in0, in1, op, reads, writes):
        return P.op(eng, lambda e: e.tensor_tensor(out=out, in0=in0, in1=in1, op=op), reads, writes)

    def ts(eng, out, in0, s1, s2, op0, op1, reads, writes):
        if s2 is None:
            return P.op(eng, lambda e: e.tensor_scalar(out=out, in0=in0, scalar1=s1, scalar2=None, op0=op0), reads, writes)
        return P.op(eng, lambda e: e.tensor_scalar(out=out, in0=in0, scalar1=s1, scalar2=s2, op0=op0, op1=op1), reads, writes)

    def stt(eng, out, in0, scalar, in1, op0, op1, reads, writes):
        return P.op(eng, lambda e: e.scalar_tensor_tensor(out=out, in0=in0, scalar=scalar, in1=in1, op0=op0, op1=op1), reads, writes)

    def cp(eng, out, in_, reads, writes):
        return P.op(eng, lambda e: e.tensor_copy(out=out, in_=in_), reads, writes)

    def recip(out, in_, reads, writes):
        return P.op("dve", lambda e: e.reciprocal(out=out, in_=in_), reads, writes)

    def memset(eng, ap, val, writes):
        return P.op(eng, lambda e: e.memset(ap, val), (), writes)

    def allgather(src, dst, idx, bsrc, bdst):
        if OPT.get('nocc'):
            return
        P.op("pool", lambda e: e.collective_compute("AllGather", ALU.bypass, replica_groups=RG,
                                                    ins=[src[idx, :, :]], outs=[dst[idx, :, :]]),
             [bsrc], (), cc=True, iwrites=[bdst])

    uid = [0]

    def sbuf(stack, name, shape, dt):
        uid[0] += 1
        name = "%s_%d" % (name, uid[0])
        return Tile(stack.enter_context(nc.sbuf_tensor(name, list(shape), dt)), P.buf(name))

    def psum(stack, name):
        return Tile(stack.enter_context(nc.psum_tensor(name, [128, 512], F32)), P.buf(name))

    G = sbuf(gstack, "gv", [128, NG], F32)
    CM = sbuf(gstack, "cm", [128, 5, 128], BF16)
    TRI = sbuf(gstack, "tri", [128, 128], F32)
    ps = [psum(gstack, "ps%d" % i) for i in range(8)]

    def consts_load():
        dma("sp", G.t[:, :], gvec[:, :], (), [G.b])
        for i in range(5):
            dma("pool", CM.t[:, i, :], cmats[i, :, :], (), [CM.b])
        dma("sp", TRI.t[:, :], cmats[5, :, :], (), [TRI.b])
        ts("dve", G.t[:, GC_GQ:GC_GQ + 1], G.t[:, GC_GQ:GC_GQ + 1], float(96 ** -0.5), None, ALU.mult, None, [G.b], [G.b])
        ts("dve", G.t[:, GC_FQ:GC_FQ + 1], G.t[:, GC_FQ:GC_FQ + 1], float(64 ** -0.5), None, ALU.mult, None, [G.b], [G.b])

    ONES_ALL = lambda: CM.t[:, 0, :]
    ONES96 = lambda: CM.t[:, 1, :]
    ONES_BD = lambda: CM.t[:, 2, :]
    IDENT = lambda: CM.t[:, 3, :]
    NEGTRI = lambda: CM.t[:, 4, :]
    EPSC = lambda p0, p1: G.t[p0:p1, GC_EPS:GC_EPS + 1]

    def rmsnorm(src, nch, nfeat, gc0, out, sq, ssps, rs):
        tt("dve", sq.t[:, 0:nch, :], src.t[:, 0:nch, :], src.t[:, 0:nch, :], ALU.mult, [src.b], [sq.b])
        for c in range(nch):
            mm(ssps.t[:, :], ONES_ALL(), sq.t[:, c, :], c == 0, c == nch - 1, [CM.b, sq.b], [ssps.b])
        act(rs.t[:, :], ssps.t[:, :], AF.Sqrt, [ssps.b, G.b], [rs.b], bias=EPSC(0, 128), scale=1.0 / nfeat)
        recip(rs.t[:, :], rs.t[:, :], [rs.b], [rs.b])
        for c in range(nch):
            stt("dve", out.t[:, c, :], src.t[:, c, :], G.t[:, gc0 + c:gc0 + c + 1], rs.t[:, :], ALU.mult, ALU.mult,
                [src.b, G.b, rs.b], [out.b])

    def rstd_act(rs_ap, ss_ap, nfeat, rb, sb_, p0=0, p1=128):
        act(rs_ap, ss_ap, AF.Ln, [sb_, G.b], [rb], bias=EPSC(p0, p1), scale=1.0 / nfeat)
        act(rs_ap, rs_ap, AF.Exp, [rb], [rb], scale=-0.5)

    def stage_tok(stage):
        NH = 2
        ST = NH * TT
        with ExitStack() as st:
            hT_slots = [sbuf(st, "hT%d" % i, [128, 8, ST], F32) for i in range(2)]
            hb_slots = [[P.buf("hA%d" % i), P.buf("hB%d" % i)] for i in range(2)]
            hT = hT_slots[0]
            hb = hb_slots[0]
            xn = sbuf(st, "xn", [128, 8, ST], BF16)
            sq = sbuf(st, "sq", [128, 8, TT], BF16)
            aT = sbuf(st, "aT", [128, NC_FF, ST], BF16)
            rs = sbuf(st, "rs", [128, TT], F32)
            sg = Rot([sbuf(st, "sg%d" % i, [128, TT], F32) for i in range(2)])
            win = Rot([sbuf(st, "win%d" % i, [128, 2048], BF16) for i in range(3 if stage == 0 else 4)])
            wom = Rot([sbuf(st, "wom%d" % i, [128, NC_FF, 128], BF16) for i in range(4)])
            ss_ps = ps[0]
            g_ps = Rot([ps[1], ps[2]])
            u_ps = Rot([ps[3], ps[4]])
            o_ps = Rot([ps[5], ps[6]])
            if stage in (1, 2):
                wg = sbuf(st, "wg", [128, 8, D], BF16)
                wp = sbuf(st, "wp", [128, 2, D], BF16)
                ptl = sbuf(st, "ptl", [128, 2, ST], BF16)
                li = stage - 1

                def rs_op(q):
                    P.op("pool", lambda e: e.collective_compute("ReduceScatter", ALU.add, replica_groups=RG,
                                                                ins=[pS[q, :, :]], outs=[pR[q, :, :]]),
                         (), [B_pRq[q]], cc=True)
                dma("pool", wg.t[:, :, :], wgate[li, :, :, :], (), [wg.b])
                dma("pool", wp.t[:, :, :], wproj[li, :, :, :], (), [wp.b])
            if stage == 0:
                mw = sbuf(st, "mw", [128, 8, 800], BF16)
                dma("pool", mw.t[:, :, :], mla_win[:, :, :], (), [mw.b])
                zf = sbuf(st, "zf", [128, 6, TT], F32)
                zn = sbuf(st, "zn", [128, 6, TT], BF16)
                kpe = sbuf(st, "kpe", [32, TT], BF16)

            xb = [P.buf("xA"), P.buf("xB")]
            ab = [P.buf("aA"), P.buf("aB")]

            def norm_sq(hf):
                cs = slice(hf * TT, (hf + 1) * TT)
                tt("dve", sq.t[:, :, :], hT.t[:, :, cs], hT.t[:, :, cs], ALU.mult, [hb[hf]], [sq.b])

            def norm_fin(hf, gc0):
                cs = slice(hf * TT, (hf + 1) * TT)
                for c in range(8):
                    mm(ss_ps.t[:, :], ONES_ALL(), sq.t[:, c, :], c == 0, c == 7, [CM.b, sq.b], [ss_ps.b])
                rstd_act(rs.t[:, :], ss_ps.t[:, :], D, rs.b, ss_ps.b)
                for c in range(8):
                    stt("dve", xn.t[:, c, cs], hT.t[:, c, cs], G.t[:, gc0 + c:gc0 + c + 1], rs.t[:, :], ALU.mult, ALU.mult,
                        [hb[hf], G.b, rs.b], [xb[hf]])

            def norm_half(hf, gc0):
                norm_sq(hf)
                norm_fin(hf, gc0)

            def norm_h(gc0):
                for hf in range(NH):
                    norm_half(hf, gc0)

            def ffn(li, win_d, wout_d, next_gc):
                wo_pref = []

                def wo_load(m):
                    wq = wom.next()
                    dma("pool", wq.t[:, :, :], wout_d[li, :, :, m * 128:(m + 1) * 128], (), [wq.b])
                    wo_pref.append(wq)

                for c in range(NC_FF):
                    w = win.next()
                    dma("pool", w.t[:, :], win_d[li, c, :, :], (), [w.b])
                    if c in (8, 12, 16, 20):
                        wo_load(len(wo_pref))
                    for hf in range(NH):
                        cs = slice(hf * TT, (hf + 1) * TT)
                        gp = g_ps.next()
                        up = u_ps.next()
                        for half, pp in ((0, gp), (1, up)):
                            for k in range(8):
                                mm(pp.t[:, :], w.t[:, (half * 8 + k) * 128:(half * 8 + k + 1) * 128], xn.t[:, k, cs],
                                   k == 0, k == 7, [w.b, xb[hf]], [pp.b])
                        s = sg.next()
                        act(s.t[:, :], gp.t[:, :], AF.Silu, [gp.b], [s.b])
                        tt("dve", aT.t[:, c, cs], s.t[:, :], up.t[:, :], ALU.mult, [s.b, up.b], [ab[hf]])
                for g4 in range(2):
                    for hf in range(NH):
                        cs = slice(hf * TT, (hf + 1) * TT)
                        for m in range(4 * g4, 4 * g4 + 4):
                            wq = wo_pref[m]
                            op_ = o_ps.next()
                            for c in range(NC_FF):
                                mm(op_.t[:, :], wq.t[:, c, :], aT.t[:, c, cs], c == 0, c == NC_FF - 1, [wq.b, ab[hf]], [op_.b])
                            stt("dve", hT.t[:, m, cs], op_.t[:, :], 0.5, hT.t[:, m, cs], ALU.mult, ALU.add, [op_.b, hb[hf]], [hb[hf]])
                            if hf == NH - 1 and len(wo_pref) < 8:
                                wo_load(len(wo_pref))
                            if g4 == 1 and next_gc is not None and hf == 1 and m == 5:
                                norm_fin(0, next_gc)
                        if g4 == 1 and next_gc is not None:
                            if hf == 0:
                                norm_sq(0)
                            else:
                                norm_half(1, next_gc)

            def ple(li, t, next_gc):
                dma("pool", ptl.t[:, :, :], pT[li, :, t * ST:(t + 1) * ST].rearrange("(c p) n -> p c n", p=128), (), [ptl.b])
                for hf in range(NH):
                    cs = slice(hf * TT, (hf + 1) * TT)
                    for m in range(8):
                        gp = g_ps.next()
                        up = u_ps.next()
                        for k in range(8):
                            mm(gp.t[:, :], wg.t[:, k, m * 128:(m + 1) * 128], xn.t[:, k, cs], k == 0, k == 7, [wg.b, xb[hf]], [gp.b])
                        for k in range(2):
                            mm(up.t[:, :], wp.t[:, k, m * 128:(m + 1) * 128], ptl.t[:, k, cs], k == 0, k == 1, [wp.b, ptl.b], [up.b])
                        s = sg.next()
                        act(s.t[:, :], gp.t[:, :], AF.Sigmoid, [gp.b], [s.b])
                        tt("dve", s.t[:, :], s.t[:, :], up.t[:, :], ALU.mult, [s.b, up.b], [s.b])
                        tt("dve", hT.t[:, m, cs], hT.t[:, m, cs], s.t[:, :], ALU.add, [hb[hf], s.b], [hb[hf]])
                        if next_gc is not None and hf == 1 and m == 1:
                            norm_fin(0, next_gc)
                    if next_gc is not None:
                        if hf == 0:
                            norm_sq(0)
                        else:
                            norm_half(1, next_gc)

            def mla_pre(t):
                for hf in range(NH):
                    cs = slice(hf * TT, (hf + 1) * TT)
                    for zc in range(7):
                        gp = g_ps.next()
                        mcols = 128 if zc < 6 else 32
                        for k in range(8):
                            mm(gp.t[0:mcols, :], mw.t[:, k, zc * 128:zc * 128 + mcols], xn.t[:, k, cs], k == 0, k == 7,
                               [mw.b, xb[hf]], [gp.b])
                        if zc < 6:
                            act(zf.t[:, zc, :], gp.t[:, :], AF.Copy, [gp.b], [zf.b])
                        else:
                            act(kpe.t[:, :], gp.t[0:32, :], AF.Copy, [gp.b], [kpe.b])
                    for (c0, nch, nf, gc) in ((0, 4, 512, GC_QLAT), (4, 2, 256, GC_KVLAT)):
                        tt("dve", sq.t[:, c0:c0 + nch, :], zf.t[:, c0:c0 + nch, :], zf.t[:, c0:c0 + nch, :], ALU.mult, [zf.b], [sq.b])
                        for c in range(nch):
                            mm(ss_ps.t[:, :], ONES_ALL(), sq.t[:, c0 + c, :], c == 0, c == nch - 1, [CM.b, sq.b], [ss_ps.b])
                        rstd_act(rs.t[:, :], ss_ps.t[:, :], nf, rs.b, ss_ps.b)
                        for c in range(nch):
                            stt("dve", zn.t[:, c0 + c, :], zf.t[:, c0 + c, :], G.t[:, gc + c:gc + c + 1], rs.t[:, :], ALU.mult, ALU.mult,
                                [zf.b, G.b, rs.b], [zn.b])
                    for h2 in range(2):
                        c2 = slice(h2 * 256, (h2 + 1) * 256)
                        idx = 2 * (NH * t + hf) + h2
                        dma("sp", agS0[idx, 0:768, :].rearrange("(c p) n -> p c n", p=128), zn.t[:, :, c2], [zn.b], iw=[B_agS0])
                        dma("sp", agS0[idx, 768:800, :], kpe.t[:, c2], [kpe.b], iw=[B_agS0])
                        allgather(agS0, agR0, idx, B_agS0, B_agR0)

            ntl = OPT.get('ntiles', NT // ST)

            def prefetch(t):
                tok = slice(t * ST, (t + 1) * ST)
                hT_ = hT_slots[t % 2]
                hb_ = hb_slots[t % 2]
                if stage in (0, 3):
                    dma("sp", hT_.t[:, :, :], xT[:, tok].rearrange("(c p) n -> p c n", p=128), (), hb_)
                else:
                    dma("sp", hT_.t[:, :, :], hS[:, tok].rearrange("(c p) n -> p c n", p=128), [B_hS], hb_)
                    for hf in range(NH):
                        cs = slice(hf * TT, (hf + 1) * TT)
                        P.op("pool", (lambda o_, i_: (lambda e: e.dma_start(out=o_, in_=i_, accum_op=ALU.add)))(
                            hT_.t[:, :, cs], pR[NH * t + hf, :, :].rearrange("(c p) n -> p c n", p=128)),
                            [B_pRq[NH * t + hf], hb_[hf]], [hb_[hf]], dma=True)

            prefetch(0)
            for t in range(ntl):
                tok = slice(t * ST, (t + 1) * ST)
                hT = hT_slots[t % 2]
                hb = hb_slots[t % 2]
                if stage in (1, 2):
                    li = stage - 1
                    norm_h(GC_FFN2 + 8 * li)
                    if stage == 2 and t + 1 < ntl:
                        prefetch(t + 1)
                    ffn(li, w2in, w2out, GC_PLE + 8 * li)
                    ple(li, t, GC_FFN1 + 8 if stage == 1 else None)
                if stage == 0:
                    norm_h(GC_FFN1 + 0)
                    if t + 1 < ntl:
                        prefetch(t + 1)
                    ffn(0, w1in, w1out, GC_MIX + 0)
                    dma("sp", hS[:, tok].rearrange("(c p) n -> p c n", p=128), hT.t[:, :, :], hb, iw=[B_hS])
                    mla_pre(t)
                elif stage in (1, 3):
                    if stage == 3:
                        norm_h(GC_FFN1 + 8)
                    if t + 1 < ntl:
                        prefetch(t + 1)
                    ffn(1, w1in, w1out, GC_MIX + 8)
                    dma("sp", hS[:, tok].rearrange("(c p) n -> p c n", p=128), hT.t[:, :, :], hb, iw=[B_hS])
                    for hf in range(NH):
                        for h2 in range(2):
                            c2 = slice(hf * TT + h2 * 256, hf * TT + (h2 + 1) * 256)
                            idx = 2 * (NH * t + hf) + h2
                            dma("sp", agS1[idx, :, :].rearrange("(c p) n -> p c n", p=128), xn.t[:, :, c2], [xb[hf]], iw=[B_agS1])
                            allgather(agS1, agR1, idx, B_agS1, B_agR1)
                else:
                    dma("sp", yT[:, tok].rearrange("(c p) n -> p c n", p=128), hT.t[:, :, :], hb, iw=[B_y])
            P.flush()

    def stage_proj(li):
        with ExitStack() as st:
            sqh = Rot([sbuf(st, "sqh%d" % i, [128, TT], BF16) for i in range(3)])
            rsh = Rot([sbuf(st, "rsh%d" % i, [128, TT], F32) for i in range(3)])
            vt = Rot([sbuf(st, "vt%d" % i, [128, 4, 256], BF16) for i in range(2)])
            qk_ps = Rot([ps[0], ps[1], ps[2], ps[7]] if li == 0 else [ps[0], ps[1], ps[2]])
            ss_ps = Rot([ps[3], ps[4], ps[5]] if li == 0 else [ps[3], ps[4]])
            v_ps = Rot([ps[6]] if li == 0 else [ps[5], ps[6]])
            f_ps = ps[7]
            Vs_v = Vs[:, :, :].rearrange("h p (b d) -> h p b d", d=64)
            if li == 0:
                w_q = sbuf(st, "w_q", [128, 4, 512], BF16)
                w_k = sbuf(st, "w_k", [128, 2, 512], BF16)
                w_v = sbuf(st, "w_v", [128, 2, 256], BF16)
                sel = sbuf(st, "sel", [32, 128], BF16)
                dma("pool", w_q.t[:, :, :], wuq[:, :, :], (), [w_q.b])
                dma("pool", w_k.t[:, :, :], wukvk[:, :, :], (), [w_k.b])
                dma("pool", w_v.t[:, :, :], wukvv[:, :, :], (), [w_v.b])
                dma("pool", sel.t[:, :], selm[:, :], (), [sel.b])
                zq = Rot([sbuf(st, "zq%d" % i, [128, 4, TT], BF16) for i in range(2)])
                zkv = Rot([sbuf(st, "zkv%d" % i, [128, 2, TT], BF16) for i in range(2)])
                kp = Rot([sbuf(st, "kp%d" % i, [32, TT], BF16) for i in range(2)])
                posi = sbuf(st, "posi", [128, TT], I32)
                ang = sbuf(st, "ang", [128, TT], F32)
                nfl = sbuf(st, "nfl", [128, TT], F32)
                nin = sbuf(st, "nin", [128, TT], I32)
                msk = sbuf(st, "msk", [128, TT], F32)
                cos_tb = [sbuf(st, "cos_t%d" % i, [128, TT], F32) for i in range(2)]
                sin_tb = [sbuf(st, "sin_t%d" % i, [128, TT], F32) for i in range(2)]
                qn = Rot([sbuf(st, "qn%d" % i, [128, TT], F32) for i in range(3)])
                sw = Rot([sbuf(st, "sw%d" % i, [128, TT], F32) for i in range(3)])
                t1 = Rot([sbuf(st, "t1%d" % i, [128, TT], F32) for i in range(3)])
                qo = Rot([sbuf(st, "qo%d" % i, [128, TT], BF16) for i in range(5)])
                R_ = slice(64, 96)

                def rope_tables(T):
                    cos_t = cos_tb[T % 2]
                    sin_t = sin_tb[T % 2]
                    dma("sp", posi.t[R_, :], pos[:, T * TT:(T + 1) * TT], (), [posi.b])
                    cp("dve", ang.t[R_, :], posi.t[R_, :], [posi.b], [ang.b])
                    ts("dve", ang.t[R_, :], ang.t[R_, :], G.t[R_, GC_INVF:GC_INVF + 1], None, ALU.mult, None, [ang.b, G.b], [ang.b])
                    ts("dve", nfl.t[R_, :], ang.t[R_, :], 1.0 / (2 * PI), None, ALU.mult, None, [ang.b], [nfl.b])
                    cp("dve", nin.t[R_, :], nfl.t[R_, :], [nfl.b], [nin.b])
                    cp("dve", nfl.t[R_, :], nin.t[R_, :], [nin.b], [nfl.b])
                    stt("dve", ang.t[R_, :], nfl.t[R_, :], -C1, ang.t[R_, :], ALU.mult, ALU.add, [nfl.b, ang.b], [ang.b])
                    stt("dve", ang.t[R_, :], nfl.t[R_, :], -C2, ang.t[R_, :], ALU.mult, ALU.add, [nfl.b, ang.b], [ang.b])
                    ts("dve", ang.t[R_, :], ang.t[R_, :], -PI, PI, ALU.max, ALU.min, [ang.b], [ang.b])
                    act(sin_t.t[R_, :], ang.t[R_, :], AF.Sin, [ang.b], [sin_t.b])
                    ts("dve", sin_t.t[R_, :], sin_t.t[R_, :], G.t[R_, GC_SGN:GC_SGN + 1], None, ALU.mult, None, [sin_t.b, G.b], [sin_t.b])
                    ts("dve", msk.t[R_, :], ang.t[R_, :], -1.0, None, ALU.mult, None, [ang.b], [msk.b])
                    tt("dve", msk.t[R_, :], msk.t[R_, :], ang.t[R_, :], ALU.max, [msk.b, ang.b], [msk.b])
                    ts("dve", msk.t[R_, :], msk.t[R_, :], -1.0, PI / 2, ALU.mult, ALU.add, [msk.b], [msk.b])
                    act(cos_t.t[R_, :], msk.t[R_, :], AF.Sin, [msk.b], [cos_t.b])

                def hA(pq):
                    s_ = sqh.next()
                    act(s_.t[:, :], pq.t[:, :], AF.Square, [pq.b], [s_.b])
                    sp_ = ss_ps.next()
                    mm(sp_.t[:, :], ONES96(), s_.t[:, :], True, True, [CM.b, s_.b], [sp_.b])
                    return sp_

                def hB(pq, sp_, gc):
                    r_ = rsh.next()
                    rstd_act(r_.t[:, :], sp_.t[:, :], 96, r_.b, sp_.b)
                    o_ = qo.next()
                    stt("dve", o_.t[:, :], pq.t[:, :], G.t[:, gc:gc + 1], r_.t[:, :], ALU.mult, ALU.mult, [pq.b, G.b, r_.b], [o_.b])
                    return o_

                def hC(o_, dst, bdst, T):
                    cos_t = cos_tb[T % 2]
                    sin_t = sin_tb[T % 2]
                    w_ = sw.next()
                    act(w_.t[64:96, :], o_.t[96:128, :], AF.Copy, [o_.b], [w_.b])
                    a_ = t1.next()
                    tt("dve", a_.t[R_, :], o_.t[R_, :], cos_t.t[R_, :], ALU.mult, [o_.b, cos_t.b], [a_.b])
                    tt("dve", w_.t[R_, :], w_.t[R_, :], sin_t.t[R_, :], ALU.mult, [w_.b, sin_t.b], [w_.b])
                    tt("dve", o_.t[R_, :], a_.t[R_, :], w_.t[R_, :], ALU.add, [a_.b, w_.b, o_.b], [o_.b])
                    dma("sp", dst, o_.t[0:96, :], [o_.b], iw=[bdst])

                G0 = agR0[:, :, :].rearrange("i (r f) n -> i r f n", r=4)
                def mla_load(T):
                    r_, lt = T // 8, T % 8
                    a = zq.next()
                    b = zkv.next()
                    c = kp.next()
                    for hf in range(2):
                        cs = slice(hf * 256, (hf + 1) * 256)
                        dma("sp", a.t[:, :, cs], G0[2 * lt + hf, r_, 0:512, :].rearrange("(c p) n -> p c n", p=128), [B_agR0], [a.b])
                        dma("sp", b.t[:, :, cs], G0[2 * lt + hf, r_, 512:768, :].rearrange("(c p) n -> p c n", p=128), [B_agR0], [b.b])
                        dma("sp", c.t[:, cs], G0[2 * lt + hf, r_, 768:800, :], [B_agR0], [c.b])
                    return a, b, c

                NTL = S // TT
                tin = {0: mla_load(0)}
                rope_tables(0)

                def v_work(T):
                    a, b, c = tin[T]
                    v_ = vt.next()
                    for blk in range(4):
                        pv = v_ps.next()
                        for k in range(2):
                            mm(pv.t[:, 0:256], b.t[:, k, blk * 128:(blk + 1) * 128], w_v.t[:, k, :], k == 0, k == 1, [b.b, w_v.b], [pv.b])
                        act(v_.t[:, blk, :], pv.t[:, 0:256], AF.Copy, [pv.b], [v_.b])
                    for h in range(4):
                        dma("sp", Vs_v[h, :, T * 4:(T + 1) * 4, :], v_.t[:, :, h * 64:(h + 1) * 64], [v_.b], iw=[B_Vs])

                jobs = [(T, kind, h) for T in range(NTL) for h in range(4) for kind in ("q", "k")]
                stA, stB = [], []
                for step in range(len(jobs) + 2):
                    if step < len(jobs):
                        T, kind, h = jobs[step]
                        jn = step % 8
                        a, b, c = tin[T]
                        gcol = slice(T * TT, (T + 1) * TT)
                        if jn == 0 and T + 1 < NTL:
                            tin[T + 1] = mla_load(T + 1)
                        pq = qk_ps.next()
                        if kind == "q":
                            for k in range(4):
                                mm(pq.t[:, :], w_q.t[:, k, h * 128:(h + 1) * 128], a.t[:, k, :], k == 0, k == 3, [w_q.b, a.b], [pq.b])
                            info = (GC_GQ, Qs[h, 0:96, gcol], B_Qs, T)
                        else:
                            for k in range(2):
                                mm(pq.t[:, :], w_k.t[:, k, h * 128:(h + 1) * 128], b.t[:, k, :], k == 0, False, [w_k.b, b.b], [pq.b])
                            mm(pq.t[:, :], sel.t[:, :], c.t[:, :], False, True, [sel.b, c.b], [pq.b])
                            info = (GC_GK, Ks[h, 0:96, gcol], B_Ks, T)
                        stA.append((pq, hA(pq), info))
                        if jn == 3 and T + 1 < NTL:
                            rope_tables(T + 1)
                        if jn == 5:
                            v_work(T)
                    if step >= 1 and stA and step - 1 < len(jobs):
                        pq, sp_, info = stA.pop(0)
                        stB.append((hB(pq, sp_, info[0]), info))
                    if step >= 2 and stB:
                        o_, info = stB.pop(0)
                        hC(o_, info[1], info[2], info[3])
            else:
                w_q = sbuf(st, "f_q", [128, 8, 256], BF16)
                w_k = sbuf(st, "f_k", [128, 8, 256], BF16)
                w_v = sbuf(st, "f_v", [128, 8, 256], BF16)
                w_f = sbuf(st, "f_f", [128, 8, 4], BF16)
                dma("pool", w_q.t[:, :, :], fwq[:, :, :], (), [w_q.b])
                dma("pool", w_k.t[:, :, :], fwk[:, :, :], (), [w_k.b])
                dma("pool", w_v.t[:, :, :], fwv[:, :, :], (), [w_v.b])
                dma("pool", w_f.t[:, :, :], fwf[:, :, :], (), [w_f.b])
                hn = Rot([sbuf(st, "hn%d" % i, [128, 8, TT], BF16) for i in range(2)])
                qo = Rot([sbuf(st, "fqo%d" % i, [128, TT], BF16) for i in range(4)])
                lf = Rot([sbuf(st, "lf%d" % i, [4, TT], F32) for i in range(2)])
                G1 = agR1[:, :, :].rearrange("i (r f) n -> i r f n", r=4)

                def head_norm(pq, gc, dstT, B_dst, pair, gcol):
                    s_ = sqh.next()
                    act(s_.t[:, :], pq.t[:, :], AF.Square, [pq.b], [s_.b])
                    sp_ = ss_ps.next()
                    mm(sp_.t[:, :], ONES_BD(), s_.t[:, :], True, True, [CM.b, s_.b], [sp_.b])
                    r_ = rsh.next()
                    rstd_act(r_.t[:, :], sp_.t[:, :], 64, r_.b, sp_.b)
                    o_ = qo.next()
                    stt("dve", o_.t[:, :], pq.t[:, :], G.t[:, gc:gc + 1], r_.t[:, :], ALU.mult, ALU.mult, [pq.b, G.b, r_.b], [o_.b])
                    for j in range(2):
                        dma("sp", dstT[2 * pair + j, 0:64, gcol], o_.t[64 * j:64 * j + 64, :], [o_.b], iw=[B_dst])

                def fox_load(T):
                    r_, lt = T // 8, T % 8
                    a = hn.next()
                    for hf in range(2):
                        cs = slice(hf * 256, (hf + 1) * 256)
                        dma("sp", a.t[:, :, cs], G1[2 * lt + hf, r_, :, :].rearrange("(c p) n -> p c n", p=128), [B_agR1], [a.b])
                    return a

                nxt = fox_load(0)
                for T in range(S // TT):
                    gcol = slice(T * TT, (T + 1) * TT)
                    a = nxt
                    if T + 1 < S // TT:
                        nxt = fox_load(T + 1)
                    for pair in range(2):
                        pq = qk_ps.next()
                        for k in range(8):
                            mm(pq.t[:, :], w_q.t[:, k, pair * 128:(pair + 1) * 128], a.t[:, k, :], k == 0, k == 7, [w_q.b, a.b], [pq.b])
                        head_norm(pq, GC_FQ, Qs, B_Qs, pair, gcol)
                        pk = qk_ps.next()
                        for k in range(8):
                            mm(pk.t[:, :], w_k.t[:, k, pair * 128:(pair + 1) * 128], a.t[:, k, :], k == 0, k == 7, [w_k.b, a.b], [pk.b])
                        head_norm(pk, GC_FK, Ks, B_Ks, pair, gcol)
                    v_ = vt.next()
                    for blk in range(4):
                        pv = v_ps.next()
                        for k in range(8):
                            mm(pv.t[:, 0:256], a.t[:, k, blk * 128:(blk + 1) * 128], w_v.t[:, k, :], k == 0, k == 7, [a.b, w_v.b], [pv.b])
                        act(v_.t[:, blk, :], pv.t[:, 0:256], AF.Copy, [pv.b], [v_.b])
                    for h in range(4):
                        dma("sp", Vs_v[h, :, T * 4:(T + 1) * 4, :], v_.t[:, :, h * 64:(h + 1) * 64], [v_.b], iw=[B_Vs])
                    for k in range(8):
                        mm(f_ps.t[0:4, :], w_f.t[:, k, :], a.t[:, k, :], k == 0, k == 7, [w_f.b, a.b], [f_ps.b])
                    l_ = lf.next()
                    act(l_.t[:, :], f_ps.t[0:4, :], AF.Sigmoid, [f_ps.b, G.b], [l_.b], bias=G.t[0:4, GC_BF:GC_BF + 1])
                    act(l_.t[:, :], l_.t[:, :], AF.Ln, [l_.b], [l_.b])
                    dma("sp", Lf[:, gcol], l_.t[:, :], [l_.b], iw=[B_Lf])
                L = sbuf(st, "L", [128, TT], F32)
                one = sbuf(st, "one", [128, TT], F32)
                sc = sbuf(st, "sc", [128, TT], F32)
                hi = sbuf(st, "hi", [128, TT], BF16)
                lo = sbuf(st, "lo", [128, TT], BF16)
                onb = sbuf(st, "onb", [128, TT], BF16)
                off = sbuf(st, "off", [128, 1], F32)
                dma("sp", L.t[:, :], Lf[:, :].rearrange("h (t n) -> (h t) n", n=TT), [B_Lf], [L.b])
                memset("dve", one.t[:, :], 1.0, [one.b])
                memset("pool", onb.t[:, :], 1.0, [onb.b])
                P.op("dve", lambda e: e.tensor_tensor_scan(out=sc.t[:, :], data0=one.t[:, :], data1=L.t[:, :], initial=0.0,
                                                           op0=ALU.mult, op1=ALU.add), [one.b, L.b], [sc.b])
                pq = ps[0]
                mm(pq.t[:, 0:1], TRI.t[:, :], sc.t[:, TT - 1:TT], True, True, [TRI.b, sc.b], [pq.b])
                cp("dve", off.t[:, :], pq.t[:, 0:1], [pq.b], [off.b])
                dma("sp", Os[:, :], off.t[:, :], [off.b], iw=[B_Os])
                cp("dve", hi.t[:, :], sc.t[:, :], [sc.b], [hi.b])
                tt("dve", lo.t[:, :], sc.t[:, :], hi.t[:, :], ALU.subtract, [sc.b, hi.b], [lo.b])
                for h in range(4):
                    dma("sp", Qs[h, 64, :].rearrange("(t n) -> t n", n=TT), hi.t[32 * h:32 * h + 32, :], [hi.b], iw=[B_Qs])
                    dma("sp", Qs[h, 65, :].rearrange("(t n) -> t n", n=TT), lo.t[32 * h:32 * h + 32, :], [lo.b], iw=[B_Qs])
                for h in range(4):
                    dma("sp", Ks[h, 64:66, :].rearrange("a (t n) -> (a t) n", n=TT), onb.t[0:64, :], [onb.b], iw=[B_Ks])
                ts("dve", sc.t[:, :], sc.t[:, :], off.t[:, 0:1], None, ALU.add, None, [sc.b, off.b], [sc.b])
                dma("sp", Cs[:, :].rearrange("h (t n) -> (h t) n", n=TT), sc.t[:, :], [sc.b], iw=[B_Cs])
            P.flush()

    def stage_attn(li):
        dk = 96 if li == 0 else 66
        wo_d = mla_wo if li == 0 else fox_wo
        with ExitStack() as st:
            Kt = sbuf(st, "Kt", [96, S], BF16)
            Vt = sbuf(st, "Vt", [128, 128, 128], BF16)
            Ot = sbuf(st, "Ot", [128, 2, S], BF16)
            Qt = Rot([sbuf(st, "Qt%d" % i, [96, TT], BF16) for i in range(4)])
            LOOKAHEAD = 2
            Pt = Rot([sbuf(st, "Pt%d" % i, [128, TT], BF16) for i in range(4)])
            rsum = Rot([sbuf(st, "rsum%d" % i, [128, TT], F32) for i in range(2)])
            wo = sbuf(st, "wo", [128, 2, D], BF16)
            pt = Rot([sbuf(st, "pt%d" % i, [128, 4, TT], F32) for i in range(7)])
            s_ps = Rot([ps[0], ps[1], ps[2], ps[7]])
            o_ps = Rot([ps[3], ps[4]])
            p_ps = Rot([ps[5], ps[6]])
            dma("pool", wo.t[:, :, :], wo_d[:, :, :], (), [wo.b])
            if li == 1:
                ck = sbuf(st, "ck", [128, 128], F32)
                Rb = sbuf(st, "Rb", [128, 32], F32)
                bT = Rot([sbuf(st, "bT%d" % i, [128, 128], F32) for i in range(2)])
            Vs_v = Vs[:, :, :].rearrange("h p (b d) -> h p b d", d=64)
            pS_v = pS[:, :, :].rearrange("q (r f) n -> q r f n", r=4)
            b3_done = [0] * 8

            def b3(T):
                q = T % 8
                for mh in range(2):
                    x_ = pt.next()
                    for m4 in range(4):
                        m = mh * 4 + m4
                        pp = p_ps.next()
                        for pr in range(2):
                            mm(pp.t[:, :], wo.t[:, pr, m * 128:(m + 1) * 128], Ot.t[:, pr, T * TT:(T + 1) * TT], pr == 0, pr == 1,
                               [wo.b, Ot.b], [pp.b])
                        cp("dve", x_.t[:, m4, :], pp.t[:, :], [pp.b], [x_.b])
                    dma("pool", pS_v[q, T // 8, mh * 512:(mh + 1) * 512, :].rearrange("(c p) n -> p c n", p=128),
                        x_.t[:, :, :], [x_.b], iw=[B_pSq[q]])
                b3_done[q] += 1
                if b3_done[q] == 4 and not OPT.get('nors'):
                    P.op("pool", lambda e: e.collective_compute("ReduceScatter", ALU.add, replica_groups=RG,
                                                                ins=[pS[q, :, :]], outs=[pR[q, :, :]]),
                         [B_pSq[q]], [B_pRq[q]], cc=True)

            for h in range(4):
                odd = h % 2
                pair = h // 2
                vo = 64 if odd else 0
                so = 0 if odd else 64
                dma("sp", Kt.t[0:dk, :], Ks[h, 0:dk, :], [B_Ks], [Kt.b])
                memset("pool", Vt.t[:, :, so:so + 64], 1.0, [Vt.b])
                dma("sp", Vt.t[:, :, vo:vo + 64], Vs_v[h, :, :, :], [B_Vs], [Vt.b])
                if li == 1:
                    dma("sp", ck.t[:, :], Cs[h, :].rearrange("(b p) -> p b", p=128), [B_Cs], [ck.b], slow=True)
                    dma("sp", Rb.t[:, :], Os[h * 32:(h + 1) * 32, :].rearrange("t o -> o t").partition_broadcast(128),
                        [B_Os], [Rb.b], slow=True)
                    ts("dve", ck.t[:, :], ck.t[:, :], -1.0, None, ALU.mult, None, [ck.b], [ck.b])
                nq = OPT.get('nqg', S // TT)
                if h == 3 and nq == S // TT:
                    Torder = [8 * r + j for j in range(8) for r in range(4)]
                else:
                    Torder = list(range(nq))
                tiles = [(T, i) for T in Torder for i in range(4 * T + 4)]
                b3_pend = []
                st_T = {}
                pend = []

                def issue_pv(T, i, p_, c0):
                    q_, b_, ob = st_T[T]
                    nblk = 4 * T + 4
                    mm(ob.t[:, c0:TT], Vt.t[:, i, :], p_.t[:, c0:TT], i == 0, i == nblk - 1, [Vt.b, p_.b], [ob.b])
                    if i == nblk - 1:
                        r_ = rsum.next()
                        act(r_.t[vo:vo + 64, :], ob.t[so:so + 64, :], AF.Copy, [ob.b], [r_.b])
                        recip(r_.t[vo:vo + 64, :], r_.t[vo:vo + 64, :], [r_.b], [r_.b])
                        tt("dve", Ot.t[vo:vo + 64, pair, T * TT:(T + 1) * TT], ob.t[vo:vo + 64, :], r_.t[vo:vo + 64, :], ALU.mult,
                           [ob.b, r_.b], [Ot.b])
                        del st_T[T]
                        if h == 3:
                            b3_pend.append(T)
                            if len(b3_pend) > 1:
                                b3(b3_pend.pop(0))

                for (T, i) in tiles:
                    if i == 0:
                        q_ = Qt.next()
                        dma("sp", q_.t[0:dk, :], Qs[h, 0:dk, T * TT:(T + 1) * TT], [B_Qs], [q_.b])
                        b_ = None
                        if li == 1:
                            b_ = bT.next()
                            nblk = 4 * T + 4
                            ts("dve", b_.t[:, 0:nblk], ck.t[:, 0:nblk], Rb.t[:, T:T + 1], None, ALU.add, None, [ck.b, Rb.b], [b_.b])
                        st_T[T] = (q_, b_, o_ps.next())
                    q_, b_, ob = st_T[T]
                    m = i - 4 * T
                    c0 = 128 * m if m > 0 else 0
                    sb_ = s_ps.next()
                    mm(sb_.t[:, c0:TT], Kt.t[0:dk, i * 128:(i + 1) * 128], q_.t[0:dk, c0:TT], True, m < 0, [Kt.b, q_.b], [sb_.b])
                    if m >= 0:
                        mm(sb_.t[:, c0:c0 + 128], IDENT(), NEGTRI(), False, True, [CM.b], [sb_.b])
                    p_ = Pt.next()
                    if li == 1:
                        act(p_.t[:, c0:TT], sb_.t[:, c0:TT], AF.Exp, [sb_.b, b_.b], [p_.b], bias=b_.t[:, i:i + 1])
                    else:
                        act(p_.t[:, c0:TT], sb_.t[:, c0:TT], AF.Exp, [sb_.b], [p_.b])
                    pend.append((T, i, p_, c0))
                    if len(pend) > LOOKAHEAD:
                        issue_pv(*pend.pop(0))
                while pend:
                    issue_pv(*pend.pop(0))
                if h == 3:
                    while b3_pend:
                        b3(b3_pend.pop(0))
            P.flush()

    def dump():
        tens = {"hS": hS, "agR0": agR0, "agR1": agR1, "Qs": Qs, "Ks": Ks, "Vs": Vs, "pR": pR, "Cs": Cs, "Lf": Lf, "pS": pS}
        for nm in dbg:
            t = tens[nm]
            ext = nc.dram_tensor("dbg_" + nm, list(t.shape), t.dtype, kind="ExternalOutput")
            if len(t.shape) == 3:
                dma("sp", ext[:, :, :], t[:, :, :], (), ())
            elif nm == "pS":
                dma("sp", ext[0, 0:D, :], t[0, 0:D, :], (), ())
            elif nm == "hS":
                nn = OPT.get('ntiles', 8) * TT
                dma("sp", ext[:, 0:nn], t[:, 0:nn], (), ())
            else:
                dma("sp", ext[:, :], t[:, :], (), ())
        P.flush()

    consts_load()
    if OPT.get('fox_first'):
        stage_tok(3)
        stage_proj(1)
        if stop_after != "proj1":
            stage_attn(1)
        P.flush()
        dump()
        gstack.close()
        return nc
    stage_tok(0)
    if stop_after != "tok0":
        stage_proj(0)
        if stop_after != "proj0":
            stage_attn(0)
            if stop_after != "attn0":
                stage_tok(1)
                stage_proj(1)
                stage_attn(1)
                stage_tok(2)
    P.flush()
    dump()
    gstack.close()
    return nc


GC_FFN1 = 0
GC_MIX = 16
GC_FFN2 = 32
GC_PLE = 48
GC_QLAT = 64
GC_KVLAT = 68
GC_GQ = 70
GC_GK = 71
GC_FQ = 72
GC_FK = 73
GC_INVF = 74
GC_SGN = 75
GC_BF = 76
GC_EPS = 77
NG = 78
OPT = {}


def _chunk_cols(v):
    return np.ascontiguousarray(v.reshape(-1, 128).T)


def prep_inputs(inp):
    f32 = np.float32
    x = inp["x"]
    p = inp["p"]
    positions = inp["positions"]
    common = {}
    for nm, key in (("w1in", "ffn1_w_in"), ("w2in", "ffn2_w_in")):
        w = inp[key].reshape(2, 8, 128, 2, NC_FF, 128)
        w = w.transpose(0, 4, 2, 3, 1, 5)
        common[nm] = np.ascontiguousarray(w).reshape(2, NC_FF, 128, 2048)
    for nm, key in (("w1out", "ffn1_w_out"), ("w2out", "ffn2_w_out")):
        w = inp[key].reshape(2, NC_FF, 128, D).transpose(0, 2, 1, 3)
        common[nm] = np.ascontiguousarray(w)
    common["wgate"] = np.ascontiguousarray(inp["ple_w_gate"].reshape(2, 8, 128, D).transpose(0, 2, 1, 3))
    common["wproj"] = np.ascontiguousarray(inp["ple_w_proj"].reshape(2, 2, 128, D).transpose(0, 2, 1, 3))
    common["mla_win"] = np.ascontiguousarray(inp["mla_w_in"][0].reshape(8, 128, 800).transpose(1, 0, 2))
    cm = np.zeros((6, 128, 128), f32)
    cm[0] = 1.0
    cm[1, :96, :] = 1.0
    cm[2, :64, :64] = 1.0
    cm[2, 64:, 64:] = 1.0
    cm[3] = np.eye(128, dtype=f32)
    kk, qq = np.meshgrid(np.arange(128), np.arange(128), indexing="ij")
    cm[4] = np.where(kk > qq, -30000.0, 0.0)
    hh, tt_ = np.arange(128) // 32, np.arange(128) % 32
    cm[5] = ((hh[:, None] == hh[None, :]) & (tt_[:, None] < tt_[None, :])).astype(f32)
    common["cmats"] = cm
    sel = np.zeros((32, 128), f32)
    for i in range(32):
        sel[i, 64 + i] = 1.0
    for j in range(32):
        sel[(j + 16) % 32, 96 + j] = 1.0
    common["selm"] = sel
    swap = np.concatenate([np.arange(16, 32), np.arange(0, 16)])
    w_uq = inp["mla_w_uq"][0].reshape(512, 16, 96)
    w_uq_ext = np.concatenate([w_uq, w_uq[:, :, 64 + swap]], axis=2)
    w_ukv = inp["mla_w_ukv"][0].reshape(256, 16, 128)
    w_k_ext = np.concatenate([w_ukv[:, :, :64], np.zeros((256, 16, 64), f32)], axis=2)
    w_v = w_ukv[:, :, 64:]
    gq = inp["mla_g_qn"][0]
    gk = inp["mla_g_kn"][0]
    gq_ext = np.concatenate([gq, gq[64 + swap]])
    gk_ext = np.concatenate([gk, gk[64 + swap]])
    inv_freq = (10000.0 ** (-np.arange(0, 32, 2, dtype=f32) / 32)).astype(f32)
    fox_in = inp["fox_w_in"][0]
    fq = fox_in[:, 0:1024].reshape(D, 16, 64)
    fk = fox_in[:, 1024:2048].reshape(D, 16, 64)
    fv = fox_in[:, 2048:3072].reshape(D, 16, 64)
    ff = fox_in[:, 3072:3088]
    in_maps = []
    for c in range(8):
        b, r = c // 4, c % 4
        hs = slice(4 * r, 4 * r + 4)
        m = dict(common)
        m["xT"] = np.ascontiguousarray(x[b, r * NT:(r + 1) * NT, :].T)
        m["pT"] = np.ascontiguousarray(p[:, b, r * NT:(r + 1) * NT, :].transpose(0, 2, 1))
        m["pos"] = np.ascontiguousarray(np.broadcast_to(positions[b].astype(np.int32)[None, :], (32, S)))
        g = np.zeros((128, NG), f32)
        for l in range(2):
            g[:, GC_FFN1 + 8 * l:GC_FFN1 + 8 * l + 8] = _chunk_cols(inp["g_ffn1"][l])
            g[:, GC_MIX + 8 * l:GC_MIX + 8 * l + 8] = _chunk_cols(inp["g_mix"][l])
            g[:, GC_FFN2 + 8 * l:GC_FFN2 + 8 * l + 8] = _chunk_cols(inp["g_ffn2"][l])
            g[:, GC_PLE + 8 * l:GC_PLE + 8 * l + 8] = _chunk_cols(inp["g_ple"][l])
        g[:, GC_QLAT:GC_QLAT + 4] = _chunk_cols(inp["mla_g_q_lat"][0])
        g[:, GC_KVLAT:GC_KVLAT + 2] = _chunk_cols(inp["mla_g_kv_lat"][0])
        g[:, GC_GQ] = gq_ext
        g[:, GC_GK] = gk_ext
        g[:, GC_FQ] = np.tile(inp["fox_g_qn"][0], 2)
        g[:, GC_FK] = np.tile(inp["fox_g_kn"][0], 2)
        g[64:96, GC_INVF] = np.tile(inv_freq, 2)
        g[64:80, GC_SGN] = -1.0
        g[80:96, GC_SGN] = 1.0
        g[0:4, GC_BF] = inp["fox_b_f"][0][hs]
        g[:, GC_EPS] = EPS
        m["gvec"] = g
        m["wuq"] = np.ascontiguousarray(w_uq_ext[:, hs, :].reshape(4, 128, 512).transpose(1, 0, 2))
        m["wukvk"] = np.ascontiguousarray(w_k_ext[:, hs, :].reshape(2, 128, 512).transpose(1, 0, 2))
        m["wukvv"] = np.ascontiguousarray(w_v[:, hs, :].reshape(2, 128, 256).transpose(1, 0, 2))
        m["mla_wo"] = np.ascontiguousarray(inp["mla_w_o"][0][256 * r:256 * (r + 1), :].reshape(2, 128, D).transpose(1, 0, 2))
        m["fwq"] = np.ascontiguousarray(fq[:, hs, :].reshape(8, 128, 256).transpose(1, 0, 2))
        m["fwk"] = np.ascontiguousarray(fk[:, hs, :].reshape(8, 128, 256).transpose(1, 0, 2))
        m["fwv"] = np.ascontiguousarray(fv[:, hs, :].reshape(8, 128, 256).transpose(1, 0, 2))
        m["fwf"] = np.ascontiguousarray(ff[:, hs].reshape(8, 128, 4).transpose(1, 0, 2))
        m["fox_wo"] = np.ascontiguousarray(inp["fox_w_o"][0][256 * r:256 * (r + 1), :].reshape(2, 128, D).transpose(1, 0, 2))
        in_maps.append(m)
    return in_maps


_NC_CACHE = {}


def kernel(**inputs):
    inp = {k: np.asarray(v) for k, v in inputs.items()}
    in_maps = prep_inputs(inp)
    if "nc" not in _NC_CACHE:
        _NC_CACHE["nc"] = build_program()
    nc = _NC_CACHE["nc"]
    res = run_bass_kernel_spmd(nc, in_maps, core_ids=list(range(8)))
    out = np.empty((2, S, D), np.float32)
    for c in range(8):
        b, r = c // 4, c % 4
        out[b, r * NT:(r + 1) * NT, :] = np.asarray(res.results[c]["yT"]).T
    return out
```

```python
import numpy as np
from contextlib import ExitStack
import concourse.bass as bass
import concourse.mybir as mybir
from concourse.bass_utils import run_bass_kernel_spmd

F32 = mybir.dt.float32
BF16 = mybir.dt.bfloat16
I32 = mybir.dt.int32
AF = mybir.ActivationFunctionType
ALU = mybir.AluOpType

D = 1024
S = 16384
NT = 4096
TT = 512
DFF = 2816
NC_FF = DFF // 128
EPS = 1e-6
PI = float(np.pi)
C1 = 6.28125
C2 = float(2 * np.pi - 6.28125)


class Buf:
    __slots__ = ("name", "last_w", "rd", "rd_dma", "wr_all")

    def __init__(self, name=""):
        self.name = name
        self.last_w = None
        self.rd = {}
        self.rd_dma = []
        self.wr_all = []

    def reset(self):
        self.last_w = None
        self.rd = {}
        self.rd_dma = []
        self.wr_all = []


class Op:
    __slots__ = ("eng", "fn", "deps", "dma", "sig", "needed", "idx", "cc", "prev")

    def __init__(self, eng, fn, dma, cc):
        self.eng = eng
        self.fn = fn
        self.deps = []
        self.dma = dma
        self.cc = cc
        self.sig = None
        self.needed = False
        self.prev = None


class Prog:
    ENGS = ("pe", "act", "dve", "pool", "sp")
    NDMASEM = 6
    SEM_ROLL = 24000

    def __init__(self, nc, stack):
        self.nc = nc
        self.stack = stack
        self.ops = []
        self.start = 0
        self.bufs = []
        self.sems = {}
        self.cnt = {}
        self.nsem = 0
        self.dma_pool = {}
        self.dma_n = {}
        self.cc_sem = None
        self.cc_n = 0
        self.out_sigs = []

    def buf(self, name=""):
        b = Buf(name)
        self.bufs.append(b)
        return b

    def newsem(self, tag):
        self.nsem += 1
        return self.stack.enter_context(self.nc.semaphore("s_%s_%d" % (tag, self.nsem)))

    def op(self, eng, fn, reads=(), writes=(), dma=False, cc=False, iwrites=()):
        o = Op(eng, fn, dma, cc)
        o.idx = len(self.ops)
        special = dma or cc
        deps = set()
        for b in reads:
            if b.last_w is not None:
                deps.add(b.last_w)
            deps.update(b.wr_all)
        for b in writes:
            if b.last_w is not None:
                deps.add(b.last_w)
            deps.update(b.wr_all)
            for j in b.rd.values():
                deps.add(j)
            for j in b.rd_dma:
                deps.add(j)
        for b in iwrites:
            if b.last_w is not None:
                deps.add(b.last_w)
        best = {}
        for j in deps:
            p = self.ops[j]
            if p.dma or p.cc:
                o.deps.append(j)
                continue
            if p.eng == eng and not special:
                if eng == "pe":
                    continue
                if not any(b.last_w == j for b in reads):
                    continue
            if best.get(p.eng, -1) < j:
                best[p.eng] = j
        for j in best.values():
            o.deps.append(j)
            self.ops[j].needed = True
        for b in reads:
            if special:
                b.rd_dma.append(o.idx)
            else:
                b.rd[eng] = o.idx
        for b in writes:
            b.last_w = o.idx
            b.rd = {}
            b.rd_dma = []
            b.wr_all = []
        for b in iwrites:
            b.wr_all.append(o.idx)
        self.ops.append(o)
        return o

    def flush(self):
        nc = self.nc
        pend = self.ops[self.start:]
        self.start = len(self.ops)
        if not pend:
            return
        streams = {e: [] for e in self.ENGS}
        for e in self.ENGS:
            if e not in self.sems or self.cnt[e] >= self.SEM_ROLL:
                self.sems[e] = self.newsem(e)
                self.cnt[e] = 0
        if self.cc_sem is None:
            self.cc_sem = self.newsem("cc")
        last_dma = {}
        for o in pend:
            e = o.eng
            streams[e].append(o)
            if o.dma:
                if e not in self.dma_pool:
                    self.dma_pool[e] = [self.newsem("dma" + e) for _ in range(self.NDMASEM)]
                    self.dma_n[e] = 0
                i = self.dma_n[e]
                self.dma_n[e] += 1
                si = i % self.NDMASEM
                s = self.dma_pool[e][si]
                prev = 16 * (i // self.NDMASEM)
                o.sig = (s, prev + 16)
                o.prev = (s, prev)
                last_dma[(e, si)] = o.sig
            elif o.cc:
                self.cc_n += 1
                o.sig = (self.cc_sem, self.cc_n)
                last_dma[("cc", 0)] = o.sig
            elif o.needed:
                self.cnt[e] += 1
                o.sig = (self.sems[e], self.cnt[e])
        ops = self.ops
        finals = list(last_dma.values())

        def run_stream(e):
            def body(eng):
                known = {}
                for o in streams[e]:
                    waits = [ops[j].sig for j in o.deps]
                    if o.dma and o.prev[1] > 0:
                        waits.append(o.prev)
                    for (s, v) in waits:
                        k = id(s)
                        if known.get(k, 0) >= v:
                            continue
                        known[k] = v
                        eng.wait_ge(s, v)
                    ins = o.fn(eng)
                    if o.sig is not None:
                        ins.then_inc(o.sig[0], 16 if o.dma else 1)
                if e == "sp":
                    for (s, v) in finals:
                        eng.wait_ge(s, v)
            return body

        with nc.Block() as block:
            for e, deco in (("pe", block.tensor), ("act", block.scalar), ("dve", block.vector),
                            ("pool", block.gpsimd), ("sp", block.sync)):
                if streams[e] or e == "sp":
                    deco(run_stream(e))
        for b in self.bufs:
            b.reset()


class Tile:
    def __init__(self, t, b):
        self.t = t
        self.b = b


class Rot:
    def __init__(self, tiles):
        self.tiles = tiles
        self.i = 0

    def next(self):
        t = self.tiles[self.i % len(self.tiles)]
        self.i += 1
        return t


def build_program(stop_after=None, dbg=()):
    nc = bass.Bass("TRN2", target_bir_lowering=False)
    gstack = ExitStack()
    P = Prog(nc, gstack)

    def din(name, shape, dt=F32):
        return nc.dram_tensor(name, list(shape), dt, kind="ExternalInput")

    def dint(name, shape, dt):
        return nc.dram_tensor(name, list(shape), dt)

    xT = din("xT", [D, NT])
    pT = din("pT", [2, 256, NT])
    pos = din("pos", [32, S], I32)
    gvec = din("gvec", [128, NG])
    w1in = din("w1in", [2, NC_FF, 128, 2048])
    w2in = din("w2in", [2, NC_FF, 128, 2048])
    w1out = din("w1out", [2, 128, NC_FF, D])
    w2out = din("w2out", [2, 128, NC_FF, D])
    wgate = din("wgate", [2, 128, 8, D])
    wproj = din("wproj", [2, 128, 2, D])
    mla_win = din("mla_win", [128, 8, 800])
    wuq = din("wuq", [128, 4, 512])
    wukvk = din("wukvk", [128, 2, 512])
    wukvv = din("wukvv", [128, 2, 256])
    selm = din("selm", [32, 128])
    mla_wo = din("mla_wo", [128, 2, D])
    fwq = din("fwq", [128, 8, 256])
    fwk = din("fwk", [128, 8, 256])
    fwv = din("fwv", [128, 8, 256])
    fwf = din("fwf", [128, 8, 4])
    fox_wo = din("fox_wo", [128, 2, D])
    cmats = din("cmats", [6, 128, 128])
    yT = nc.dram_tensor("yT", [D, NT], F32, kind="ExternalOutput")

    hS = dint("hS", [D, NT], F32)
    agS0 = dint("agS0", [16, 800, 256], BF16)
    agR0 = dint("agR0", [16, 4 * 800, 256], BF16)
    agS1 = dint("agS1", [16, D, 256], BF16)
    agR1 = dint("agR1", [16, 4 * D, 256], BF16)
    Qs = dint("Qs", [4, 96, S], BF16)
    Ks = dint("Ks", [4, 96, S], BF16)
    Vs = dint("Vs", [4, 128, 128 * 64], BF16)
    Lf = dint("Lf", [4, S], F32)
    Cs = dint("Cs", [4, S], F32)
    Os = dint("Os", [128, 1], F32)
    pS = dint("pS", [8, 4 * D, 512], F32)
    pR = dint("pR", [8, D, 512], F32)
    B_hS, B_agS0, B_agR0, B_agS1, B_agR1 = (P.buf(n) for n in ("hS", "agS0", "agR0", "agS1", "agR1"))
    B_Qs, B_Ks, B_Vs, B_Lf, B_Cs, B_Os, B_pS, B_pR = (P.buf(n) for n in ("Qs", "Ks", "Vs", "Lf", "Cs", "Os", "pS", "pR"))
    B_y = P.buf("y")
    B_pRq = [P.buf("pR%d" % i) for i in range(8)]
    B_pSq = [P.buf("pS%d" % i) for i in range(8)]
    RG = [[0, 1, 2, 3], [4, 5, 6, 7]]

    def dma(eng, out, in_, reads=(), writes=(), slow=False, iw=()):
        if slow:
            return P.op(eng, lambda e: e.dma_start(out=out, in_=in_, allow_slow_non_contiguous=True), reads, writes, dma=True, iwrites=iw)
        return P.op(eng, lambda e: e.dma_start(out=out, in_=in_), reads, writes, dma=True, iwrites=iw)

    def mm(out, lhsT, rhs, start, stop, reads, writes):
        return P.op("pe", lambda e: e.matmul(out, lhsT=lhsT, rhs=rhs, start=start, stop=stop, skip_group_check=True), reads, writes)

    def act(out, in_, func, reads, writes, bias=None, scale=1.0):
        if bias is None:
            return P.op("act", lambda e: e.activation(out=out, in_=in_, func=func, scale=scale), reads, writes)
        return P.op("act", lambda e: e.activation(out=out, in_=in_, func=func, bias=bias, scale=scale), reads, writes)

    def tt(eng, out, in0, in1, op, reads, writes):
        return P.op(eng, lambda e: e.tensor_tensor(out=out, in0=in0, in1=in1, op=op), reads, writes)

    def ts(eng, out, in0, s1, s2, op0, op1, reads, writes):
        if s2 is None:
            return P.op(eng, lambda e: e.tensor_scalar(out=out, in0=in0, scalar1=s1, scalar2=None, op0=op0), reads, writes)
        return P.op(eng, lambda e: e.tensor_scalar(out=out, in0=in0, scalar1=s1, scalar2=s2, op0=op0, op1=op1), reads, writes)

    def stt(eng, out, in0, scalar, in1, op0, op1, reads, writes):
        return P.op(eng, lambda e: e.scalar_tensor_tensor(out=out, in0=in0, scalar=scalar, in1=in1, op0=op0, op1=op1), reads, writes)

    def cp(eng, out, in_, reads, writes):
        return P.op(eng, lambda e: e.tensor_copy(out=out, in_=in_), reads, writes)

    def recip(out, in_, reads, writes):
        return P.op("dve", lambda e: e.reciprocal(out=out, in_=in_), reads, writes)

    def memset(eng, ap, val, writes):
        return P.op(eng, lambda e: e.memset(ap, val), (), writes)

    def allgather(src, dst, idx, bsrc, bdst):
        if OPT.get('nocc'):
            return
        P.op("pool", lambda e: e.collective_compute("AllGather", ALU.bypass, replica_groups=RG,
                                                    ins=[src[idx, :, :]], outs=[dst[idx, :, :]]),
             [bsrc], (), cc=True, iwrites=[bdst])

    uid = [0]

    def sbuf(stack, name, shape, dt):
        uid[0] += 1
        name = "%s_%d" % (name, uid[0])
        return Tile(stack.enter_context(nc.sbuf_tensor(name, list(shape), dt)), P.buf(name))

    def psum(stack, name):
        return Tile(stack.enter_context(nc.psum_tensor(name, [128, 512], F32)), P.buf(name))

    G = sbuf(gstack, "gv", [128, NG], F32)
    CM = sbuf(gstack, "cm", [128, 5, 128], BF16)
    TRI = sbuf(gstack, "tri", [128, 128], F32)
    ps = [psum(gstack, "ps%d" % i) for i in range(8)]

    def consts_load():
        dma("sp", G.t[:, :], gvec[:, :], (), [G.b])
        for i in range(5):
            dma("pool", CM.t[:, i, :], cmats[i, :, :], (), [CM.b])
        dma("sp", TRI.t[:, :], cmats[5, :, :], (), [TRI.b])
        ts("dve", G.t[:, GC_GQ:GC_GQ + 1], G.t[:, GC_GQ:GC_GQ + 1], float(96 ** -0.5), None, ALU.mult, None, [G.b], [G.b])
        ts("dve", G.t[:, GC_FQ:GC_FQ + 1], G.t[:, GC_FQ:GC_FQ + 1], float(64 ** -0.5), None, ALU.mult, None, [G.b], [G.b])

    ONES_ALL = lambda: CM.t[:, 0, :]
    ONES96 = lambda: CM.t[:, 1, :]
    ONES_BD = lambda: CM.t[:, 2, :]
    IDENT = lambda: CM.t[:, 3, :]
    NEGTRI = lambda: CM.t[:, 4, :]
    EPSC = lambda p0, p1: G.t[p0:p1, GC_EPS:GC_EPS + 1]

    def rmsnorm(src, nch, nfeat, gc0, out, sq, ssps, rs):
        tt("dve", sq.t[:, 0:nch, :], src.t[:, 0:nch, :], src.t[:, 0:nch, :], ALU.mult, [src.b], [sq.b])
        for c in range(nch):
            mm(ssps.t[:, :], ONES_ALL(), sq.t[:, c, :], c == 0, c == nch - 1, [CM.b, sq.b], [ssps.b])
        act(rs.t[:, :], ssps.t[:, :], AF.Sqrt, [ssps.b, G.b], [rs.b], bias=EPSC(0, 128), scale=1.0 / nfeat)
        recip(rs.t[:, :], rs.t[:, :], [rs.b], [rs.b])
        for c in range(nch):
            stt("dve", out.t[:, c, :], src.t[:, c, :], G.t[:, gc0 + c:gc0 + c + 1], rs.t[:, :], ALU.mult, ALU.mult,
                [src.b, G.b, rs.b], [out.b])

    def rstd_act(rs_ap, ss_ap, nfeat, rb, sb_, p0=0, p1=128):
        act(rs_ap, ss_ap, AF.Ln, [sb_, G.b], [rb], bias=EPSC(p0, p1), scale=1.0 / nfeat)
        act(rs_ap, rs_ap, AF.Exp, [rb], [rb], scale=-0.5)

    def stage_tok(stage):
        NH = 2
        ST = NH * TT
        with ExitStack() as st:
            hT_slots = [sbuf(st, "hT%d" % i, [128, 8, ST], F32) for i in range(2)]
            hb_slots = [[P.buf("hA%d" % i), P.buf("hB%d" % i)] for i in range(2)]
            hT = hT_slots[0]
            hb = hb_slots[0]
            xn = sbuf(st, "xn", [128, 8, ST], BF16)
            sq = sbuf(st, "sq", [128, 8, TT], BF16)
            aT = sbuf(st, "aT", [128, NC_FF, ST], BF16)
            rs = sbuf(st, "rs", [128, TT], F32)
            sg = Rot([sbuf(st, "sg%d" % i, [128, TT], F32) for i in range(2)])
            win = Rot([sbuf(st, "win%d" % i, [128, 2048], BF16) for i in range(3 if stage == 0 else 4)])
            wom = Rot([sbuf(st, "wom%d" % i, [128, NC_FF, 128], BF16) for i in range(4)])
            ss_ps = ps[0]
            g_ps = Rot([ps[1], ps[2]])
            u_ps = Rot([ps[3], ps[4]])
            o_ps = Rot([ps[5], ps[6]])
            if stage in (1, 2):
                wg = sbuf(st, "wg", [128, 8, D], BF16)
                wp = sbuf(st, "wp", [128, 2, D], BF16)
                ptl = sbuf(st, "ptl", [128, 2, ST], BF16)
                li = stage - 1

                def rs_op(q):
                    P.op("pool", lambda e: e.collective_compute("ReduceScatter", ALU.add, replica_groups=RG,
                                                                ins=[pS[q, :, :]], outs=[pR[q, :, :]]),
                         (), [B_pRq[q]], cc=True)
                dma("pool", wg.t[:, :, :], wgate[li, :, :, :], (), [wg.b])
                dma("pool", wp.t[:, :, :], wproj[li, :, :, :], (), [wp.b])
            if stage == 0:
                mw = sbuf(st, "mw", [128, 8, 800], BF16)
                dma("pool", mw.t[:, :, :], mla_win[:, :, :], (), [mw.b])
                zf = sbuf(st, "zf", [128, 6, TT], F32)
                zn = sbuf(st, "zn", [128, 6, TT], BF16)
                kpe = sbuf(st, "kpe", [32, TT], BF16)

            xb = [P.buf("xA"), P.buf("xB")]
            ab = [P.buf("aA"), P.buf("aB")]

            def norm_sq(hf):
                cs = slice(hf * TT, (hf + 1) * TT)
                tt("dve", sq.t[:, :, :], hT.t[:, :, cs], hT.t[:, :, cs], ALU.mult, [hb[hf]], [sq.b])

            def norm_fin(hf, gc0):
                cs = slice(hf * TT, (hf + 1) * TT)
                for c in range(8):
                    mm(ss_ps.t[:, :], ONES_ALL(), sq.t[:, c, :], c == 0, c == 7, [CM.b, sq.b], [ss_ps.b])
                rstd_act(rs.t[:, :], ss_ps.t[:, :], D, rs.b, ss_ps.b)
                for c in range(8):
                    stt("dve", xn.t[:, c, cs], hT.t[:, c, cs], G.t[:, gc0 + c:gc0 + c + 1], rs.t[:, :], ALU.mult, ALU.mult,
                        [hb[hf], G.b, rs.b], [xb[hf]])

            def norm_half(hf, gc0):
                norm_sq(hf)
                norm_fin(hf, gc0)

            def norm_h(gc0):
                for hf in range(NH):
                    norm_half(hf, gc0)

            def ffn(li, win_d, wout_d, next_gc):
                wo_pref = []

                def wo_load(m):
                    wq = wom.next()
                    dma("pool", wq.t[:, :, :], wout_d[li, :, :, m * 128:(m + 1) * 128], (), [wq.b])
                    wo_pref.append(wq)

                for c in range(NC_FF):
                    w = win.next()
                    dma("pool", w.t[:, :], win_d[li, c, :, :], (), [w.b])
                    if c in (8, 12, 16, 20):
                        wo_load(len(wo_pref))
                    for hf in range(NH):
                        cs = slice(hf * TT, (hf + 1) * TT)
                        gp = g_ps.next()
                        up = u_ps.next()
                        for half, pp in ((0, gp), (1, up)):
                            for k in range(8):
                                mm(pp.t[:, :], w.t[:, (half * 8 + k) * 128:(half * 8 + k + 1) * 128], xn.t[:, k, cs],
                                   k == 0, k == 7, [w.b, xb[hf]], [pp.b])
                        s = sg.next()
                        act(s.t[:, :], gp.t[:, :], AF.Silu, [gp.b], [s.b])
                        tt("dve", aT.t[:, c, cs], s.t[:, :], up.t[:, :], ALU.mult, [s.b, up.b], [ab[hf]])
                for g4 in range(2):
                    for hf in range(NH):
                        cs = slice(hf * TT, (hf + 1) * TT)
                        for m in range(4 * g4, 4 * g4 + 4):
                            wq = wo_pref[m]
                            op_ = o_ps.next()
                            for c in range(NC_FF):
                                mm(op_.t[:, :], wq.t[:, c, :], aT.t[:, c, cs], c == 0, c == NC_FF - 1, [wq.b, ab[hf]], [op_.b])
                            stt("dve", hT.t[:, m, cs], op_.t[:, :], 0.5, hT.t[:, m, cs], ALU.mult, ALU.add, [op_.b, hb[hf]], [hb[hf]])
                            if hf == NH - 1 and len(wo_pref) < 8:
                                wo_load(len(wo_pref))
                            if g4 == 1 and next_gc is not None and hf == 1 and m == 5:
                                norm_fin(0, next_gc)
                        if g4 == 1 and next_gc is not None:
                            if hf == 0:
                                norm_sq(0)
                            else:
                                norm_half(1, next_gc)

            def ple(li, t, next_gc):
                dma("pool", ptl.t[:, :, :], pT[li, :, t * ST:(t + 1) * ST].rearrange("(c p) n -> p c n", p=128), (), [ptl.b])
                for hf in range(NH):
                    cs = slice(hf * TT, (hf + 1) * TT)
                    for m in range(8):
                        gp = g_ps.next()
                        up = u_ps.next()
                        for k in range(8):
                            mm(gp.t[:, :], wg.t[:, k, m * 128:(m + 1) * 128], xn.t[:, k, cs], k == 0, k == 7, [wg.b, xb[hf]], [gp.b])
                        for k in range(2):
                            mm(up.t[:, :], wp.t[:, k, m * 128:(m + 1) * 128], ptl.t[:, k, cs], k == 0, k == 1, [wp.b, ptl.b], [up.b])
                        s = sg.next()
                        act(s.t[:, :], gp.t[:, :], AF.Sigmoid, [gp.b], [s.b])
                        tt("dve", s.t[:, :], s.t[:, :], up.t[:, :], ALU.mult, [s.b, up.b], [s.b])
                        tt("dve", hT.t[:, m, cs], hT.t[:, m, cs], s.t[:, :], ALU.add, [hb[hf], s.b], [hb[hf]])
                        if next_gc is not None and hf == 1 and m == 1:
                            norm_fin(0, next_gc)
                    if next_gc is not None:
                        if hf == 0:
                            norm_sq(0)
                        else:
                            norm_half(1, next_gc)

            def mla_pre(t):
                for hf in range(NH):
                    cs = slice(hf * TT, (hf + 1) * TT)
                    for zc in range(7):
                        gp = g_ps.next()
                        mcols = 128 if zc < 6 else 32
                        for k in range(8):
                            mm(gp.t[0:mcols, :], mw.t[:, k, zc * 128:zc * 128 + mcols], xn.t[:, k, cs], k == 0, k == 7,
                               [mw.b, xb[hf]], [gp.b])
                        if zc < 6:
                            act(zf.t[:, zc, :], gp.t[:, :], AF.Copy, [gp.b], [zf.b])
                        else:
                            act(kpe.t[:, :], gp.t[0:32, :], AF.Copy, [gp.b], [kpe.b])
                    for (c0, nch, nf, gc) in ((0, 4, 512, GC_QLAT), (4, 2, 256, GC_KVLAT)):
                        tt("dve", sq.t[:, c0:c0 + nch, :], zf.t[:, c0:c0 + nch, :], zf.t[:, c0:c0 + nch, :], ALU.mult, [zf.b], [sq.b])
                        for c in range(nch):
                            mm(ss_ps.t[:, :], ONES_ALL(), sq.t[:, c0 + c, :], c == 0, c == nch - 1, [CM.b, sq.b], [ss_ps.b])
                        rstd_act(rs.t[:, :], ss_ps.t[:, :], nf, rs.b, ss_ps.b)
                        for c in range(nch):
                            stt("dve", zn.t[:, c0 + c, :], zf.t[:, c0 + c, :], G.t[:, gc + c:gc + c + 1], rs.t[:, :], ALU.mult, ALU.mult,
                                [zf.b, G.b, rs.b], [zn.b])
                    for h2 in range(2):
                        c2 = slice(h2 * 256, (h2 + 1) * 256)
                        idx = 2 * (NH * t + hf) + h2
                        dma("sp", agS0[idx, 0:768, :].rearrange("(c p) n -> p c n", p=128), zn.t[:, :, c2], [zn.b], iw=[B_agS0])
                        dma("sp", agS0[idx, 768:800, :], kpe.t[:, c2], [kpe.b], iw=[B_agS0])
                        allgather(agS0, agR0, idx, B_agS0, B_agR0)

            ntl = OPT.get('ntiles', NT // ST)

            def prefetch(t):
                tok = slice(t * ST, (t + 1) * ST)
                hT_ = hT_slots[t % 2]
                hb_ = hb_slots[t % 2]
                if stage in (0, 3):
                    dma("sp", hT_.t[:, :, :], xT[:, tok].rearrange("(c p) n -> p c n", p=128), (), hb_)
                else:
                    dma("sp", hT_.t[:, :, :], hS[:, tok].rearrange("(c p) n -> p c n", p=128), [B_hS], hb_)
                    for hf in range(NH):
                        cs = slice(hf * TT, (hf + 1) * TT)
                        P.op("pool", (lambda o_, i_: (lambda e: e.dma_start(out=o_, in_=i_, accum_op=ALU.add)))(
                            hT_.t[:, :, cs], pR[NH * t + hf, :, :].rearrange("(c p) n -> p c n", p=128)),
                            [B_pRq[NH * t + hf], hb_[hf]], [hb_[hf]], dma=True)

            prefetch(0)
            for t in range(ntl):
                tok = slice(t * ST, (t + 1) * ST)
                hT = hT_slots[t % 2]
                hb = hb_slots[t % 2]
                if stage in (1, 2):
                    li = stage - 1
                    norm_h(GC_FFN2 + 8 * li)
                    if stage == 2 and t + 1 < ntl:
                        prefetch(t + 1)
                    ffn(li, w2in, w2out, GC_PLE + 8 * li)
                    ple(li, t, GC_FFN1 + 8 if stage == 1 else None)
                if stage == 0:
                    norm_h(GC_FFN1 + 0)
                    if t + 1 < ntl:
                        prefetch(t + 1)
                    ffn(0, w1in, w1out, GC_MIX + 0)
                    dma("sp", hS[:, tok].rearrange("(c p) n -> p c n", p=128), hT.t[:, :, :], hb, iw=[B_hS])
                    mla_pre(t)
                elif stage in (1, 3):
                    if stage == 3:
                        norm_h(GC_FFN1 + 8)
                    if t + 1 < ntl:
                        prefetch(t + 1)
                    ffn(1, w1in, w1out, GC_MIX + 8)
                    dma("sp", hS[:, tok].rearrange("(c p) n -> p c n", p=128), hT.t[:, :, :], hb, iw=[B_hS])
                    for hf in range(NH):
                        for h2 in range(2):
                            c2 = slice(hf * TT + h2 * 256, hf * TT + (h2 + 1) * 256)
                            idx = 2 * (NH * t + hf) + h2
                            dma("sp", agS1[idx, :, :].rearrange("(c p) n -> p c n", p=128), xn.t[:, :, c2], [xb[hf]], iw=[B_agS1])
                            allgather(agS1, agR1, idx, B_agS1, B_agR1)
                else:
                    dma("sp", yT[:, tok].rearrange("(c p) n -> p c n", p=128), hT.t[:, :, :], hb, iw=[B_y])
            P.flush()

    def stage_proj(li):
        with ExitStack() as st:
            sqh = Rot([sbuf(st, "sqh%d" % i, [128, TT], BF16) for i in range(3)])
            rsh = Rot([sbuf(st, "rsh%d" % i, [128, TT], F32) for i in range(3)])
            vt = Rot([sbuf(st, "vt%d" % i, [128, 4, 256], BF16) for i in range(2)])
            qk_ps = Rot([ps[0], ps[1], ps[2], ps[7]] if li == 0 else [ps[0], ps[1], ps[2]])
            ss_ps = Rot([ps[3], ps[4], ps[5]] if li == 0 else [ps[3], ps[4]])
            v_ps = Rot([ps[6]] if li == 0 else [ps[5], ps[6]])
            f_ps = ps[7]
            Vs_v = Vs[:, :, :].rearrange("h p (b d) -> h p b d", d=64)
            if li == 0:
                w_q = sbuf(st, "w_q", [128, 4, 512], BF16)
                w_k = sbuf(st, "w_k", [128, 2, 512], BF16)
                w_v = sbuf(st, "w_v", [128, 2, 256], BF16)
                sel = sbuf(st, "sel", [32, 128], BF16)
                dma("pool", w_q.t[:, :, :], wuq[:, :, :], (), [w_q.b])
                dma("pool", w_k.t[:, :, :], wukvk[:, :, :], (), [w_k.b])
                dma("pool", w_v.t[:, :, :], wukvv[:, :, :], (), [w_v.b])
                dma("pool", sel.t[:, :], selm[:, :], (), [sel.b])
                zq = Rot([sbuf(st, "zq%d" % i, [128, 4, TT], BF16) for i in range(2)])
                zkv = Rot([sbuf(st, "zkv%d" % i, [128, 2, TT], BF16) for i in range(2)])
                kp = Rot([sbuf(st, "kp%d" % i, [32, TT], BF16) for i in range(2)])
                posi = sbuf(st, "posi", [128, TT], I32)
                ang = sbuf(st, "ang", [128, TT], F32)
                nfl = sbuf(st, "nfl", [128, TT], F32)
                nin = sbuf(st, "nin", [128, TT], I32)
                msk = sbuf(st, "msk", [128, TT], F32)
                cos_tb = [sbuf(st, "cos_t%d" % i, [128, TT], F32) for i in range(2)]
                sin_tb = [sbuf(st, "sin_t%d" % i, [128, TT], F32) for i in range(2)]
                qn = Rot([sbuf(st, "qn%d" % i, [128, TT], F32) for i in range(3)])
                sw = Rot([sbuf(st, "sw%d" % i, [128, TT], F32) for i in range(3)])
                t1 = Rot([sbuf(st, "t1%d" % i, [128, TT], F32) for i in range(3)])
                qo = Rot([sbuf(st, "qo%d" % i, [128, TT], BF16) for i in range(5)])
                R_ = slice(64, 96)

                def rope_tables(T):
                    cos_t = cos_tb[T % 2]
                    sin_t = sin_tb[T % 2]
                    dma("sp", posi.t[R_, :], pos[:, T * TT:(T + 1) * TT], (), [posi.b])
                    cp("dve", ang.t[R_, :], posi.t[R_, :], [posi.b], [ang.b])
                    ts("dve", ang.t[R_, :], ang.t[R_, :], G.t[R_, GC_INVF:GC_INVF + 1], None, ALU.mult, None, [ang.b, G.b], [ang.b])
                    ts("dve", nfl.t[R_, :], ang.t[R_, :], 1.0 / (2 * PI), None, ALU.mult, None, [ang.b], [nfl.b])
                    cp("dve", nin.t[R_, :], nfl.t[R_, :], [nfl.b], [nin.b])
                    cp("dve", nfl.t[R_, :], nin.t[R_, :], [nin.b], [nfl.b])
                    stt("dve", ang.t[R_, :], nfl.t[R_, :], -C1, ang.t[R_, :], ALU.mult, ALU.add, [nfl.b, ang.b], [ang.b])
                    stt("dve", ang.t[R_, :], nfl.t[R_, :], -C2, ang.t[R_, :], ALU.mult, ALU.add, [nfl.b, ang.b], [ang.b])
                    ts("dve", ang.t[R_, :], ang.t[R_, :], -PI, PI, ALU.max, ALU.min, [ang.b], [ang.b])
                    act(sin_t.t[R_, :], ang.t[R_, :], AF.Sin, [ang.b], [sin_t.b])
                    ts("dve", sin_t.t[R_, :], sin_t.t[R_, :], G.t[R_, GC_SGN:GC_SGN + 1], None, ALU.mult, None, [sin_t.b, G.b], [sin_t.b])
                    ts("dve", msk.t[R_, :], ang.t[R_, :], -1.0, None, ALU.mult, None, [ang.b], [msk.b])
                    tt("dve", msk.t[R_, :], msk.t[R_, :], ang.t[R_, :], ALU.max, [msk.b, ang.b], [msk.b])
                    ts("dve", msk.t[R_, :], msk.t[R_, :], -1.0, PI / 2, ALU.mult, ALU.add, [msk.b], [msk.b])
                    act(cos_t.t[R_, :], msk.t[R_, :], AF.Sin, [msk.b], [cos_t.b])

                def hA(pq):
                    s_ = sqh.next()
                    act(s_.t[:, :], pq.t[:, :], AF.Square, [pq.b], [s_.b])
                    sp_ = ss_ps.next()
                    mm(sp_.t[:, :], ONES96(), s_.t[:, :], True, True, [CM.b, s_.b], [sp_.b])
                    return sp_

                def hB(pq, sp_, gc):
                    r_ = rsh.next()
                    rstd_act(r_.t[:, :], sp_.t[:, :], 96, r_.b, sp_.b)
                    o_ = qo.next()
                    stt("dve", o_.t[:, :], pq.t[:, :], G.t[:, gc:gc + 1], r_.t[:, :], ALU.mult, ALU.mult, [pq.b, G.b, r_.b], [o_.b])
                    return o_

                def hC(o_, dst, bdst, T):
                    cos_t = cos_tb[T % 2]
                    sin_t = sin_tb[T % 2]
                    w_ = sw.next()
                    act(w_.t[64:96, :], o_.t[96:128, :], AF.Copy, [o_.b], [w_.b])
                    a_ = t1.next()
                    tt("dve", a_.t[R_, :], o_.t[R_, :], cos_t.t[R_, :], ALU.mult, [o_.b, cos_t.b], [a_.b])
                    tt("dve", w_.t[R_, :], w_.t[R_, :], sin_t.t[R_, :], ALU.mult, [w_.b, sin_t.b], [w_.b])
                    tt("dve", o_.t[R_, :], a_.t[R_, :], w_.t[R_, :], ALU.add, [a_.b, w_.b, o_.b], [o_.b])
                    dma("sp", dst, o_.t[0:96, :], [o_.b], iw=[bdst])

                G0 = agR0[:, :, :].rearrange("i (r f) n -> i r f n", r=4)
                def mla_load(T):
                    r_, lt = T // 8, T % 8
                    a = zq.next()
                    b = zkv.next()
                    c = kp.next()
                    for hf in range(2):
                        cs = slice(hf * 256, (hf + 1) * 256)
                        dma("sp", a.t[:, :, cs], G0[2 * lt + hf, r_, 0:512, :].rearrange("(c p) n -> p c n", p=128), [B_agR0], [a.b])
                        dma("sp", b.t[:, :, cs], G0[2 * lt + hf, r_, 512:768, :].rearrange("(c p) n -> p c n", p=128), [B_agR0], [b.b])
                        dma("sp", c.t[:, cs], G0[2 * lt + hf, r_, 768:800, :], [B_agR0], [c.b])
                    return a, b, c

                NTL = S // TT
                tin = {0: mla_load(0)}
                rope_tables(0)

                def v_work(T):
                    a, b, c = tin[T]
                    v_ = vt.next()
                    for blk in range(4):
                        pv = v_ps.next()
                        for k in range(2):
                            mm(pv.t[:, 0:256], b.t[:, k, blk * 128:(blk + 1) * 128], w_v.t[:, k, :], k == 0, k == 1, [b.b, w_v.b], [pv.b])
                        act(v_.t[:, blk, :], pv.t[:, 0:256], AF.Copy, [pv.b], [v_.b])
                    for h in range(4):
                        dma("sp", Vs_v[h, :, T * 4:(T + 1) * 4, :], v_.t[:, :, h * 64:(h + 1) * 64], [v_.b], iw=[B_Vs])

                jobs = [(T, kind, h) for T in range(NTL) for h in range(4) for kind in ("q", "k")]
                stA, stB = [], []
                for step in range(len(jobs) + 2):
                    if step < len(jobs):
                        T, kind, h = jobs[step]
                        jn = step % 8
                        a, b, c = tin[T]
                        gcol = slice(T * TT, (T + 1) * TT)
                        if jn == 0 and T + 1 < NTL:
                            tin[T + 1] = mla_load(T + 1)
                        pq = qk_ps.next()
                        if kind == "q":
                            for k in range(4):
                                mm(pq.t[:, :], w_q.t[:, k, h * 128:(h + 1) * 128], a.t[:, k, :], k == 0, k == 3, [w_q.b, a.b], [pq.b])
                            info = (GC_GQ, Qs[h, 0:96, gcol], B_Qs, T)
                        else:
                            for k in range(2):
                                mm(pq.t[:, :], w_k.t[:, k, h * 128:(h + 1) * 128], b.t[:, k, :], k == 0, False, [w_k.b, b.b], [pq.b])
                            mm(pq.t[:, :], sel.t[:, :], c.t[:, :], False, True, [sel.b, c.b], [pq.b])
                            info = (GC_GK, Ks[h, 0:96, gcol], B_Ks, T)
                        stA.append((pq, hA(pq), info))
                        if jn == 3 and T + 1 < NTL:
                            rope_tables(T + 1)
                        if jn == 5:
                            v_work(T)
                    if step >= 1 and stA and step - 1 < len(jobs):
                        pq, sp_, info = stA.pop(0)
                        stB.append((hB(pq, sp_, info[0]), info))
                    if step >= 2 and stB:
                        o_, info = stB.pop(0)
                        hC(o_, info[1], info[2], info[3])
            else:
                w_q = sbuf(st, "f_q", [128, 8, 256], BF16)
                w_k = sbuf(st, "f_k", [128, 8, 256], BF16)
                w_v = sbuf(st, "f_v", [128, 8, 256], BF16)
                w_f = sbuf(st, "f_f", [128, 8, 4], BF16)
                dma("pool", w_q.t[:, :, :], fwq[:, :, :], (), [w_q.b])
                dma("pool", w_k.t[:, :, :], fwk[:, :, :], (), [w_k.b])
                dma("pool", w_v.t[:, :, :], fwv[:, :, :], (), [w_v.b])
                dma("pool", w_f.t[:, :, :], fwf[:, :, :], (), [w_f.b])
                hn = Rot([sbuf(st, "hn%d" % i, [128, 8, TT], BF16) for i in range(2)])
                qo = Rot([sbuf(st, "fqo%d" % i, [128, TT], BF16) for i in range(4)])
                lf = Rot([sbuf(st, "lf%d" % i, [4, TT], F32) for i in range(2)])
                G1 = agR1[:, :, :].rearrange("i (r f) n -> i r f n", r=4)

                def fA(pq):
                    s_ = sqh.next()
                    act(s_.t[:, :], pq.t[:, :], AF.Square, [pq.b], [s_.b])
                    sp_ = ss_ps.next()
                    mm(sp_.t[:, :], ONES_BD(), s_.t[:, :], True, True, [CM.b, s_.b], [sp_.b])
                    return sp_

                def fB(pq, sp_, gc):
                    r_ = rsh.next()
                    rstd_act(r_.t[:, :], sp_.t[:, :], 64, r_.b, sp_.b)
                    o_ = qo.next()
                    stt("dve", o_.t[:, :], pq.t[:, :], G.t[:, gc:gc + 1], r_.t[:, :], ALU.mult, ALU.mult, [pq.b, G.b, r_.b], [o_.b])
                    return o_

                def fC(o_, dstT, B_dst, pair, gcol):
                    for j in range(2):
                        dma("sp", dstT[2 * pair + j, 0:64, gcol], o_.t[64 * j:64 * j + 64, :], [o_.b], iw=[B_dst])

                def fox_load(T):
                    r_, lt = T // 8, T % 8
                    a = hn.next()
                    for hf in range(2):
                        cs = slice(hf * 256, (hf + 1) * 256)
                        dma("sp", a.t[:, :, cs], G1[2 * lt + hf, r_, :, :].rearrange("(c p) n -> p c n", p=128), [B_agR1], [a.b])
                    return a

                nbf = sbuf(st, "nbf", [4, 1], F32)
                ts("dve", nbf.t[:, :], G.t[0:4, GC_BF:GC_BF + 1], -1.0, None, ALU.mult, None, [G.b], [nbf.b])

                def side_work(T, a):
                    v_ = vt.next()
                    for blk in range(4):
                        pv = v_ps.next()
                        for k in range(8):
                            mm(pv.t[:, 0:256], a.t[:, k, blk * 128:(blk + 1) * 128], w_v.t[:, k, :], k == 0, k == 7, [a.b, w_v.b], [pv.b])
                        act(v_.t[:, blk, :], pv.t[:, 0:256], AF.Copy, [pv.b], [v_.b])
                    for h in range(4):
                        dma("sp", Vs_v[h, :, T * 4:(T + 1) * 4, :], v_.t[:, :, h * 64:(h + 1) * 64], [v_.b], iw=[B_Vs])
                    for k in range(8):
                        mm(f_ps.t[0:4, :], w_f.t[:, k, :], a.t[:, k, :], k == 0, k == 7, [w_f.b, a.b], [f_ps.b])
                    l_ = lf.next()
                    act(l_.t[:, :], f_ps.t[0:4, :], AF.Exp, [f_ps.b, nbf.b], [l_.b], bias=nbf.t[0:4, 0:1], scale=-1.0)
                    act(l_.t[:, :], l_.t[:, :], AF.Ln, [l_.b], [l_.b], bias=G.t[0:4, GC_ONE:GC_ONE + 1])
                    ts("dve", l_.t[:, :], l_.t[:, :], -1.0, None, ALU.mult, None, [l_.b], [l_.b])
                    dma("sp", Lf[:, slice(T * TT, (T + 1) * TT)], l_.t[:, :], [l_.b], iw=[B_Lf])

                NTL = S // TT
                tin = {0: fox_load(0)}
                jobs = [(T, kind, pair) for T in range(NTL) for pair in range(2) for kind in ("q", "k")]
                stA, stB = [], []
                for step in range(len(jobs) + 2):
                    if step < len(jobs):
                        T, kind, pair = jobs[step]
                        jn = step % 4
                        a = tin[T]
                        gcol = slice(T * TT, (T + 1) * TT)
                        if jn == 0 and T + 1 < NTL:
                            tin[T + 1] = fox_load(T + 1)
                        pq = qk_ps.next()
                        w_ = w_q if kind == "q" else w_k
                        for k in range(8):
                            mm(pq.t[:, :], w_.t[:, k, pair * 128:(pair + 1) * 128], a.t[:, k, :], k == 0, k == 7, [w_.b, a.b], [pq.b])
                        info = (GC_FQ, Qs, B_Qs, pair, gcol) if kind == "q" else (GC_FK, Ks, B_Ks, pair, gcol)
                        stA.append((pq, fA(pq), info))
                        if jn == 2:
                            side_work(T, a)
                    if step >= 1 and stA and step - 1 < len(jobs):
                        pq, sp_, info = stA.pop(0)
                        stB.append((fB(pq, sp_, info[0]), info))
                    if step >= 2 and stB:
                        o_, info = stB.pop(0)
                        fC(o_, info[1], info[2], info[3], info[4])
                L = sbuf(st, "L", [128, TT], F32)
                one = sbuf(st, "one", [128, TT], F32)
                sc = sbuf(st, "sc", [128, TT], F32)
                hi = sbuf(st, "hi", [128, TT], BF16)
                lo = sbuf(st, "lo", [128, TT], BF16)
                onb = sbuf(st, "onb", [128, TT], BF16)
                off = sbuf(st, "off", [128, 1], F32)
                dma("sp", L.t[:, :], Lf[:, :].rearrange("h (t n) -> (h t) n", n=TT), [B_Lf], [L.b])
                memset("dve", one.t[:, :], 1.0, [one.b])
                memset("pool", onb.t[:, :], 1.0, [onb.b])
                P.op("dve", lambda e: e.tensor_tensor_scan(out=sc.t[:, :], data0=one.t[:, :], data1=L.t[:, :], initial=0.0,
                                                           op0=ALU.mult, op1=ALU.add), [one.b, L.b], [sc.b])
                pq = ps[0]
                mm(pq.t[:, 0:1], TRI.t[:, :], sc.t[:, TT - 1:TT], True, True, [TRI.b, sc.b], [pq.b])
                cp("dve", off.t[:, :], pq.t[:, 0:1], [pq.b], [off.b])
                dma("sp", Os[:, :], off.t[:, :], [off.b], iw=[B_Os])
                ts("dve", sc.t[:, :], sc.t[:, :], off.t[:, 0:1], None, ALU.add, None, [sc.b, off.b], [sc.b])
                pcs = [sbuf(st, "pc%d" % i, [128, TT], BF16) for i in range(3)]
                ncs = [sbuf(st, "nc%d" % i, [128, TT], BF16) for i in range(3)]
                for j in range(3):
                    cp("dve", pcs[j].t[:, :], sc.t[:, :], [sc.b], [pcs[j].b])
                    if j < 2:
                        tt("dve", sc.t[:, :], sc.t[:, :], pcs[j].t[:, :], ALU.subtract, [sc.b, pcs[j].b], [sc.b])
                    ts("dve", ncs[j].t[:, :], pcs[j].t[:, :], -1.0, None, ALU.mult, None, [pcs[j].b], [ncs[j].b])
                for h in range(4):
                    for j in range(3):
                        dma("sp", Qs[h, 64 + j, :].rearrange("(t n) -> t n", n=TT), pcs[j].t[32 * h:32 * h + 32, :], [pcs[j].b], iw=[B_Qs])
                        dma("sp", Ks[h, 67 + j, :].rearrange("(t n) -> t n", n=TT), ncs[j].t[32 * h:32 * h + 32, :], [ncs[j].b], iw=[B_Ks])
                    dma("sp", Qs[h, 67:70, :].rearrange("a (t n) -> (a t) n", n=TT), onb.t[0:96, :], [onb.b], iw=[B_Qs])
                    dma("sp", Ks[h, 64:67, :].rearrange("a (t n) -> (a t) n", n=TT), onb.t[0:96, :], [onb.b], iw=[B_Ks])
            P.flush()

    def stage_attn(li):
        dk = 96 if li == 0 else 70
        wo_d = mla_wo if li == 0 else fox_wo
        with ExitStack() as st:
            Kt = sbuf(st, "Kt", [96, S], BF16)
            Vt = sbuf(st, "Vt", [128, 128, 128], BF16)
            Ot = sbuf(st, "Ot", [128, 2, S], BF16)
            Qt = Rot([sbuf(st, "Qt%d" % i, [96, TT], BF16) for i in range(4)])
            LOOKAHEAD = 2
            Pt = Rot([sbuf(st, "Pt%d" % i, [128, TT], BF16) for i in range(4)])
            rsum = Rot([sbuf(st, "rsum%d" % i, [128, TT], F32) for i in range(2)])
            wo = sbuf(st, "wo", [128, 2, D], BF16)
            pt = Rot([sbuf(st, "pt%d" % i, [128, 4, TT], F32) for i in range(7)])
            s_ps = Rot([ps[0], ps[1], ps[2], ps[7]])
            o_ps = Rot([ps[3], ps[4]])
            p_ps = Rot([ps[5], ps[6]])
            dma("pool", wo.t[:, :, :], wo_d[:, :, :], (), [wo.b])
            Vs_v = Vs[:, :, :].rearrange("h p (b d) -> h p b d", d=64)
            pS_v = pS[:, :, :].rearrange("q (r f) n -> q r f n", r=4)
            b3_done = [0] * 8

            def b3(T):
                q = T % 8
                for mh in range(2):
                    x_ = pt.next()
                    for m4 in range(4):
                        m = mh * 4 + m4
                        pp = p_ps.next()
                        for pr in range(2):
                            mm(pp.t[:, :], wo.t[:, pr, m * 128:(m + 1) * 128], Ot.t[:, pr, T * TT:(T + 1) * TT], pr == 0, pr == 1,
                               [wo.b, Ot.b], [pp.b])
                        cp("dve", x_.t[:, m4, :], pp.t[:, :], [pp.b], [x_.b])
                    dma("pool", pS_v[q, T // 8, mh * 512:(mh + 1) * 512, :].rearrange("(c p) n -> p c n", p=128),
                        x_.t[:, :, :], [x_.b], iw=[B_pSq[q]])
                b3_done[q] += 1
                if b3_done[q] == 4 and not OPT.get('nors'):
                    P.op("pool", lambda e: e.collective_compute("ReduceScatter", ALU.add, replica_groups=RG,
                                                                ins=[pS[q, :, :]], outs=[pR[q, :, :]]),
                         [B_pSq[q]], [B_pRq[q]], cc=True)

            for h in range(4):
                odd = h % 2
                pair = h // 2
                vo = 64 if odd else 0
                so = 0 if odd else 64
                dma("sp", Kt.t[0:dk, :], Ks[h, 0:dk, :], [B_Ks], [Kt.b])
                memset("pool", Vt.t[:, :, so:so + 64], 1.0, [Vt.b])
                dma("sp", Vt.t[:, :, vo:vo + 64], Vs_v[h, :, :, :], [B_Vs], [Vt.b])
                nq = OPT.get('nqg', S // TT)
                if h == 3 and nq == S // TT:
                    Torder = [8 * r + j for j in range(8) for r in range(4)]
                else:
                    Torder = list(range(nq))
                tiles = [(T, i) for T in Torder for i in range(4 * T + 4)]
                b3_pend = []
                st_T = {}
                pend = []

                def issue_pv(T, i, p_, c0):
                    q_, b_, ob = st_T[T]
                    nblk = 4 * T + 4
                    mm(ob.t[:, c0:TT], Vt.t[:, i, :], p_.t[:, c0:TT], i == 0, i == nblk - 1, [Vt.b, p_.b], [ob.b])
                    if i == nblk - 1:
                        r_ = rsum.next()
                        act(r_.t[vo:vo + 64, :], ob.t[so:so + 64, :], AF.Copy, [ob.b], [r_.b])
                        recip(r_.t[vo:vo + 64, :], r_.t[vo:vo + 64, :], [r_.b], [r_.b])
                        tt("dve", Ot.t[vo:vo + 64, pair, T * TT:(T + 1) * TT], ob.t[vo:vo + 64, :], r_.t[vo:vo + 64, :], ALU.mult,
                           [ob.b, r_.b], [Ot.b])
                        del st_T[T]
                        if h == 3:
                            b3_pend.append(T)
                            if len(b3_pend) > 1:
                                b3(b3_pend.pop(0))

                for (T, i) in tiles:
                    if i == 0:
                        q_ = Qt.next()
                        dma("sp", q_.t[0:dk, :], Qs[h, 0:dk, T * TT:(T + 1) * TT], [B_Qs], [q_.b])
                        b_ = None
                        st_T[T] = (q_, b_, o_ps.next())
                    q_, b_, ob = st_T[T]
                    m = i - 4 * T
                    c0 = 128 * m if m > 0 else 0
                    sb_ = s_ps.next()
                    mm(sb_.t[:, c0:TT], Kt.t[0:dk, i * 128:(i + 1) * 128], q_.t[0:dk, c0:TT], True, m < 0, [Kt.b, q_.b], [sb_.b])
                    if m >= 0:
                        mm(sb_.t[:, c0:c0 + 128], IDENT(), NEGTRI(), False, True, [CM.b], [sb_.b])
                    p_ = Pt.next()
                    act(p_.t[:, c0:TT], sb_.t[:, c0:TT], AF.Exp, [sb_.b], [p_.b])
                    pend.append((T, i, p_, c0))
                    if len(pend) > LOOKAHEAD:
                        issue_pv(*pend.pop(0))
                while pend:
                    issue_pv(*pend.pop(0))
                if h == 3:
                    while b3_pend:
                        b3(b3_pend.pop(0))
            P.flush()

    def dump():
        tens = {"hS": hS, "agR0": agR0, "agR1": agR1, "Qs": Qs, "Ks": Ks, "Vs": Vs, "pR": pR, "Cs": Cs, "Lf": Lf, "pS": pS}
        for nm in dbg:
            t = tens[nm]
            ext = nc.dram_tensor("dbg_" + nm, list(t.shape), t.dtype, kind="ExternalOutput")
            if len(t.shape) == 3:
                dma("sp", ext[:, :, :], t[:, :, :], (), ())
            elif nm == "pS":
                dma("sp", ext[0, 0:D, :], t[0, 0:D, :], (), ())
            elif nm == "hS":
                nn = OPT.get('ntiles', 8) * TT
                dma("sp", ext[:, 0:nn], t[:, 0:nn], (), ())
            else:
                dma("sp", ext[:, :], t[:, :], (), ())
        P.flush()

    consts_load()
    if OPT.get('fox_first'):
        stage_tok(3)
        stage_proj(1)
        if stop_after != "proj1":
            stage_attn(1)
        P.flush()
        dump()
        gstack.close()
        return nc
    stage_tok(0)
    if stop_after != "tok0":
        stage_proj(0)
        if stop_after != "proj0":
            stage_attn(0)
            if stop_after != "attn0":
                stage_tok(1)
                stage_proj(1)
                stage_attn(1)
                stage_tok(2)
    P.flush()
    dump()
    gstack.close()
    return nc


GC_FFN1 = 0
GC_MIX = 16
GC_FFN2 = 32
GC_PLE = 48
GC_QLAT = 64
GC_KVLAT = 68
GC_GQ = 70
GC_GK = 71
GC_FQ = 72
GC_FK = 73
GC_INVF = 74
GC_SGN = 75
GC_BF = 76
GC_EPS = 77
GC_ONE = 78
NG = 79
OPT = {}


def _chunk_cols(v):
    return np.ascontiguousarray(v.reshape(-1, 128).T)


def prep_inputs(inp):
    f32 = np.float32
    x = inp["x"]
    p = inp["p"]
    positions = inp["positions"]
    common = {}
    for nm, key in (("w1in", "ffn1_w_in"), ("w2in", "ffn2_w_in")):
        w = inp[key].reshape(2, 8, 128, 2, NC_FF, 128)
        w = w.transpose(0, 4, 2, 3, 1, 5)
        common[nm] = np.ascontiguousarray(w).reshape(2, NC_FF, 128, 2048)
    for nm, key in (("w1out", "ffn1_w_out"), ("w2out", "ffn2_w_out")):
        w = inp[key].reshape(2, NC_FF, 128, D).transpose(0, 2, 1, 3)
        common[nm] = np.ascontiguousarray(w)
    common["wgate"] = np.ascontiguousarray(inp["ple_w_gate"].reshape(2, 8, 128, D).transpose(0, 2, 1, 3))
    common["wproj"] = np.ascontiguousarray(inp["ple_w_proj"].reshape(2, 2, 128, D).transpose(0, 2, 1, 3))
    common["mla_win"] = np.ascontiguousarray(inp["mla_w_in"][0].reshape(8, 128, 800).transpose(1, 0, 2))
    cm = np.zeros((6, 128, 128), f32)
    cm[0] = 1.0
    cm[1, :96, :] = 1.0
    cm[2, :64, :64] = 1.0
    cm[2, 64:, 64:] = 1.0
    cm[3] = np.eye(128, dtype=f32)
    kk, qq = np.meshgrid(np.arange(128), np.arange(128), indexing="ij")
    cm[4] = np.where(kk > qq, -30000.0, 0.0)
    hh, tt_ = np.arange(128) // 32, np.arange(128) % 32
    cm[5] = ((hh[:, None] == hh[None, :]) & (tt_[:, None] < tt_[None, :])).astype(f32)
    common["cmats"] = cm
    sel = np.zeros((32, 128), f32)
    for i in range(32):
        sel[i, 64 + i] = 1.0
    for j in range(32):
        sel[(j + 16) % 32, 96 + j] = 1.0
    common["selm"] = sel
    swap = np.concatenate([np.arange(16, 32), np.arange(0, 16)])
    w_uq = inp["mla_w_uq"][0].reshape(512, 16, 96)
    w_uq_ext = np.concatenate([w_uq, w_uq[:, :, 64 + swap]], axis=2)
    w_ukv = inp["mla_w_ukv"][0].reshape(256, 16, 128)
    w_k_ext = np.concatenate([w_ukv[:, :, :64], np.zeros((256, 16, 64), f32)], axis=2)
    w_v = w_ukv[:, :, 64:]
    gq = inp["mla_g_qn"][0]
    gk = inp["mla_g_kn"][0]
    gq_ext = np.concatenate([gq, gq[64 + swap]])
    gk_ext = np.concatenate([gk, gk[64 + swap]])
    inv_freq = (10000.0 ** (-np.arange(0, 32, 2, dtype=f32) / 32)).astype(f32)
    fox_in = inp["fox_w_in"][0]
    fq = fox_in[:, 0:1024].reshape(D, 16, 64)
    fk = fox_in[:, 1024:2048].reshape(D, 16, 64)
    fv = fox_in[:, 2048:3072].reshape(D, 16, 64)
    ff = fox_in[:, 3072:3088]
    in_maps = []
    for c in range(8):
        b, r = c // 4, c % 4
        hs = slice(4 * r, 4 * r + 4)
        m = dict(common)
        m["xT"] = np.ascontiguousarray(x[b, r * NT:(r + 1) * NT, :].T)
        m["pT"] = np.ascontiguousarray(p[:, b, r * NT:(r + 1) * NT, :].transpose(0, 2, 1))
        m["pos"] = np.ascontiguousarray(np.broadcast_to(positions[b].astype(np.int32)[None, :], (32, S)))
        g = np.zeros((128, NG), f32)
        for l in range(2):
            g[:, GC_FFN1 + 8 * l:GC_FFN1 + 8 * l + 8] = _chunk_cols(inp["g_ffn1"][l])
            g[:, GC_MIX + 8 * l:GC_MIX + 8 * l + 8] = _chunk_cols(inp["g_mix"][l])
            g[:, GC_FFN2 + 8 * l:GC_FFN2 + 8 * l + 8] = _chunk_cols(inp["g_ffn2"][l])
            g[:, GC_PLE + 8 * l:GC_PLE + 8 * l + 8] = _chunk_cols(inp["g_ple"][l])
        g[:, GC_QLAT:GC_QLAT + 4] = _chunk_cols(inp["mla_g_q_lat"][0])
        g[:, GC_KVLAT:GC_KVLAT + 2] = _chunk_cols(inp["mla_g_kv_lat"][0])
        g[:, GC_GQ] = gq_ext
        g[:, GC_GK] = gk_ext
        g[:, GC_FQ] = np.tile(inp["fox_g_qn"][0], 2)
        g[:, GC_FK] = np.tile(inp["fox_g_kn"][0], 2)
        g[64:96, GC_INVF] = np.tile(inv_freq, 2)
        g[64:80, GC_SGN] = -1.0
        g[80:96, GC_SGN] = 1.0
        g[0:4, GC_BF] = inp["fox_b_f"][0][hs]
        g[:, GC_EPS] = EPS
        g[:, GC_ONE] = 1.0
        m["gvec"] = g
        m["wuq"] = np.ascontiguousarray(w_uq_ext[:, hs, :].reshape(4, 128, 512).transpose(1, 0, 2))
        m["wukvk"] = np.ascontiguousarray(w_k_ext[:, hs, :].reshape(2, 128, 512).transpose(1, 0, 2))
        m["wukvv"] = np.ascontiguousarray(w_v[:, hs, :].reshape(2, 128, 256).transpose(1, 0, 2))
        m["mla_wo"] = np.ascontiguousarray(inp["mla_w_o"][0][256 * r:256 * (r + 1), :].reshape(2, 128, D).transpose(1, 0, 2))
        m["fwq"] = np.ascontiguousarray(fq[:, hs, :].reshape(8, 128, 256).transpose(1, 0, 2))
        m["fwk"] = np.ascontiguousarray(fk[:, hs, :].reshape(8, 128, 256).transpose(1, 0, 2))
        m["fwv"] = np.ascontiguousarray(fv[:, hs, :].reshape(8, 128, 256).transpose(1, 0, 2))
        m["fwf"] = np.ascontiguousarray(ff[:, hs].reshape(8, 128, 4).transpose(1, 0, 2))
        m["fox_wo"] = np.ascontiguousarray(inp["fox_w_o"][0][256 * r:256 * (r + 1), :].reshape(2, 128, D).transpose(1, 0, 2))
        in_maps.append(m)
    return in_maps


_NC_CACHE = {}


def kernel(**inputs):
    inp = {k: np.asarray(v) for k, v in inputs.items()}
    in_maps = prep_inputs(inp)
    if "nc" not in _NC_CACHE:
        _NC_CACHE["nc"] = build_program()
    nc = _NC_CACHE["nc"]
    res = run_bass_kernel_spmd(nc, in_maps, core_ids=list(range(8)))
    out = np.empty((2, S, D), np.float32)
    for c in range(8):
        b, r = c // 4, c % 4
        out[b, r * NT:(r + 1) * NT, :] = np.asarray(res.results[c]["yT"]).T
    return out
```
